# Optimizing a Trainium2 kernel written in Bass

```python
import math
import jax, jax.numpy as jnp
from jax import lax
import numpy as np

D_MODEL = 2048
BATCH = 16
SEQ = 256
DEPTH = 4
DEC_BATCH = 2
DEC_SEQ = 4096
PAST_LEN = 256

F32 = jnp.float32
GRID_W = 64
N_AB_LAYERS = (DEPTH + 1) // 2
N_C_LAYERS = DEPTH // 2
MIX_WIDTH = 2 * D_MODEL
SSD_INNER = MIX_WIDTH // 2
SSD_HEAD_DIM = 64
SSD_HEADS = SSD_INNER // SSD_HEAD_DIM
SSD_GROUPS = 4
SSD_HEADS_PER_GROUP = SSD_HEADS // SSD_GROUPS
D_STATE = 128
SSD_CONV = 3
SSD_CONV_DIM = SSD_INNER + 2 * SSD_GROUPS * D_STATE
RET_INNER = MIX_WIDTH // 2
RET_HEADS = 8
RET_DK = RET_INNER // RET_HEADS
RET_DV = RET_INNER // RET_HEADS
AB_IN_DIM = SSD_INNER + SSD_CONV_DIM + 2 * SSD_HEADS + 4 * RET_INNER
CHUNK = 128
ROPE_BASE = 10000.0
HY_WIDTH = D_MODEL
HY_ORDER = 2
HY_SHORT = 3
HY_BANDS = 16
HY_EMB = 1 + 2 * HY_BANDS
HY_HIDDEN = 64
HY_SIN_FREQ = 1.0
HY_FAST_DECAY = 0.3
HY_SLOW_DECAY = 1.5
HY_TARGET = 1e-2
D_FF = 5632
FFN_CONV = 3
EPS = 1e-6

kernel_name = 'bidir_ssd_retention_hyena_prefix_dit_step'


def rms_norm(x, w):
    xf = x.astype(F32)
    y = xf * lax.rsqrt(jnp.mean(xf * xf, axis=-1, keepdims=True) + EPS)
    return (y * w.astype(F32)).astype(x.dtype)


def group_rms_norm(x, w, groups):
    shp = x.shape
    xf = x.astype(F32).reshape(*shp[:-1], groups, shp[-1] // groups)
    xf = xf * lax.rsqrt(jnp.mean(xf * xf, axis=-1, keepdims=True) + EPS)
    return xf.reshape(shp) * w.astype(F32)


def modulate(x, shift, scale):
    return x * (1 + scale) + shift


def dwconv(x, w, b):
    k, ch = w.shape
    y = lax.conv_general_dilated(x, w[:, None, :].astype(x.dtype), window_strides=(1,),
                                 padding=[(k // 2, k // 2)], dimension_numbers=('NWC', 'WIO', 'NWC'),
                                 feature_group_count=ch)
    return y + b.astype(x.dtype)


def rope_half(t, pos):
    n = t.shape[-1] // 2
    inv = ROPE_BASE ** (-jnp.arange(n, dtype=F32) / n)
    ang = pos[:, None] * inv[None, :]
    cos = jnp.cos(ang)[None, :, None, :]
    sin = jnp.sin(ang)[None, :, None, :]
    t1 = t[..., :n].astype(F32)
    t2 = t[..., n:].astype(F32)
    return jnp.concatenate([t1 * cos - t2 * sin, t1 * sin + t2 * cos], axis=-1)


def axial_rope(x, rows, cols):
    half = x.shape[-1] // 2
    return jnp.concatenate([rope_half(x[..., :half], rows), rope_half(x[..., half:], cols)], axis=-1).astype(x.dtype)


def chunked_decay_scan(q, k, v, log_a, h0):
    b, L, G, N = q.shape
    R, P = v.shape[3], v.shape[4]
    nc = L // CHUNK
    qc = q.reshape(b, nc, CHUNK, G, N)
    kc = k.reshape(b, nc, CHUNK, G, N)
    vc = v.reshape(b, nc, CHUNK, G, R, P)
    a_cum = jnp.cumsum(log_a.astype(F32).reshape(b, nc, CHUNK, G, R), axis=2)
    idx = jnp.arange(CHUNK)
    lower = (idx[:, None] >= idx[None, :])[None, None, :, :, None, None]
    seg = a_cum[:, :, :, None] - a_cum[:, :, None, :]
    decay = jnp.exp(jnp.where(lower, seg, -jnp.inf))
    scores = jnp.einsum('bcign,bcjgn->bcijg', qc, kc)
    y_intra = jnp.einsum('bcijg,bcijgr,bcjgrp->bcigrp', scores, decay, vc)
    to_end = jnp.exp(a_cum[:, :, -1:] - a_cum)
    local = jnp.einsum('bcjgn,bcjgr,bcjgrp->bcgrpn', kc, to_end, vc)
    chunk_decay = jnp.exp(a_cum[:, :, -1])

    def step(h, inp):
        loc, dec = inp
        return h * dec[..., None, None] + loc, h

    final, h_in = lax.scan(step, h0.astype(F32), (jnp.moveaxis(local, 1, 0), jnp.moveaxis(chunk_decay, 1, 0)))
    h_in = jnp.moveaxis(h_in, 0, 1)
    y_inter = jnp.einsum('bcign,bcgrpn,bcigr->bcigrp', qc, h_in, jnp.exp(a_cum))
    return (y_intra + y_inter).reshape(b, L, G, R, P), final


def bidirectional_scan(q, k, v_dirs, log_a_dirs, h0):
    rev = lambda t: jnp.flip(t, axis=1)
    y_f, s_f = chunked_decay_scan(q, k, v_dirs[0], log_a_dirs[0], h0[:, 0])
    y_b, s_b = chunked_decay_scan(rev(q), rev(k), rev(v_dirs[1]), rev(log_a_dirs[1]), h0[:, 1])
    return y_f + rev(y_b), jnp.stack([s_f, s_b], axis=1)


def ab_mixer(u, pos, h0_ssd, h0_ret, w_in, conv_w, conv_b, dt_bias, a_log, d_skip, ssd_norm_w,
             log_gamma, ret_norm_w, w_out):
    b, L, _ = u.shape
    G, R, P, N = SSD_GROUPS, SSD_HEADS_PER_GROUP, SSD_HEAD_DIM, D_STATE
    o1 = SSD_INNER
    o2 = o1 + SSD_CONV_DIM
    o3 = o2 + 2 * SSD_HEADS
    o4 = o3 + RET_INNER
    o5 = o4 + RET_INNER
    o6 = o5 + RET_INNER
    proj = u @ w_in
    z, xbc, dt_raw, q, k, v, g = jnp.split(proj, [o1, o2, o3, o4, o5, o6], axis=-1)
    xbc = jax.nn.silu(dwconv(xbc, conv_w, conv_b))
    xs, bm, cm = jnp.split(xbc, [SSD_INNER, SSD_INNER + G * N], axis=-1)
    xs = xs.reshape(b, L, G, R, P)
    bm = bm.reshape(b, L, G, N)
    cm = cm.reshape(b, L, G, N)
    dt = jax.nn.softplus(dt_raw.astype(F32).reshape(b, L, 2, G, R) + dt_bias.astype(F32).reshape(2, G, R))
    a = -jnp.exp(a_log.astype(F32)).reshape(2, G, R)
    y_s, st_s = bidirectional_scan(
        cm, bm,
        (xs * dt[:, :, 0, :, :, None], xs * dt[:, :, 1, :, :, None]),
        (dt[:, :, 0] * a[0], dt[:, :, 1] * a[1]),
        h0_ssd.reshape(b, 2, G, R, P, N))
    y_s = y_s + xs * d_skip.astype(F32).reshape(G, R, 1)
    y_s = y_s.reshape(b, L, SSD_INNER) * jax.nn.silu(z)
    y_s = group_rms_norm(y_s, ssd_norm_w, SSD_GROUPS)
    q = q.reshape(b, L, RET_HEADS, RET_DK)
    k = k.reshape(b, L, RET_HEADS, RET_DK) * (RET_DK ** -0.5)
    if pos is not None:
        q = axial_rope(q, pos[0], pos[1])
        k = axial_rope(k, pos[0], pos[1])
    v = v.reshape(b, L, RET_HEADS, 1, RET_DV)
    lg = log_gamma.astype(F32)
    la = (jnp.broadcast_to(lg[0][None, None, :, None], (b, L, RET_HEADS, 1)),
          jnp.broadcast_to(lg[1][None, None, :, None], (b, L, RET_HEADS, 1)))
    y_r, st_r = bidirectional_scan(q, k, (v, v), la, h0_ret.reshape(b, 2, RET_HEADS, 1, RET_DV, RET_DK))
    y_r = y_r.reshape(b, L, RET_HEADS, RET_DV)
    mu = jnp.mean(y_r, axis=-1, keepdims=True)
    var = jnp.mean(jnp.square(y_r - mu), axis=-1, keepdims=True)
    y_r = ((y_r - mu) * lax.rsqrt(var + EPS)).reshape(b, L, RET_INNER) * ret_norm_w.astype(F32)
    y_r = jax.nn.silu(g) * y_r
    out = jnp.concatenate([y_s.astype(u.dtype), y_r.astype(u.dtype)], axis=-1) @ w_out
    return (out, st_s.reshape(b, 2, SSD_HEADS, P, N), st_r.reshape(b, 2, RET_HEADS, RET_DV, RET_DK))


def hyena_filters(L, w1, b1, w2, b2, w3):
    t = jnp.arange(L, dtype=F32) / L
    bands = jnp.linspace(1e-4, HY_BANDS - 1, HY_BANDS, dtype=F32)
    ang = 2 * math.pi * t[:, None] * bands[None, :]
    feats = jnp.concatenate([t[:, None], jnp.cos(ang), jnp.sin(ang)], axis=-1)
    h = jnp.sin(HY_SIN_FREQ * (feats @ w1.astype(F32) + b1.astype(F32)))
    h = jnp.sin(HY_SIN_FREQ * (h @ w2.astype(F32) + b2.astype(F32)))
    h = (h @ w3.astype(F32)).reshape(L, 2, HY_ORDER, HY_WIDTH)
    deltas = jnp.abs(jnp.linspace(math.log(HY_TARGET) / HY_SLOW_DECAY, math.log(HY_TARGET) / HY_FAST_DECAY,
                                  HY_WIDTH, dtype=F32))
    window = jnp.exp(-t[:, None] * deltas[None, :])
    h = h * window[:, None, None, :]
    return h / (jnp.sum(jnp.abs(h), axis=0, keepdims=True) + EPS)


def bidir_long_conv(z, h_f, h_b, bias):
    L, ch = h_f.shape
    filt = jnp.concatenate([h_f, jnp.zeros((1, ch), h_f.dtype), jnp.flip(h_b[1:], axis=0)], axis=0)
    zf = jnp.fft.rfft(z.astype(F32), n=2 * L, axis=1)
    ff = jnp.fft.rfft(filt, n=2 * L, axis=0)
    y = jnp.fft.irfft(zf * ff[None], n=2 * L, axis=1)[:, :L]
    return y + z.astype(F32) * bias.astype(F32)


def hyena_mixer(u, w_in, short_w, short_b, f_w1, f_b1, f_w2, f_b2, f_w3, bias, w_out):
    L = u.shape[1]
    proj = dwconv(u @ w_in, short_w, short_b)
    x1, x2, v = jnp.split(proj, 3, axis=-1)
    filt = hyena_filters(L, f_w1, f_b1, f_w2, f_b2, f_w3)
    z = v
    for o, gate in enumerate((x1, x2)):
        z = gate * bidir_long_conv(z, filt[:, 0, o], filt[:, 1, o], bias[o])
    return z.astype(u.dtype) @ w_out


def conv_glu(u, w_gate, w_up, conv_w, conv_b, w_down):
    a = dwconv(u @ w_gate, conv_w, conv_b)
    return (jax.nn.gelu(a) * (u @ w_up)) @ w_down


def trunk(x, cond, pos, h0_ssd, h0_ret, p):
    silu_c = jax.nn.silu(cond.astype(F32))
    fin_ssd, fin_ret = [], []
    for l in range(DEPTH):
        mod = silu_c @ p['ada_w'][l].astype(F32) + p['ada_b'][l].astype(F32)
        sh1, sc1, g1, sh2, sc2, g2 = jnp.split(mod.astype(x.dtype)[:, None, :], 6, axis=-1)
        h = modulate(rms_norm(x, p['norm1_w'][l]), sh1, sc1)
        i = l // 2
        if l % 2 == 0:
            mix, st_s, st_r = ab_mixer(h, pos, h0_ssd[:, i], h0_ret[:, i], p['ab_w_in'][i], p['ab_conv_w'][i],
                                       p['ab_conv_b'][i], p['ssd_dt_bias'][i], p['ssd_a_log'][i], p['ssd_d'][i],
                                       p['ssd_norm_w'][i], p['ret_log_gamma'][i], p['ret_norm_w'][i],
                                       p['ab_w_out'][i])
            fin_ssd.append(st_s)
            fin_ret.append(st_r)
        else:
            mix = hyena_mixer(h, p['hy_w_in'][i], p['hy_short_w'][i], p['hy_short_b'][i], p['hy_f_w1'][i],
                              p['hy_f_b1'][i], p['hy_f_w2'][i], p['hy_f_b2'][i], p['hy_f_w3'][i],
                              p['hy_bias'][i], p['hy_w_out'][i])
        x = x + (g1 * mix).astype(x.dtype)
        h = modulate(rms_norm(x, p['norm2_w'][l]), sh2, sc2)
        x = x + (g2 * conv_glu(h, p['ffn_w_gate'][l], p['ffn_w_up'][l], p['ffn_conv_w'][l], p['ffn_conv_b'][l],
                               p['ffn_w_down'][l])).astype(x.dtype)
    return rms_norm(x, p['final_norm_w']), fin_ssd, fin_ret


def setup_inputs(seed: int = 0) -> dict:
    key = jax.random.key(seed)
    keys = list(jax.random.split(key, 48))

    def nrm(shape, scale):
        return jax.random.normal(keys.pop(), shape, F32) * scale

    def gain(shape):
        return 1.0 + nrm(shape, 0.02)

    LA, LC = N_AB_LAYERS, N_C_LAYERS
    dt0 = jnp.exp(jax.random.uniform(keys.pop(), (LA, 2, SSD_HEADS), F32, math.log(1e-3), math.log(1e-1)))
    a0 = jax.random.uniform(keys.pop(), (LA, 2, SSD_HEADS), F32, 1.0, 16.0)
    gamma_base = jnp.log(1.0 - 2.0 ** (-5.0 - jnp.arange(RET_HEADS, dtype=F32)))
    hy_in = (HY_ORDER + 1) * HY_WIDTH
    return {
        'x_prompt': nrm((BATCH, SEQ, D_MODEL), 1.0),
        'x_sample': nrm((DEC_BATCH, DEC_SEQ, D_MODEL), 1.0),
        'c': nrm((DEC_BATCH, D_MODEL), 1.0),
        'state_ssd': nrm((DEC_BATCH, LA, 2, SSD_HEADS, SSD_HEAD_DIM, D_STATE), 0.1),
        'state_ret': nrm((DEC_BATCH, LA, 2, RET_HEADS, RET_DV, RET_DK), 1.0),
        'c_ctx': nrm((D_MODEL,), 1.0),
        'ada_w': nrm((DEPTH, D_MODEL, 6 * D_MODEL), 0.5 * D_MODEL ** -0.5),
        'ada_b': nrm((DEPTH, 6 * D_MODEL), 0.02),
        'norm1_w': gain((DEPTH, D_MODEL)),
        'norm2_w': gain((DEPTH, D_MODEL)),
        'ab_w_in': nrm((LA, D_MODEL, AB_IN_DIM), D_MODEL ** -0.5),
        'ab_conv_w': nrm((LA, SSD_CONV, SSD_CONV_DIM), SSD_CONV ** -0.5),
        'ab_conv_b': nrm((LA, SSD_CONV_DIM), 0.02),
        'ssd_dt_bias': dt0 + jnp.log(-jnp.expm1(-dt0)),
        'ssd_a_log': jnp.log(a0),
        'ssd_d': 1.0 + nrm((LA, SSD_HEADS), 0.1),
        'ssd_norm_w': gain((LA, SSD_INNER)),
        'ret_log_gamma': gamma_base * (1.0 + nrm((LA, 2, RET_HEADS), 0.05)),
        'ret_norm_w': gain((LA, RET_INNER)),
        'ab_w_out': nrm((LA, MIX_WIDTH, D_MODEL), MIX_WIDTH ** -0.5),
        'hy_w_in': nrm((LC, D_MODEL, hy_in), D_MODEL ** -0.5),
        'hy_short_w': nrm((LC, HY_SHORT, hy_in), HY_SHORT ** -0.5),
        'hy_short_b': nrm((LC, hy_in), 0.02),
        'hy_f_w1': nrm((LC, HY_EMB, HY_HIDDEN), HY_EMB ** -0.5),
        'hy_f_b1': nrm((LC, HY_HIDDEN), 0.02),
        'hy_f_w2': nrm((LC, HY_HIDDEN, HY_HIDDEN), HY_HIDDEN ** -0.5),
        'hy_f_b2': nrm((LC, HY_HIDDEN), 0.02),
        'hy_f_w3': nrm((LC, HY_HIDDEN, 2 * HY_ORDER * HY_WIDTH), HY_HIDDEN ** -0.5),
        'hy_bias': nrm((LC, HY_ORDER, HY_WIDTH), 1.0),
        'hy_w_out': nrm((LC, HY_WIDTH, D_MODEL), HY_WIDTH ** -0.5),
        'ffn_w_gate': nrm((DEPTH, D_MODEL, D_FF), D_MODEL ** -0.5),
        'ffn_w_up': nrm((DEPTH, D_MODEL, D_FF), D_MODEL ** -0.5),
        'ffn_conv_w': nrm((DEPTH, FFN_CONV, D_FF), FFN_CONV ** -0.5),
        'ffn_conv_b': nrm((DEPTH, D_FF), 0.02),
        'ffn_w_down': nrm((DEPTH, D_FF, D_MODEL), D_FF ** -0.5),
        'final_norm_w': gain((D_MODEL,)),
    }


def reference(x_prompt, x_sample, c, state_ssd, state_ret, c_ctx, ada_w, ada_b, norm1_w, norm2_w, ab_w_in,
              ab_conv_w, ab_conv_b, ssd_dt_bias, ssd_a_log, ssd_d, ssd_norm_w, ret_log_gamma, ret_norm_w, ab_w_out,
              hy_w_in, hy_short_w, hy_short_b, hy_f_w1, hy_f_b1, hy_f_w2, hy_f_b2, hy_f_w3, hy_bias, hy_w_out,
              ffn_w_gate, ffn_w_up, ffn_conv_w, ffn_conv_b, ffn_w_down, final_norm_w):
    p = {
        'ada_w': ada_w, 'ada_b': ada_b, 'norm1_w': norm1_w, 'norm2_w': norm2_w,
        'ab_w_in': ab_w_in, 'ab_conv_w': ab_conv_w, 'ab_conv_b': ab_conv_b, 'ssd_dt_bias': ssd_dt_bias,
        'ssd_a_log': ssd_a_log, 'ssd_d': ssd_d, 'ssd_norm_w': ssd_norm_w, 'ret_log_gamma': ret_log_gamma,
        'ret_norm_w': ret_norm_w, 'ab_w_out': ab_w_out,
        'hy_w_in': hy_w_in, 'hy_short_w': hy_short_w, 'hy_short_b': hy_short_b, 'hy_f_w1': hy_f_w1,
        'hy_f_b1': hy_f_b1, 'hy_f_w2': hy_f_w2, 'hy_f_b2': hy_f_b2, 'hy_f_w3': hy_f_w3, 'hy_bias': hy_bias,
        'hy_w_out': hy_w_out,
        'ffn_w_gate': ffn_w_gate, 'ffn_w_up': ffn_w_up, 'ffn_conv_w': ffn_conv_w, 'ffn_conv_b': ffn_conv_b,
        'ffn_w_down': ffn_w_down, 'final_norm_w': final_norm_w,
    }
    nb = x_prompt.shape[0]
    zeros_ssd = jnp.zeros((nb, N_AB_LAYERS, 2, SSD_HEADS, SSD_HEAD_DIM, D_STATE), F32)
    zeros_ret = jnp.zeros((nb, N_AB_LAYERS, 2, RET_HEADS, RET_DV, RET_DK), F32)
    y_prompt, ctx_ssd, ctx_ret = trunk(x_prompt, c_ctx[None, :], None, zeros_ssd, zeros_ret, p)
    new_state_ssd = jnp.stack(ctx_ssd, axis=1)
    new_state_ret = jnp.stack(ctx_ret, axis=1)
    lat_len = x_sample.shape[1]
    n_rows = lat_len // GRID_W
    rows = jnp.repeat(jnp.arange(n_rows, dtype=F32), GRID_W)
    cols = jnp.broadcast_to(jnp.arange(GRID_W, dtype=F32)[None, :], (n_rows, GRID_W)).reshape(-1)
    y_sample, _, _ = trunk(x_sample, c, (rows, cols), state_ssd, state_ret, p)
    return (y_prompt, y_sample, new_state_ssd, new_state_ret)
```

```python
import contextlib
import math
import numpy as np
import concourse.bass as bass
import concourse.mybir as mybir
from concourse.bass_utils import run_bass_kernel_spmd

F32 = mybir.dt.float32
BF16 = mybir.dt.bfloat16
AF = mybir.ActivationFunctionType
ALU = mybir.AluOpType
AX = mybir.AxisListType

D = 2048
AB_IN = 13376
DFF = 5632
EPS = 1e-6
NDQ = 8


class Prog:
    def __init__(self, nc):
        self.nc = nc
        self.names = ['pe', 'act', 'dve', 'pool', 'sp']
        self.ops = {e: [] for e in self.names}
        self.cnt = {e: 0 for e in self.names}
        self.dma_i = {'sp': 0, 'pool': 0}
        self.waited = {e: {} for e in self.names}
        self.lastw = {}
        self.readers = {}
        self.semnames = list(self.names) + [f"dq_{q}_{i}" for q in ('sp', 'pool') for i in range(NDQ)]
        self.semval = {s: 0 for s in self.semnames}

    def _need(self, eng, deps):
        out = []
        wd = self.waited[eng]
        for s, v in deps.items():
            if v > wd.get(s, 0):
                wd[s] = v
                out.append((s, v))
        return out

    def _collect(self, r, w):
        deps = {}

        def add(ev):
            if ev is None:
                return
            s, v = ev
            if v > deps.get(s, 0):
                deps[s] = v
        for k in list(r) + list(w):
            add(self.lastw.get(k))
        for k in w:
            for s, v in self.readers.get(k, {}).items():
                add((s, v))
        return deps

    def _commit(self, ev, r, w):
        for k in w:
            self.lastw[k] = ev
            self.readers[k] = {}
        for k in r:
            if k in w:
                continue
            d = self.readers.setdefault(k, {})
            if ev[1] > d.get(ev[0], 0):
                d[ev[0]] = ev[1]

    def op(self, eng, fn, r=(), w=()):
        deps = self._collect(r, w)
        waits = self._need(eng, deps)
        self.cnt[eng] += 1
        ev = (eng, self.cnt[eng])
        self.semval[eng] = self.cnt[eng]
        self.ops[eng].append((waits, fn, (eng, 1)))
        self._commit(ev, r, w)

    def dma(self, q, out, in_, r=(), w=(), **kw):
        deps = self._collect(r, w)
        i = self.dma_i[q]
        self.dma_i[q] += 1
        s = f"dq_{q}_{i % NDQ}"
        prev = 16 * (i // NDQ)
        if prev > 0:
            deps[s] = max(deps.get(s, 0), prev)
        waits = self._need(q, deps)
        ev = (s, prev + 16)
        self.semval[s] = prev + 16

        def fn(e, out=out, in_=in_, kw=kw):
            return e.dma_start(out=out, in_=in_, **kw)
        self.ops[q].append((waits, fn, (s, 16)))
        self._commit(ev, r, w)

    def barrier(self):
        allv = {s: v for s, v in self.semval.items() if v > 0}
        for e in self.names:
            waits = self._need(e, dict(allv))
            if waits:
                self.ops[e].append((waits, None, None))
        self.lastw = {}
        self.readers = {}

    def emit(self, es):
        nc = self.nc
        sems = {s: es.enter_context(nc.semaphore(s)) for s in self.semnames}
        block = es.enter_context(nc.Block())
        engs = {'pe': block.tensor, 'act': block.scalar, 'dve': block.vector, 'pool': block.gpsimd, 'sp': block.sync}
        final = {s: v for s, v in self.semval.items() if v > 0}
        for name in self.names:
            ops = self.ops[name]
            last = (name == 'sp')

            def body(e, ops=ops, last=last):
                for waits, fn, inc in ops:
                    for s, v in waits:
                        e.wait_ge(sems[s], v)
                    if fn is not None:
                        ins = fn(e)
                        ins.then_inc(sems[inc[0]], inc[1])
                if last:
                    for s, v in final.items():
                        e.wait_ge(sems[s], v)
            engs[name](body)


class KB:
    def __init__(self, NP, LS, DEPTH, dbg=None):
        self.NP, self.LS, self.DEPTH = NP, LS, DEPTH
        self.T = NP * 256 + LS
        self.NCH = self.T // 128
        self.seqs = [(2 * p, 2, 0) for p in range(NP)] + [(2 * NP, LS // 128, 1)]
        self.cond_of = []
        for c0, n, cd in self.seqs:
            self.cond_of += [cd] * n
        self.nc = bass.Bass("TRN2", target_bir_lowering=False)
        self.P = Prog(self.nc)
        self.es = contextlib.ExitStack()
        self.inputs = {}
        self.dbg = dbg or {}
        self.dbg_level = 99
        self._uid = 0

    def inp(self, name, shape, dtype=F32):
        ap = self.nc.dram_tensor(name, list(shape), dtype, kind="ExternalInput").ap()
        self.inputs[name] = ap
        return ap

    def outp(self, name, shape, dtype=F32):
        return self.nc.dram_tensor(name, list(shape), dtype, kind="ExternalOutput").ap()

    def scr(self, name, shape, dtype=F32):
        return self.nc.dram_tensor(name, list(shape), dtype, kind="Internal").ap()

    def sb(self, name, shape, dtype=F32):
        return self.es.enter_context(self.nc.sbuf_tensor(name, list(shape), dtype))

    def ps(self, name, shape, dtype=F32):
        return self.es.enter_context(self.nc.psum_tensor(name, list(shape), dtype))

    def act(self, out, in_, func, r, w, bias=None, scale=None, accum_out=None):
        kw = {}
        if bias is not None:
            kw['bias'] = bias
        if scale is not None:
            kw['scale'] = scale
        if accum_out is not None:
            kw['accum_out'] = accum_out
        self.P.op('act', lambda e: e.activation(out=out, in_=in_, func=func, **kw), r, w)

    def tt(self, out, in0, in1, op, r, w, eng='dve'):
        self.P.op(eng, lambda e: e.tensor_tensor(out=out, in0=in0, in1=in1, op=op), r, w)

    def ts(self, out, in0, s1, op0, r, w, s2=None, op1=None, eng='dve'):
        if op1 is None:
            self.P.op(eng, lambda e: e.tensor_scalar(out=out, in0=in0, scalar1=s1, scalar2=None, op0=op0), r, w)
        else:
            self.P.op(eng, lambda e: e.tensor_scalar(out=out, in0=in0, scalar1=s1, scalar2=s2, op0=op0, op1=op1), r, w)

    def stt(self, out, in0, scalar, in1, op0, op1, r, w):
        self.P.op('dve', lambda e: e.scalar_tensor_tensor(out=out, in0=in0, scalar=scalar, in1=in1, op0=op0, op1=op1), r, w)

    def cp(self, out, in_, r, w, eng='dve'):
        if eng == 'act':
            self.P.op('act', lambda e: e.copy(out=out, in_=in_), r, w)
        else:
            self.P.op(eng, lambda e: e.tensor_copy(out=out, in_=in_), r, w)

    def memset(self, out, val, w):
        self.P.op('dve', lambda e: e.memset(out, val), (), w)

    def mm(self, out, pairs, r, w, transpose=False):
        n = len(pairs)

        def fn(e):
            ins = None
            for i, (l, rr) in enumerate(pairs):
                ins = e.matmul(out, lhsT=l, rhs=rr, start=(i == 0), stop=(i == n - 1))
            return ins
        self.P.op('pe', fn, r, w)

    def mmx(self, items, r, w):
        def fn(e):
            ins = None
            for (o, l, rr, st, sp) in items:
                ins = e.matmul(o, lhsT=l, rhs=rr, start=st, stop=sp)
            return ins
        self.P.op('pe', fn, r, w)

    def tr(self, outs_ins, ident, r, w):
        def fn(e):
            ins = None
            for o, i in outs_ins:
                ins = e.transpose(o, i, ident)
            return ins
        self.P.op('pe', fn, r, w)

    def ld(self, out, in_, r, w, q='sp', **kw):
        self.P.dma(q, out, in_, r, w, **kw)


class Arena:
    def __init__(self, kb, sizes):
        self.regs = [kb.sb(f"arena{i}", [128, n], F32) for i, n in enumerate(sizes)]
        self.sizes = sizes
        self.reset()
        self.uid = 0

    def reset(self):
        self.off = [0] * len(self.regs)

    def tile(self, shape, dtype=F32):
        n = 1
        for s in shape[1:]:
            n *= s
        nw = n if dtype == F32 else (n + 1) // 2
        for i, reg in enumerate(self.regs):
            if self.off[i] + nw <= self.sizes[i]:
                ap = reg[:, self.off[i]:self.off[i] + nw]
                self.off[i] += nw
                if dtype != F32:
                    ap = ap.bitcast(dtype)
                    if n != nw * 2:
                        ap = ap[:, :n]
                if len(shape) == 3:
                    ap = ap.rearrange("p (a b) -> p a b", b=shape[2])
                elif len(shape) == 4:
                    ap = ap.rearrange("p (a b c) -> p a b c", b=shape[2], c=shape[3])
                self.uid += 1
                return ap, f"t{self.uid}"
        raise RuntimeError(f"arena full for {shape}")


def build(NP, LS, DEPTH, dbg=False):
    kb = KB(NP, LS, DEPTH)
    nc, P = kb.nc, kb.P
    T, NCH = kb.T, kb.NCH
    NLA = (DEPTH + 1) // 2
    NLC = DEPTH // 2
    LENS = sorted({256, LS})

    I = {}
    def inp(name, shape):
        I[name] = kb.inp(name, shape)
    inp('xp', [NP * 256, D]); inp('xs', [LS, D]); inp('cvec', [2, D])
    inp('sssd', [NLA, 2, 2048, 128]); inp('sret', [NLA, 2, 2048, 256])
    inp('ada_w', [DEPTH, D, 6 * D]); inp('ada_b', [DEPTH, 6 * D]); inp('norm1_w', [DEPTH, D]); inp('norm2_w', [DEPTH, D])
    inp('ab_w_in', [NLA, D, AB_IN]); inp('ab_conv_w', [NLA, 3, 3072]); inp('ab_conv_b', [NLA, 3072])
    inp('ssd_dt_bias', [NLA, 64]); inp('ssd_a_log', [NLA, 64]); inp('ssd_d', [NLA, 32]); inp('ssd_norm_w', [NLA, D])
    inp('ret_log_gamma', [NLA, 16]); inp('ret_norm_w', [NLA, D]); inp('ab_w_out', [NLA, 2 * D, D])
    NLH = max(NLC, 1)
    inp('hy_w_in', [NLH, D, 3 * D]); inp('hy_short_w', [NLH, 3, 3 * D]); inp('hy_short_b', [NLH, 3 * D])
    inp('hy_f_w1', [NLH, 33, 64]); inp('hy_f_b1', [NLH, 64]); inp('hy_f_w2', [NLH, 64, 64]); inp('hy_f_b2', [NLH, 64])
    inp('hy_f_w3', [NLH, 64, 4 * D]); inp('hy_bias', [NLH, 2, D]); inp('hy_w_out', [NLH, D, D])
    inp('ffn_w_gate', [DEPTH, D, DFF]); inp('ffn_w_up', [DEPTH, D, DFF]); inp('ffn_conv_w', [DEPTH, 3, DFF])
    inp('ffn_conv_b', [DEPTH, DFF]); inp('ffn_w_down', [DEPTH, DFF, D]); inp('final_norm_w', [D])
    inp('k_ident', [128, 128]); inp('k_ule', [128, 128]); inp('k_lge', [128, 128])
    inp('k_maskf', [128, 128]); inp('k_maskb', [128, 128]); inp('k_difff', [128, 128]); inp('k_diffb', [128, 128])
    inp('k_posc', [128, 4]); inp('k_alt', [128, 128])
    inp('k_rope', [4, LS, 64]); inp('k_delta', [D])
    for L in LENS:
        inp(f'k_featsT{L}', [33, L]); inp(f'k_negt{L}', [128, L // 128])
        inp(f'k_dftc{L}', [L, L]); inp(f'k_dfts{L}', [L, L]); inp(f'k_dftst{L}', [L, L])

    O = {}
    O['yp'] = kb.outp('yp', [NP * 256, D]); O['ys'] = kb.outp('ys', [LS, D])
    O['nssd'] = kb.outp('nssd', [NP, NLA, 2, 2048, 128]); O['nret'] = kb.outp('nret', [NP, NLA, 2, 2048, 256])

    X = kb.scr('X', [T, D]); MODD = kb.scr('MODD', [8, 6 * D])
    PROJA = kb.scr('PROJA', [T, 5184]); PROJB = kb.scr('PROJB', [T, 8192]); XBC = kb.scr('XBC', [T, 3072]); QK = kb.scr('QK', [T, 2 * D], BF16)
    YF = kb.scr('YF', [T, 2 * D]); YG = kb.scr('YG', [T, 2 * D], BF16); MIXO = kb.scr('MIXO', [T, D])
    FA = kb.scr('FA', [T, DFF]); FU = kb.scr('FU', [T, DFF]); FH = kb.scr('FH', [T, DFF], BF16)
    XG = kb.scr('XG', [T, 3 * D]); Z1 = kb.scr('Z1', [T, D]); Z2 = kb.scr('Z2', [T, D])
    FILT = {L: kb.scr(f'FILT{L}', [L, 4 * D]) for L in LENS}
    RN = {L: kb.scr(f'RN{L}', [4 * D]) for L in LENS}
    FSP = {L: kb.scr(f'FSP{L}', [2, 2, L, D]) for L in LENS}
    ZERO = kb.scr('ZERO', [1, 6144])

    IDf = kb.sb('IDf', [128, 128]); IDb = kb.sb('IDb', [128, 128], BF16)
    ULE = kb.sb('ULE', [128, 128]); LGE = kb.sb('LGE', [128, 128]); ONES = kb.sb('ONES', [128, 128])
    MASK = [kb.sb('MASKF', [128, 128]), kb.sb('MASKB', [128, 128])]
    DIFF = [kb.sb('DIFFF', [128, 128]), kb.sb('DIFFB', [128, 128])]
    POSC = kb.sb('POSC', [128, 4]); ALTb = kb.sb('ALTb', [128, 128], BF16)
    AR = Arena(kb, [16384, 16384, 12288, 3072])
    PSL = [kb.ps('PSL0', [128, 512]), kb.ps('PSL1', [128, 512])]
    PST = kb.ps('PST', [128, 1024], BF16)
    PSA = [kb.ps('PSA0', [128, 512]), kb.ps('PSA1', [128, 512])]
    PSS = kb.ps('PSS', [128, 512]); PSY = kb.ps('PSY', [128, 512]); PSI = kb.ps('PSI', [128, 512])

    def ldc(dst, key, src, cast=False):
        kb.ld(dst, src, [], [key], q='pool' if cast else 'sp')
    ldc(IDf[:], 'IDf', I['k_ident']); ldc(IDb[:], 'IDb', I['k_ident'], True)
    ldc(ULE[:], 'ULE', I['k_ule']); ldc(LGE[:], 'LGE', I['k_lge'])
    ldc(MASK[0][:], 'MASKF', I['k_maskf']); ldc(MASK[1][:], 'MASKB', I['k_maskb'])
    ldc(DIFF[0][:], 'DIFFF', I['k_difff']); ldc(DIFF[1][:], 'DIFFB', I['k_diffb'])
    ldc(POSC[:], 'POSC', I['k_posc']); ldc(ALTb[:], 'ALTb', I['k_alt'], True)
    kb.memset(ONES[:], 1.0, ['ONES'])
    zt, zk = AR.tile([128, 6144])
    kb.memset(zt[0:1, :], 0.0, [zk])
    kb.ld(ZERO[:, :], zt[0:1, :], [zk], ['ZERO'])
    for c in range(NCH):
        r0 = c * 128
        if r0 < NP * 256:
            src = I['xp'][r0:r0 + 128, :]
        else:
            src = I['xs'][r0 - NP * 256:r0 - NP * 256 + 128, :]
        kb.ld(X[r0:r0 + 128, :], src, [], [('X', c)])
    P.barrier(); AR.reset()

    rr = {'i': 0}
    def alt_eng():
        rr['i'] += 1
        return 'act' if rr['i'] % 2 else 'dve'

    def bcast_row(dst, key, vec_ap, r=()):
        kb.ld(dst, vec_ap.partition_broadcast(128), list(r), [key])

    def stage_mod():
        cT, ck = AR.tile([128, 2, 16]); sT, sk = AR.tile([128, 2, 16])
        lhs, lk = AR.tile([128, 2, 16, 128], BF16)
        for cd in range(2):
            kb.ld(cT[:, cd, :], I['cvec'][cd, :].rearrange("(kc p) -> p kc", p=128), [], [ck], allow_slow_non_contiguous=True)
        kb.act(sT, cT, AF.Silu, [ck], [sk])
        for cd in range(2):
            kb.cp(lhs[:, cd, :, :], sT[:, cd, :].unsqueeze(2).to_broadcast([128, 16, 128]), [sk], [lk])
        wbs = [AR.tile([128, 16, 512], BF16) for _ in range(2)]
        abs_ = [AR.tile([128, 512]) for _ in range(2)]
        mos = [AR.tile([128, 512]) for _ in range(2)]
        n = 0
        for l in range(DEPTH):
            wv = I['ada_w'][l].rearrange("(kc p) f -> p kc f", p=128)
            for blk in range(24):
                wb, wk = wbs[n % 2]; ab, ak = abs_[n % 2]
                kb.ld(wb, wv[:, :, blk * 512:(blk + 1) * 512], [], [wk], q='pool')
                bcast_row(ab, ak, I['ada_b'][l, blk * 512:(blk + 1) * 512])
                for cd in range(2):
                    ps = PSL[cd]
                    kb.mm(ps[:, :], [(lhs[:, cd, kc, :], wb[:, kc, :]) for kc in range(16)], [lk, wk], [f'PSL{cd}'])
                    mo, mk = mos[cd]
                    kb.tt(mo, ps[:, :], ab, ALU.add, [f'PSL{cd}', ak], [mk])
                    kb.ld(MODD[l * 2 + cd:l * 2 + cd + 1, blk * 512:(blk + 1) * 512], mo[0:1, :], [mk], ['MODD'])
                n += 1
        P.barrier(); AR.reset()

    def linear(prep_setup, Fin, outs, rkeys_of):
        kch = Fin // 128
        GC = {16: 12, 32: 6, 44: 4}[kch]
        BLK = 512 if kch <= 32 else 256
        HT, hk = AR.tile([128, kch, GC * 128], BF16)
        wbs = [AR.tile([128, kch, BLK], BF16) for _ in range(2)]
        ots = [AR.tile([128, 512]) for _ in range(2)]
        prep = prep_setup()
        chunks = list(range(NCH))
        wi = 0; oi = 0
        for g0 in range(0, NCH, GC):
            grp = chunks[g0:g0 + GC]
            for gi, c in enumerate(grp):
                src, skey = prep(c)
                for k0 in range(0, kch, 8):
                    nb = min(8, kch - k0)
                    kb.tr([(PST[:, j * 128:(j + 1) * 128], src[:, (k0 + j) * 128:(k0 + j + 1) * 128]) for j in range(nb)],
                          IDb[:], [skey, 'IDb'], ['PST'])
                    kb.cp(HT[:, k0:k0 + nb, gi * 128:(gi + 1) * 128],
                          PST[:, :nb * 128].rearrange("p (a b) -> p a b", b=128), ['PST'], [(hk, gi)], eng=alt_eng())
            for (W, c_lo, c_hi, dst, dname, dcol) in outs:
                wv = W.rearrange("(kc p) f -> p kc f", p=128)
                for b0 in range(c_lo, c_hi, BLK):
                    ncol = min(BLK, c_hi - b0)
                    wb, wk = wbs[wi % 2]; wi += 1
                    kb.ld(wb[:, :, :ncol], wv[:, :, b0:b0 + ncol], [], [wk], q='pool')
                    for gi, c in enumerate(grp):
                        ps = PSL[oi % 2]; pk = f'PSL{oi % 2}'
                        ot, ok = ots[oi % 2]; oi += 1
                        kb.mm(ps[:, :ncol], [(HT[:, kc, gi * 128:(gi + 1) * 128], wb[:, kc, :ncol]) for kc in range(kch)],
                              [(hk, gi), wk], [pk])
                        kb.cp(ot[:, :ncol], ps[:, :ncol], [pk], [ok], eng=alt_eng())
                        kb.ld(dst[c * 128:(c + 1) * 128, dcol + b0 - c_lo:dcol + b0 - c_lo + ncol], ot[:, :ncol], [ok], [(dname, c)])
        P.barrier(); AR.reset()

    def make_prep_norm(l, which, normw_ap, resid_from=None):
        def setup():
            arow = []; srow = []; grow = []
            nw, nk = AR.tile([128, D])
            bcast_row(nw, nk, normw_ap)
            for cd in range(2):
                a, ak = AR.tile([128, D]); s, sk = AR.tile([128, D])
                bcast_row(s, sk, MODD[l * 2 + cd, (3 * which) * D:(3 * which + 1) * D], ['MODD'])
                bcast_row(a, ak, MODD[l * 2 + cd, (3 * which + 1) * D:(3 * which + 2) * D], ['MODD'])
                kb.stt(a, a, 1.0, nw, ALU.add, ALU.mult, [ak, nk], [ak])
                arow.append((a, ak)); srow.append((s, sk))
                if resid_from is not None:
                    gl, gidx = resid_from
                    g, gk = AR.tile([128, D])
                    bcast_row(g, gk, MODD[gl * 2 + cd, gidx * D:(gidx + 1) * D], ['MODD'])
                    grow.append((g, gk))
            xts = [AR.tile([128, D]) for _ in range(2)]
            mts = [AR.tile([128, D]) for _ in range(2)]
            hbs = [AR.tile([128, D], BF16) for _ in range(2)]
            sts = [AR.tile([128, 4]) for _ in range(2)]
            st = {'i': 0}

            def prep(c):
                i = st['i'] % 2; st['i'] += 1
                cd = kb.cond_of[c]
                xt, xk = xts[i]; mt, mk = mts[i]; hb, hk2 = hbs[i]; s4, s4k = sts[i]
                kb.ld(xt, X[c * 128:(c + 1) * 128, :], [('X', c)], [xk])
                if resid_from is not None:
                    kb.ld(mt, MIXO[c * 128:(c + 1) * 128, :], [('MIXO', c)], [mk])
                    g, gk = grow[cd]
                    kb.tt(mt, mt, g, ALU.mult, [mk, gk], [mk])
                    kb.tt(xt, xt, mt, ALU.add, [xk, mk], [xk])
                    kb.ld(X[c * 128:(c + 1) * 128, :], xt, [xk], [('X', c)])
                kb.tt(mt, xt, xt, ALU.mult, [xk], [mk])
                P.op('dve', lambda e, s4=s4, mt=mt: e.reduce_sum(out=s4[:, 0:1], in_=mt, axis=AX.X), [mk], [s4k])
                kb.ts(s4[:, 1:2], s4[:, 0:1], 1.0 / D, ALU.mult, [s4k], [s4k], s2=EPS, op1=ALU.add)
                kb.act(s4[:, 2:3], s4[:, 1:2], AF.Sqrt, [s4k], [s4k])
                P.op('dve', lambda e, s4=s4: e.reciprocal(out=s4[:, 3:4], in_=s4[:, 2:3]), [s4k], [s4k])
                a, ak = arow[cd]; s, sk = srow[cd]
                kb.stt(mt, xt, s4[:, 3:4], a, ALU.mult, ALU.mult, [xk, s4k, ak], [mk])
                kb.tt(hb, mt, s, ALU.add, [mk, sk], [hk2])
                return hb, hk2
            return prep
        return setup

    def make_prep_load(src, sname, Fin, is_bf16):
        def setup():
            if is_bf16:
                hbs = [AR.tile([128, Fin], BF16) for _ in range(2)]
            else:
                fts = [AR.tile([128, Fin]) for _ in range(2)]
                hbs = [AR.tile([128, Fin], BF16) for _ in range(2)]
            st = {'i': 0}

            def prep(c):
                i = st['i'] % 2; st['i'] += 1
                hb, hk2 = hbs[i]
                if is_bf16:
                    kb.ld(hb, src[c * 128:(c + 1) * 128, :], [(sname, c)], [hk2])
                else:
                    ft, fk = fts[i]
                    kb.ld(ft, src[c * 128:(c + 1) * 128, :], [(sname, c)], [fk])
                    kb.cp(hb, ft, [fk], [hk2], eng=alt_eng())
                return hb, hk2
            return prep
        return setup

    def conv3(src, sname, scol, F, w_ap, b_ap, post_setup, CB=1024):
        for col0 in range(0, F, CB):
            ncol = min(CB, F - col0)
            wts = []
            for k in range(3):
                wt, wk = AR.tile([128, CB]); bcast_row(wt[:, :ncol], wk, w_ap[k, col0:col0 + ncol]); wts.append((wt, wk))
            bt, bk = AR.tile([128, CB]); bcast_row(bt[:, :ncol], bk, b_ap[col0:col0 + ncol])
            bufs = [[AR.tile([128, CB]) for _ in range(3)] for _ in range(2)]
            accs = [AR.tile([128, CB]) for _ in range(2)]
            post = post_setup()
            n = 0
            for (c0, nchk, cd) in kb.seqs:
                for c in range(c0, c0 + nchk):
                    (pv, pk), (cu, ck), (nx, nk) = bufs[n % 2]; acc, akey = accs[n % 2]; n += 1
                    r0 = c * 128
                    cs = slice(scol + col0, scol + col0 + ncol)
                    kb.ld(cu[:, :ncol], src[r0:r0 + 128, cs], [(sname, c)], [ck])
                    if c == c0:
                        kb.ld(pv[0:1, :ncol], ZERO[:, :ncol], ['ZERO'], [pk])
                        kb.ld(pv[1:128, :ncol], src[r0:r0 + 127, cs], [(sname, c)], [pk])
                    else:
                        kb.ld(pv[:, :ncol], src[r0 - 1:r0 + 127, cs], [(sname, c), (sname, c - 1)], [pk])
                    if c == c0 + nchk - 1:
                        kb.ld(nx[127:128, :ncol], ZERO[:, :ncol], ['ZERO'], [nk])
                        kb.ld(nx[0:127, :ncol], src[r0 + 1:r0 + 128, cs], [(sname, c)], [nk])
                    else:
                        kb.ld(nx[:, :ncol], src[r0 + 1:r0 + 129, cs], [(sname, c), (sname, c + 1)], [nk])
                    kb.tt(acc[:, :ncol], cu[:, :ncol], wts[1][0][:, :ncol], ALU.mult, [ck, wts[1][1]], [akey])
                    kb.tt(pv[:, :ncol], pv[:, :ncol], wts[0][0][:, :ncol], ALU.mult, [pk, wts[0][1]], [pk])
                    kb.tt(nx[:, :ncol], nx[:, :ncol], wts[2][0][:, :ncol], ALU.mult, [nk, wts[2][1]], [nk])
                    kb.tt(acc[:, :ncol], acc[:, :ncol], pv[:, :ncol], ALU.add, [akey, pk], [akey])
                    kb.tt(acc[:, :ncol], acc[:, :ncol], nx[:, :ncol], ALU.add, [akey, nk], [akey])
                    kb.tt(acc[:, :ncol], acc[:, :ncol], bt[:, :ncol], ALU.add, [akey, bk], [akey])
                    post(c, col0, ncol, acc, akey)
            P.barrier(); AR.reset()

    def stage_ffn(l, resid_from):
        if kb.dbg_level < 2:
            return
        linear(make_prep_norm(l, 1, I['norm2_w'][l, :], resid_from), D,
               [(I['ffn_w_gate'][l], 0, DFF, FA, 'FA', 0), (I['ffn_w_up'][l], 0, DFF, FU, 'FU', 0)], None)
        def post_setup():
            us = [AR.tile([128, 1024]) for _ in range(2)]
            tsx = [AR.tile([128, 1024]) for _ in range(2)]
            hs = [AR.tile([128, 1024], BF16) for _ in range(2)]
            st = {'i': 0}

            def post(c, col0, ncol, acc, akey):
                i = st['i'] % 2; st['i'] += 1
                u, uk = us[i]; t, tk = tsx[i]; h, hk = hs[i]
                kb.ld(u[:, :ncol], FU[c * 128:(c + 1) * 128, col0:col0 + ncol], [('FU', c)], [uk])
                a = acc[:, :ncol]; tt_ = t[:, :ncol]
                kb.tt(tt_, a, a, ALU.mult, [akey], [tk])
                kb.ts(tt_, tt_, 0.044715, ALU.mult, [tk], [tk], s2=1.0, op1=ALU.add)
                kb.tt(tt_, tt_, a, ALU.mult, [tk, akey], [tk])
                kb.act(tt_, tt_, AF.Sigmoid, [tk], [tk], scale=1.5957691216057308)
                kb.tt(tt_, tt_, a, ALU.mult, [tk, akey], [tk])
                kb.tt(h[:, :ncol], tt_, u[:, :ncol], ALU.mult, [tk, uk], [hk])
                kb.ld(FH[c * 128:(c + 1) * 128, col0:col0 + ncol], h[:, :ncol], [hk], [('FH', c)])
            return post
        if kb.dbg_level < 3:
            return
        conv3(FA, 'FA', 0, DFF, I['ffn_conv_w'][l], I['ffn_conv_b'][l], post_setup)
        if kb.dbg_level < 4:
            return
        linear(make_prep_load(FH, 'FH', DFF, True), DFF, [(I['ffn_w_down'][l], 0, D, MIXO, 'MIXO', 0)], None)


    def stage_ab(l, resid_from):
        li = l // 2
        linear(make_prep_norm(l, 0, I['norm1_w'][l, :], resid_from), D, [(I['ab_w_in'][li], 0, 5184, PROJA, 'PROJA', 0), (I['ab_w_in'][li], 5184, AB_IN, PROJB, 'PROJB', 0)], None)

        def post_setup():
            os_ = [AR.tile([128, 1024]) for _ in range(2)]
            st = {'i': 0}

            def post(c, col0, ncol, acc, akey):
                o, ok = os_[st['i'] % 2]; st['i'] += 1
                kb.act(o[:, :ncol], acc[:, :ncol], AF.Silu, [akey], [ok])
                kb.ld(XBC[c * 128:(c + 1) * 128, col0:col0 + ncol], o[:, :ncol], [ok], [('XBC', c)])
            return post
        conv3(PROJA, 'PROJA', 2048, 3072, I['ab_conv_w'][li], I['ab_conv_b'][li], post_setup)

        qk, qkk = AR.tile([128, 4096]); ob, obk = AR.tile([128, 4096], BF16)
        ta, tak = AR.tile([128, 16, 64]); tb, tbk = AR.tile([128, 16, 64])
        rt = [AR.tile([128, 64]) for _ in range(4)]
        for (c0, nchk, cd) in kb.seqs:
            for c in range(c0, c0 + nchk):
                kb.ld(qk, PROJB[c * 128:(c + 1) * 128, 0:4096], [('PROJB', c)], [qkk])
                if cd == 0:
                    kb.cp(ob, qk, [qkk], [obk])
                else:
                    p0 = (c - c0) * 128
                    for j in range(4):
                        kb.ld(rt[j][0], I['k_rope'][j, p0:p0 + 128, :], [], [rt[j][1]])
                    qv = qk.rearrange("p (h q f) -> p h q f", q=4, f=64)
                    ov = ob.rearrange("p (h q f) -> p h q f", q=4, f=64)
                    for half in range(2):
                        cs_, csk = rt[2 * half]; sn, snk = rt[2 * half + 1]
                        cb = cs_.unsqueeze(1).to_broadcast([128, 16, 64]); sbb = sn.unsqueeze(1).to_broadcast([128, 16, 64])
                        t1 = qv[:, :, 2 * half, :]; t2 = qv[:, :, 2 * half + 1, :]
                        kb.tt(ta, t1, cb, ALU.mult, [qkk, csk], [tak])
                        kb.tt(tb, t2, sbb, ALU.mult, [qkk, snk], [tbk])
                        kb.tt(ov[:, :, 2 * half, :], ta, tb, ALU.subtract, [tak, tbk], [obk])
                        kb.tt(ta, t1, sbb, ALU.mult, [qkk, snk], [tak])
                        kb.tt(tb, t2, cb, ALU.mult, [qkk, csk], [tbk])
                        kb.tt(ov[:, :, 2 * half + 1, :], ta, tb, ALU.add, [tak, tbk], [obk])
                kb.ts(ob[:, 2048:], ob[:, 2048:], 0.0625, ALU.mult, [obk], [obk])
                kb.ld(QK[c * 128:(c + 1) * 128, :], ob, [obk], [('QK', c)])
        P.barrier(); AR.reset()

        T_ = AR.tile
        arow, ark = T_([128, 64]); dtb, dtbk = T_([128, 64]); Drow, Dk = T_([128, 32]); lgb, lgk = T_([128, 16])
        bcast_row(arow, ark, I['ssd_a_log'][li, :]); bcast_row(dtb, dtbk, I['ssd_dt_bias'][li, :])
        bcast_row(Drow, Dk, I['ssd_d'][li, :]); bcast_row(lgb, lgk, I['ret_log_gamma'][li, :])
        kb.act(arow, arow, AF.Exp, [ark], [ark])
        kb.ts(arow, arow, -1.0, ALU.mult, [ark], [ark])
        DM = [T_([128, 8, 128]) for _ in range(2)]
        GP = [T_([128, 8]) for _ in range(2)]; TE = [T_([128, 8]) for _ in range(2)]; CDr = [T_([128, 8]) for _ in range(2)]
        for d in range(2):
            for h in range(8):
                kb.act(DM[d][0][:, h, :], DIFF[d][:], AF.Exp, ['DIFFF' if d == 0 else 'DIFFB', lgk], [DM[d][1]], scale=lgb[:, d * 8 + h:d * 8 + h + 1])
            kb.act(GP[d][0], lgb[:, d * 8:d * 8 + 8], AF.Exp, [lgk, 'POSC'], [GP[d][1]], scale=POSC[:, d:d + 1])
            kb.act(TE[d][0], lgb[:, d * 8:d * 8 + 8], AF.Exp, [lgk, 'POSC'], [TE[d][1]], scale=POSC[:, 2 + d:3 + d])
            kb.act(CDr[d][0], lgb[:, d * 8:d * 8 + 8], AF.Exp, [lgk], [CDr[d][1]], scale=128.0)
        nws, nwsk = T_([128, D]); nwr, nwrk = T_([128, D])
        bcast_row(nws, nwsk, I['ssd_norm_w'][li, :]); bcast_row(nwr, nwrk, I['ret_norm_w'][li, :])
        HS, HSk = T_([128, 4, 512]); HSb, HSbk = T_([128, 4, 512], BF16)
        HR, HRk = T_([128, 8, 2, 256]); HRb, HRbk = T_([128, 8, 2, 256], BF16)
        xbc, xbck = T_([128, 3072]); bcb, bcbk = T_([128, 1024], BF16)
        vd, vdk = T_([128, 32, 64], BF16); vte, vtek = T_([128, 32, 64], BF16)
        sm, smk = T_([128, 12, 32])
        BCT, BCTk = T_([128, 8, 128], BF16)
        seg, segk = T_([128, 4, 128]); MT, MTk = T_([128, 4, 128], BF16)
        qkb, qkbk = T_([128, 4096], BF16); QKT, QKTk = T_([128, 32, 128], BF16)
        kte, ktek = T_([128, 2048], BF16)
        vt, vtk = T_([128, D]); vb, vbk = T_([128, D], BF16)
        ych, ychk = T_([128, 4096]); yfl, yflk = T_([128, 4096])
        tmp, tmpk = T_([128, 512])
        yg, ygk = T_([128, 4096], BF16)
        st8, st8k = T_([128, 4, 8])
        blk, blkk = T_([128, 256])
        DTR, X0, AB_, EX, L1, DT, LA, ACM, TOT, EAC, TEv, CDv = range(12)

        def chunk(c, d, sample):
            r0 = c * 128
            tri = ULE if d == 0 else LGE
            trik = 'ULE' if d == 0 else 'LGE'
            kb.ld(xbc, XBC[r0:r0 + 128, :], [('XBC', c)], [xbck])
            kb.ld(sm[:, DTR, :], PROJA[r0:r0 + 128, 5120 + d * 32:5152 + d * 32], [('PROJA', c)], [smk])
            kb.ld(qkb, QK[r0:r0 + 128, :], [('QK', c)], [qkbk])
            kb.ld(vt, PROJB[r0:r0 + 128, 4096:6144], [('PROJB', c)], [vtk])
            kb.cp(bcb, xbc[:, 2048:3072], [xbck], [bcbk])
            kb.cp(vb, vt, [vtk], [vbk], eng='act')
            kb.tt(sm[:, X0, :], sm[:, DTR, :], dtb[:, d * 32:d * 32 + 32], ALU.add, [smk, dtbk], [smk])
            kb.ts(sm[:, AB_, :], sm[:, X0, :], -1.0, ALU.mult, [smk], [smk])
            kb.tt(sm[:, AB_, :], sm[:, AB_, :], sm[:, X0, :], ALU.max, [smk], [smk])
            kb.act(sm[:, EX, :], sm[:, AB_, :], AF.Exp, [smk], [smk], scale=-1.0)
            kb.ts(sm[:, EX, :], sm[:, EX, :], 1.0, ALU.add, [smk], [smk])
            kb.act(sm[:, L1, :], sm[:, EX, :], AF.Ln, [smk], [smk])
            kb.ts(sm[:, X0, :], sm[:, X0, :], 0.0, ALU.max, [smk], [smk])
            kb.tt(sm[:, DT, :], sm[:, X0, :], sm[:, L1, :], ALU.add, [smk], [smk])
            kb.tt(sm[:, LA, :], sm[:, DT, :], arow[:, d * 32:d * 32 + 32], ALU.mult, [smk, ark], [smk])
            kb.mmx([(PSI[:, 0:32], tri[:], sm[:, LA, :], True, True), (PSI[:, 32:64], ONES[:], sm[:, LA, :], True, True)],
                   [trik, 'ONES', smk], ['PSI'])
            kb.cp(sm[:, ACM:ACM + 2, :], PSI[:, 0:64].rearrange("p (a b) -> p a b", b=32), ['PSI'], [smk])
            kb.act(sm[:, EAC, :], sm[:, ACM, :], AF.Exp, [smk], [smk])
            kb.tt(sm[:, TEv, :], sm[:, TOT, :], sm[:, ACM, :], ALU.subtract, [smk], [smk])
            kb.act(sm[:, TEv, :], sm[:, TEv, :], AF.Exp, [smk], [smk])
            kb.act(sm[:, CDv, :], sm[:, TOT, :], AF.Exp, [smk], [smk])
            xsv = xbc[:, 0:2048].rearrange("p (h f) -> p h f", f=64)
            kb.tt(vd, xsv, sm[:, DT, :].unsqueeze(2).to_broadcast([128, 32, 64]), ALU.mult, [xbck, smk], [vdk])
            kb.tt(sm[:, X0, :], sm[:, DT, :], sm[:, TEv, :], ALU.mult, [smk], [smk])
            kb.tt(vte, xsv, sm[:, X0, :].unsqueeze(2).to_broadcast([128, 32, 64]), ALU.mult, [xbck, smk], [vtek])
            kb.tr([(PST[:, j * 128:(j + 1) * 128], bcb[:, j * 128:(j + 1) * 128]) for j in range(8)], IDb[:], [bcbk, 'IDb'], ['PST'])
            kb.cp(BCT, PST[:, :].rearrange("p (a b) -> p a b", b=128), ['PST'], [BCTk])
            kb.mmx([(PSS[:, g * 128:(g + 1) * 128], BCT[:, g, :], BCT[:, 4 + g, :], True, True) for g in range(4)], [BCTk], ['PSS'])
            kb.cp(HSb, HS, [HSk], [HSbk])
            for g in range(4):
                for hh in (0, 4):
                    pa = PSA[(hh // 4) % 2]; pak = f'PSA{(hh // 4) % 2}'
                    kb.mmx([(pa[:, j * 128:(j + 1) * 128], sm[:, LA, g * 8 + hh + j:g * 8 + hh + j + 1].to_broadcast([128, 128]), tri[:], True, True)
                            for j in range(4)], [smk, trik], [pak])
                    for j in range(4):
                        h = g * 8 + hh + j
                        kb.stt(seg[:, j, :], pa[:, j * 128:(j + 1) * 128], sm[:, ACM, h:h + 1], MASK[d][:], ALU.subtract, ALU.add,
                               [pak, smk, 'MASKF' if d == 0 else 'MASKB'], [segk])
                    kb.act(seg, seg, AF.Exp, [segk], [segk])
                    kb.tt(MT, seg, PSS[:, g * 128:(g + 1) * 128].unsqueeze(1).to_broadcast([128, 4, 128]), ALU.mult, [segk, 'PSS'], [MTk])
                    kb.mmx([(PSY[:, (hh + j) * 64:(hh + j + 1) * 64], MT[:, j, :], vd[:, g * 8 + hh + j, :], True, True) for j in range(4)],
                           [MTk, vdk], ['PSY'])
                kb.mmx([(PSI[:, :], BCT[:, 4 + g, :], HSb[:, g, :], True, True)], [BCTk, HSbk], ['PSI'])
                kb.tt(tmp.rearrange("p (h f) -> p h f", f=64), PSI[:, :].rearrange("p (h f) -> p h f", f=64),
                      sm[:, EAC, g * 8:g * 8 + 8].unsqueeze(2).to_broadcast([128, 8, 64]), ALU.mult, ['PSI', smk], [tmpk])
                kb.tt(ych[:, g * 512:(g + 1) * 512], PSY[:, :], tmp, ALU.add, ['PSY', tmpk], [ychk])
                kb.mmx([(PSI[:, :], bcb[:, g * 128:(g + 1) * 128], vte[:, g * 8:g * 8 + 8, :].rearrange("p h f -> p (h f)"), True, True)],
                       [bcbk, vtek], ['PSI'])
                hsg = HS[:, g, :].rearrange("p (h f) -> p h f", f=64)
                kb.tt(hsg, hsg, sm[:, CDv, g * 8:g * 8 + 8].unsqueeze(2).to_broadcast([128, 8, 64]), ALU.mult, [HSk, smk], [HSk])
                kb.tt(HS[:, g, :], HS[:, g, :], PSI[:, :], ALU.add, [HSk, 'PSI'], [HSk])
            for rnd in range(4):
                kb.tr([(PST[:, j * 128:(j + 1) * 128], qkb[:, (rnd * 8 + j) * 128:(rnd * 8 + j + 1) * 128]) for j in range(8)],
                      IDb[:], [qkbk, 'IDb'], ['PST'])
                kb.cp(QKT[:, rnd * 8:rnd * 8 + 8, :], PST[:, :].rearrange("p (a b) -> p a b", b=128), ['PST'], [QKTk], eng=alt_eng())
            kb.cp(HRb, HR, [HRk], [HRbk])
            for hq in (0, 4):
                items = []
                for j in range(4):
                    h = hq + j
                    for kc in range(2):
                        items.append((PSS[:, j * 128:(j + 1) * 128], QKT[:, 16 + 2 * h + kc, :], QKT[:, 2 * h + kc, :], kc == 0, kc == 1))
                kb.mmx(items, [QKTk], ['PSS'])
                kb.tt(MT, PSS[:, :].rearrange("p (a b) -> p a b", b=128), DM[d][0][:, hq:hq + 4, :], ALU.mult, ['PSS', DM[d][1]], [MTk])
                for jp in (0, 2):
                    h0 = hq + jp
                    kb.mmx([(PSY[:, jj * 256:(jj + 1) * 256], MT[:, jp + jj, :], vb[:, (h0 + jj) * 256:(h0 + jj + 1) * 256], True, True)
                            for jj in range(2)], [MTk, vbk], ['PSY'])
                    items = []
                    for jj in range(2):
                        for kc in range(2):
                            items.append((PSI[:, jj * 256:(jj + 1) * 256], QKT[:, 2 * (h0 + jj) + kc, :], HRb[:, h0 + jj, kc, :], kc == 0, kc == 1))
                    kb.mmx(items, [QKTk, HRbk], ['PSI'])
                    for jj in range(2):
                        kb.tt(tmp[:, jj * 256:(jj + 1) * 256], PSI[:, jj * 256:(jj + 1) * 256],
                              GP[d][0][:, h0 + jj:h0 + jj + 1].to_broadcast([128, 256]), ALU.mult, ['PSI', GP[d][1]], [tmpk])
                    kb.tt(ych[:, 2048 + h0 * 256:2048 + (h0 + 2) * 256], PSY[:, :], tmp, ALU.add, ['PSY', tmpk], [ychk])
            for h in range(8):
                kb.tt(kte[:, h * 256:(h + 1) * 256], qkb[:, 2048 + h * 256:2048 + (h + 1) * 256],
                      TE[d][0][:, h:h + 1].to_broadcast([128, 256]), ALU.mult, [qkbk, TE[d][1]], [ktek])
                for kc in range(2):
                    kb.mmx([(PSI[:, 0:256], kte[:, h * 256 + kc * 128:h * 256 + (kc + 1) * 128], vb[:, h * 256:(h + 1) * 256], True, True)],
                           [ktek, vbk], ['PSI'])
                    kb.stt(HR[:, h, kc, :], HR[:, h, kc, :], CDr[d][0][:, h:h + 1], PSI[:, 0:256], ALU.mult, ALU.add, [HRk, CDr[d][1], 'PSI'], [HRk])
            if d == 0:
                kb.ld(YF[r0:r0 + 128, :], ych, [ychk], [('YF', c)])
                return
            kb.ld(yfl, YF[r0:r0 + 128, :], [('YF', c)], [yflk])
            kb.tt(ych, ych, yfl, ALU.add, [ychk, yflk], [ychk])
            zt_ = yfl[:, 0:2048]; gt_ = yfl[:, 2048:4096]
            kb.ld(zt_, PROJA[r0:r0 + 128, 0:2048], [('PROJA', c)], [yflk])
            kb.ld(gt_, PROJB[r0:r0 + 128, 6144:8192], [('PROJB', c)], [yflk])
            ys_ = ych[:, 0:2048]
            kb.tt(vt.rearrange("p (h f) -> p h f", f=64), xsv, Drow.unsqueeze(2).to_broadcast([128, 32, 64]), ALU.mult, [xbck, Dk], [vtk])
            kb.tt(ys_, ys_, vt, ALU.add, [ychk, vtk], [ychk])
            kb.act(zt_, zt_, AF.Silu, [yflk], [yflk])
            kb.tt(ys_, ys_, zt_, ALU.mult, [ychk, yflk], [ychk])
            kb.tt(vt, ys_, ys_, ALU.mult, [ychk], [vtk])
            P.op('dve', lambda e: e.reduce_sum(out=st8[:, 0, 0:4], in_=vt.rearrange("p (g f) -> p g f", f=512), axis=AX.X), [vtk], [st8k])
            kb.ts(st8[:, 0, 0:4], st8[:, 0, 0:4], 1.0 / 512, ALU.mult, [st8k], [st8k], s2=EPS, op1=ALU.add)
            kb.act(st8[:, 0, 0:4], st8[:, 0, 0:4], AF.Sqrt, [st8k], [st8k])
            P.op('dve', lambda e: e.reciprocal(out=st8[:, 1, 0:4], in_=st8[:, 0, 0:4]), [st8k], [st8k])
            for g in range(4):
                kb.stt(yg[:, g * 512:(g + 1) * 512], ys_[:, g * 512:(g + 1) * 512], st8[:, 1, g:g + 1], nws[:, g * 512:(g + 1) * 512],
                       ALU.mult, ALU.mult, [ychk, st8k, nwsk], [ygk])
            yr3 = ych[:, 2048:4096].rearrange("p (h f) -> p h f", f=256)
            vt3 = vt.rearrange("p (h f) -> p h f", f=256)
            P.op('dve', lambda e: e.reduce_sum(out=st8[:, 2, :], in_=yr3, axis=AX.X), [ychk], [st8k])
            kb.ts(st8[:, 2, :], st8[:, 2, :], 1.0 / 256, ALU.mult, [st8k], [st8k])
            kb.tt(yr3, yr3, st8[:, 2, :].unsqueeze(2).to_broadcast([128, 8, 256]), ALU.subtract, [ychk, st8k], [ychk])
            kb.tt(vt3, yr3, yr3, ALU.mult, [ychk], [vtk])
            P.op('dve', lambda e: e.reduce_sum(out=st8[:, 3, :], in_=vt3, axis=AX.X), [vtk], [st8k])
            kb.ts(st8[:, 3, :], st8[:, 3, :], 1.0 / 256, ALU.mult, [st8k], [st8k], s2=EPS, op1=ALU.add)
            kb.act(st8[:, 3, :], st8[:, 3, :], AF.Sqrt, [st8k], [st8k])
            P.op('dve', lambda e: e.reciprocal(out=st8[:, 2, :], in_=st8[:, 3, :]), [st8k], [st8k])
            kb.tt(yr3, yr3, st8[:, 2, :].unsqueeze(2).to_broadcast([128, 8, 256]), ALU.mult, [ychk, st8k], [ychk])
            kb.tt(ych[:, 2048:4096], ych[:, 2048:4096], nwr, ALU.mult, [ychk, nwrk], [ychk])
            kb.act(gt_, gt_, AF.Silu, [yflk], [yflk])
            kb.tt(yg[:, 2048:4096], ych[:, 2048:4096], gt_, ALU.mult, [ychk, yflk], [ygk])
            kb.ld(YG[r0:r0 + 128, :], yg, [ygk], [('YG', c)])

        for si, (c0, nchk, cd) in enumerate(kb.seqs):
            sample = (cd == 1)
            for d in range(2):
                if not sample:
                    kb.memset(HS, 0.0, [HSk]); kb.memset(HR, 0.0, [HRk])
                else:
                    for rb in range(16):
                        kb.ld(blk[:, 0:128], I['sssd'][li, d, rb * 128:(rb + 1) * 128, :], [], [blkk])
                        kb.tr([(PSI[:, 0:128], blk[:, 0:128])], IDf[:], [blkk, 'IDf'], ['PSI'])
                        kb.cp(HS[:, rb // 4, (rb % 4) * 128:(rb % 4 + 1) * 128], PSI[:, 0:128], ['PSI'], [HSk])
                    for h in range(8):
                        for dvb in range(2):
                            kb.ld(blk, I['sret'][li, d, h * 256 + dvb * 128:h * 256 + (dvb + 1) * 128, :], [], [blkk])
                            for kc in range(2):
                                kb.tr([(PSI[:, 0:128], blk[:, kc * 128:(kc + 1) * 128])], IDf[:], [blkk, 'IDf'], ['PSI'])
                                kb.cp(HR[:, h, kc, dvb * 128:(dvb + 1) * 128], PSI[:, 0:128], ['PSI'], [HRk])
                order = range(c0, c0 + nchk) if d == 0 else range(c0 + nchk - 1, c0 - 1, -1)
                for c in order:
                    chunk(c, d, sample)
                if not sample:
                    for rb in range(16):
                        kb.tr([(PSI[:, 0:128], HS[:, rb // 4, (rb % 4) * 128:(rb % 4 + 1) * 128])], IDf[:], [HSk, 'IDf'], ['PSI'])
                        kb.cp(blk[:, 0:128], PSI[:, 0:128], ['PSI'], [blkk])
                        kb.ld(O['nssd'][si, li, d, rb * 128:(rb + 1) * 128, :], blk[:, 0:128], [blkk], [('NS', si, li, d, rb)])
                    for h in range(8):
                        for dvb in range(2):
                            for kc in range(2):
                                kb.tr([(PSI[:, 0:128], HR[:, h, kc, dvb * 128:(dvb + 1) * 128])], IDf[:], [HRk, 'IDf'], ['PSI'])
                                kb.cp(blk[:, kc * 128:(kc + 1) * 128], PSI[:, 0:128], ['PSI'], [blkk])
                            kb.ld(O['nret'][si, li, d, h * 256 + dvb * 128:h * 256 + (dvb + 1) * 128, :], blk, [blkk], [('NR', si, li, d, h, dvb)])
        P.barrier(); AR.reset()
        linear(make_prep_load(YG, 'YG', 2 * D, True), 2 * D, [(I['ab_w_out'][li], 0, D, MIXO, 'MIXO', 0)], None)


    def stage_hy(l, resid_from):
        li = l // 2
        T_ = AR.tile
        linear(make_prep_norm(l, 0, I['norm1_w'][l, :], resid_from), D, [(I['hy_w_in'][li], 0, 3 * D, PROJB, 'PROJB', 0)], None)

        def post_setup():
            def post(c, col0, ncol, acc, akey):
                kb.ld(XG[c * 128:(c + 1) * 128, col0:col0 + ncol], acc[:, :ncol], [akey], [('XG', c)])
            return post
        conv3(PROJB, 'PROJB', 0, 3 * D, I['hy_short_w'][li], I['hy_short_b'][li], post_setup)
        MAGIC = 12582912.0
        TWO_PI = 2.0 * math.pi
        for L in LENS:
            nch = L // 128
            N2 = 2 * L
            ft, ftk = T_([128, L]); H1, H1k = T_([128, L]); H2, H2k = T_([128, L])
            w1, w1k = T_([128, 64]); w2, w2k = T_([128, 64]); w3, w3k = T_([128, 4 * D])
            bb, bbk = T_([128, 2]); u, uk = T_([128, 512]); r_, rk = T_([128, 512])
            dl, dlk = T_([128, D]); ng, ngk = T_([128, nch])
            kb.ld(ft[0:33, :], I[f'k_featsT{L}'], [], [ftk])
            kb.ld(w1[0:33, :], I['hy_f_w1'][li], [], [w1k]); kb.ld(w2[0:64, :], I['hy_f_w2'][li], [], [w2k])
            kb.ld(w3[0:64, :], I['hy_f_w3'][li], [], [w3k])
            kb.ld(bb[0:64, 0:1], I['hy_f_b1'][li, :].rearrange("(p o) -> p o", o=1), [], [bbk])
            kb.ld(bb[0:64, 1:2], I['hy_f_b2'][li, :].rearrange("(p o) -> p o", o=1), [], [bbk])
            bcast_row(dl, dlk, I['k_delta']); kb.ld(ng, I[f'k_negt{L}'], [], [ngk])
            for layer in range(2):
                src, srck, kk = (ft, ftk, 33) if layer == 0 else (H1, H1k, 64)
                wt, wtk = (w1, w1k) if layer == 0 else (w2, w2k)
                dstT, dstk = (H1, H1k) if layer == 0 else (H2, H2k)
                for b0 in range(0, L, 512):
                    bw = min(512, L - b0)
                    kb.mmx([(PSL[0][0:64, :bw], wt[0:kk, :], src[0:kk, b0:b0 + bw], True, True)], [wtk, srck], ['PSL0'])
                    kb.act(u[0:64, :bw], PSL[0][0:64, :bw], AF.Identity, ['PSL0', bbk], [uk], bias=bb[0:64, layer:layer + 1])
                    kb.ts(r_[0:64, :bw], u[0:64, :bw], 1.0 / TWO_PI, ALU.mult, [uk], [rk], s2=MAGIC, op1=ALU.add)
                    kb.ts(r_[0:64, :bw], r_[0:64, :bw], -MAGIC, ALU.add, [rk], [rk])
                    kb.stt(u[0:64, :bw], r_[0:64, :bw], -TWO_PI, u[0:64, :bw], ALU.mult, ALU.add, [rk, uk], [uk])
                    kb.ts(u[0:64, :bw], u[0:64, :bw], -3.14159, ALU.max, [uk], [uk], s2=3.14159, op1=ALU.min)
                    kb.act(dstT[0:64, b0:b0 + bw], u[0:64, :bw], AF.Sin, [uk], [dstk])
            win, wink = T_([128, 512]); hw, hwk = T_([128, 512]); ab, abk = T_([128, 512]); rn, rnk = T_([128, 512])
            for cbk in range(16):
                ccol = (cbk * 512) % D
                for tch in range(nch):
                    kb.mmx([(PSL[0][:, :], H2[0:64, tch * 128:(tch + 1) * 128], w3[0:64, cbk * 512:(cbk + 1) * 512], True, True)],
                           [H2k, w3k], ['PSL0'])
                    kb.act(win, dl[:, ccol:ccol + 512], AF.Exp, [dlk, ngk], [wink], scale=ng[:, tch:tch + 1])
                    kb.tt(hw, PSL[0][:, :], win, ALU.mult, ['PSL0', wink], [hwk])
                    kb.ld(FILT[L][tch * 128:(tch + 1) * 128, cbk * 512:(cbk + 1) * 512], hw, [hwk], [('FILT', tch, cbk)])
                    kb.ts(ab, hw, -1.0, ALU.mult, [hwk], [abk])
                    kb.tt(ab, ab, hw, ALU.max, [abk, hwk], [abk])
                    kb.mmx([(PSA[0][:, :], ONES[:], ab, tch == 0, tch == nch - 1)], ['ONES', abk], ['PSA0'])
                kb.ts(rn, PSA[0][:, :], EPS, ALU.add, ['PSA0'], [rnk])
                P.op('dve', lambda e, rn=rn: e.reciprocal(out=rn, in_=rn), [rnk], [rnk])
                kb.ld(RN[L][cbk * 512:(cbk + 1) * 512].rearrange("(o f) -> o f", o=1), rn[0:1, :], [rnk], [('RN', cbk)])
            P.barrier(); AR.reset()
            CW = 256
            FW = min(512, L)
            hp, hpk = T_([128, nch, CW], BF16); hm, hmk = T_([128, nch, CW], BF16)
            Cb, Cbk = T_([128, nch, FW], BF16); Sb, Sbk = T_([128, nch, FW], BF16)
            hf, hfk = T_([128, CW]); hb_, hbk = T_([128, CW]); rf, rfk = T_([128, CW]); rb, rbk = T_([128, CW])
            fr, frk = T_([128, CW]); fi, fik = T_([128, CW])
            cv = I[f'k_dftc{L}'].rearrange("(tc p) f -> p tc f", p=128)
            sv = I[f'k_dfts{L}'].rearrange("(tc p) f -> p tc f", p=128)
            stv = I[f'k_dftst{L}'].rearrange("(tc p) f -> p tc f", p=128)
            for o in range(2):
                for cb in range(D // CW):
                    colf = (0 * 2 + o) * D + cb * CW; colb = (1 * 2 + o) * D + cb * CW
                    bcast_row(rf, rfk, RN[L][colf:colf + CW], [('RN', colf // 512)])
                    bcast_row(rb, rbk, RN[L][colb:colb + CW], [('RN', colb // 512)])
                    for tch in range(nch):
                        kb.ld(hf, FILT[L][tch * 128:(tch + 1) * 128, colf:colf + CW], [('FILT', tch, colf // 512)], [hfk])
                        kb.ld(hb_, FILT[L][tch * 128:(tch + 1) * 128, colb:colb + CW], [('FILT', tch, colb // 512)], [hbk])
                        if tch == 0:
                            kb.ld(hb_[0:1, :], ZERO[:, :CW], ['ZERO'], [hbk])
                        kb.tt(hf, hf, rf, ALU.mult, [hfk, rfk], [hfk])
                        kb.tt(hb_, hb_, rb, ALU.mult, [hbk, rbk], [hbk])
                        kb.tt(hp[:, tch, :], hf, hb_, ALU.add, [hfk, hbk], [hpk])
                        kb.tt(hm[:, tch, :], hf, hb_, ALU.subtract, [hfk, hbk], [hmk])
                    for f0 in range(0, L, FW):
                        kb.ld(Cb, cv[:, :, f0:f0 + FW], [], [Cbk], q='pool')
                        kb.ld(Sb, sv[:, :, f0:f0 + FW], [], [Sbk], q='pool')
                        for fc in range(FW // 128):
                            fch = f0 // 128 + fc
                            kb.mmx([(PSL[0][:, :CW], Cb[:, t_, fc * 128:(fc + 1) * 128], hp[:, t_, :], t_ == 0, t_ == nch - 1) for t_ in range(nch)],
                                   [Cbk, hpk], ['PSL0'])
                            kb.mmx([(PSL[1][:, :CW], Sb[:, t_, fc * 128:(fc + 1) * 128], hm[:, t_, :], t_ == 0, t_ == nch - 1) for t_ in range(nch)],
                                   [Sbk, hmk], ['PSL1'])
                            kb.ts(fr, PSL[0][:, :CW], 2.0 / N2, ALU.mult, ['PSL0'], [frk])
                            kb.ts(fi, PSL[1][:, :CW], 2.0 / N2, ALU.mult, ['PSL1'], [fik])
                            if fch == 0:
                                kb.mmx([(PSA[1][:, :CW], ALTb[:], hp[:, t_, :], t_ == 0, t_ == nch - 1) for t_ in range(nch)], ['ALTb', hpk], ['PSA1'])
                                kb.ts(fi[0:1, :], PSA[1][0:1, :CW], 1.0 / N2, ALU.mult, ['PSA1', fik], [fik])
                                kb.ts(fr[0:1, :], fr[0:1, :], 0.5, ALU.mult, [frk], [frk])
                            kb.ld(FSP[L][o, 0, fch * 128:(fch + 1) * 128, cb * CW:(cb + 1) * CW], fr, [frk], [('FSP', o, 0, fch, cb)])
                            kb.ld(FSP[L][o, 1, fch * 128:(fch + 1) * 128, cb * CW:(cb + 1) * CW], fi, [fik], [('FSP', o, 1, fch, cb)])
            P.barrier(); AR.reset()
            zb, zbk = T_([128, nch, CW], BF16); Yb, Ybk = T_([128, nch, 2, CW], BF16)
            Cb, Cbk = T_([128, nch, FW], BF16); Sb, Sbk = T_([128, nch, FW], BF16)
            zf, zfk = T_([128, CW]); gt, gtk = T_([128, CW]); brow, browk = T_([128, CW])
            fr, frk = T_([128, CW]); fi, fik = T_([128, CW]); t1, t1k = T_([128, CW]); t2, t2k = T_([128, CW])
            for (c0, nchk, cd) in kb.seqs:
                if nchk != nch:
                    continue
                for o in range(2):
                    zsrc, zname, zcol = (XG, 'XG', 2 * D) if o == 0 else (Z1, 'Z1', 0)
                    gcol = o * D
                    zdst, zdname = (Z1, 'Z1') if o == 0 else (Z2, 'Z2')
                    for cb in range(D // CW):
                        bcast_row(brow, browk, I['hy_bias'][li, o, cb * CW:(cb + 1) * CW])
                        for tch in range(nch):
                            c = c0 + tch
                            kb.ld(zf, zsrc[c * 128:(c + 1) * 128, zcol + cb * CW:zcol + (cb + 1) * CW], [(zname, c)], [zfk])
                            kb.cp(zb[:, tch, :], zf, [zfk], [zbk], eng=alt_eng())
                        for f0 in range(0, L, FW):
                            kb.ld(Cb, cv[:, :, f0:f0 + FW], [], [Cbk], q='pool')
                            kb.ld(Sb, sv[:, :, f0:f0 + FW], [], [Sbk], q='pool')
                            for fc in range(FW // 128):
                                fch = f0 // 128 + fc
                                kb.mmx([(PSL[0][:, :CW], Cb[:, t_, fc * 128:(fc + 1) * 128], zb[:, t_, :], t_ == 0, t_ == nch - 1) for t_ in range(nch)],
                                       [Cbk, zbk], ['PSL0'])
                                kb.mmx([(PSL[1][:, :CW], Sb[:, t_, fc * 128:(fc + 1) * 128], zb[:, t_, :], t_ == 0, t_ == nch - 1) for t_ in range(nch)],
                                       [Sbk, zbk], ['PSL1'])
                                kb.ld(fr, FSP[L][o, 0, fch * 128:(fch + 1) * 128, cb * CW:(cb + 1) * CW], [('FSP', o, 0, fch, cb)], [frk])
                                kb.ld(fi, FSP[L][o, 1, fch * 128:(fch + 1) * 128, cb * CW:(cb + 1) * CW], [('FSP', o, 1, fch, cb)], [fik])
                                kb.tt(t1, PSL[0][:, :CW], fr, ALU.mult, ['PSL0', frk], [t1k])
                                kb.tt(t2, PSL[1][:, :CW], fi, ALU.mult, ['PSL1', fik], [t2k])
                                kb.tt(Yb[:, fch, 0, :], t1, t2, ALU.subtract, [t1k, t2k], [Ybk])
                                if fch == 0:
                                    kb.cp(Yb[0:1, 0, 0, :], t1[0:1, :], [t1k], [Ybk])
                                    kb.cp(Yb[0:1, 0, 1, :], t2[0:1, :], [t2k], [Ybk])
                                kb.tt(t1, PSL[0][:, :CW], fi, ALU.mult, ['PSL0', fik], [t1k])
                                kb.tt(t2, PSL[1][:, :CW], fr, ALU.mult, ['PSL1', frk], [t2k])
                                if fch == 0:
                                    kb.tt(Yb[1:128, fch, 1, :], t1[1:128, :], t2[1:128, :], ALU.add, [t1k, t2k], [Ybk]) if False else None
                                    kb.tt(t1, t1, t2, ALU.add, [t1k, t2k], [t1k])
                                    kb.cp(t1[0:1, :], Yb[0:1, 0, 1, :], [Ybk], [t1k])
                                    kb.cp(Yb[:, fch, 1, :], t1, [t1k], [Ybk])
                                else:
                                    kb.tt(Yb[:, fch, 1, :], t1, t2, ALU.add, [t1k, t2k], [Ybk])
                        for t0 in range(0, L, FW):
                            kb.ld(Cb, cv[:, :, t0:t0 + FW], [], [Cbk], q='pool')
                            kb.ld(Sb, stv[:, :, t0:t0 + FW], [], [Sbk], q='pool')
                            for tc in range(FW // 128):
                                tch = t0 // 128 + tc
                                c = c0 + tch
                                items = []
                                for f_ in range(nch):
                                    items.append((PSL[0][:, :CW], Cb[:, f_, tc * 128:(tc + 1) * 128], Yb[:, f_, 0, :], f_ == 0, False))
                                    items.append((PSL[0][:, :CW], Sb[:, f_, tc * 128:(tc + 1) * 128], Yb[:, f_, 1, :], False, f_ == nch - 1))
                                kb.mmx(items, [Cbk, Sbk, Ybk], ['PSL0'])
                                kb.ld(zf, zsrc[c * 128:(c + 1) * 128, zcol + cb * CW:zcol + (cb + 1) * CW], [(zname, c)], [zfk])
                                kb.ld(gt, XG[c * 128:(c + 1) * 128, gcol + cb * CW:gcol + (cb + 1) * CW], [('XG', c)], [gtk])
                                kb.tt(zf, zf, brow, ALU.mult, [zfk, browk], [zfk])
                                kb.tt(zf, zf, PSL[0][:, :CW], ALU.add, [zfk, 'PSL0'], [zfk])
                                kb.tt(zf, zf, gt, ALU.mult, [zfk, gtk], [zfk])
                                kb.ld(zdst[c * 128:(c + 1) * 128, cb * CW:(cb + 1) * CW], zf, [zfk], [(zdname, c)])
            P.barrier(); AR.reset()
        linear(make_prep_load(Z2, 'Z2', D, False), D, [(I['hy_w_out'][li], 0, D, MIXO, 'MIXO', 0)], None)

    def stage_final(resid_from):
        nw, nk = AR.tile([128, D]); bcast_row(nw, nk, I['final_norm_w'])
        grow = []
        for cd in range(2):
            g, gk = AR.tile([128, D])
            if resid_from is not None:
                gl, gidx = resid_from
                bcast_row(g, gk, MODD[gl * 2 + cd, gidx * D:(gidx + 1) * D], ['MODD'])
            grow.append((g, gk))
        xts = [AR.tile([128, D]) for _ in range(2)]
        mts = [AR.tile([128, D]) for _ in range(2)]
        sts = [AR.tile([128, 4]) for _ in range(2)]
        for c in range(NCH):
            i = c % 2
            cd = kb.cond_of[c]
            xt, xk = xts[i]; mt, mk = mts[i]; s4, s4k = sts[i]
            kb.ld(xt, X[c * 128:(c + 1) * 128, :], [('X', c)], [xk])
            if resid_from is not None:
                kb.ld(mt, MIXO[c * 128:(c + 1) * 128, :], [('MIXO', c)], [mk])
                g, gk = grow[cd]
                kb.tt(mt, mt, g, ALU.mult, [mk, gk], [mk])
                kb.tt(xt, xt, mt, ALU.add, [xk, mk], [xk])
            kb.tt(mt, xt, xt, ALU.mult, [xk], [mk])
            P.op('dve', lambda e, s4=s4, mt=mt: e.reduce_sum(out=s4[:, 0:1], in_=mt, axis=AX.X), [mk], [s4k])
            kb.ts(s4[:, 1:2], s4[:, 0:1], 1.0 / D, ALU.mult, [s4k], [s4k], s2=EPS, op1=ALU.add)
            kb.act(s4[:, 2:3], s4[:, 1:2], AF.Sqrt, [s4k], [s4k])
            P.op('dve', lambda e, s4=s4: e.reciprocal(out=s4[:, 3:4], in_=s4[:, 2:3]), [s4k], [s4k])
            kb.stt(mt, xt, s4[:, 3:4], nw, ALU.mult, ALU.mult, [xk, s4k, nk], [mk])
            r0 = c * 128
            if r0 < NP * 256:
                dst = O['yp'][r0:r0 + 128, :]
            else:
                dst = O['ys'][r0 - NP * 256:r0 - NP * 256 + 128, :]
            kb.ld(dst, mt, [mk], [('OUT', c)])
        P.barrier(); AR.reset()

    kb._stuff = dict(stage_final=stage_final, stage_ab=stage_ab, stage_hy=stage_hy, I=I, O=O, X=X, MODD=MODD, XBC=XBC, QK=QK, YF=YF, YG=YG, MIXO=MIXO, XG=XG, Z1=Z1, Z2=Z2,
                     FILT=FILT, RN=RN, FSP=FSP, ZERO=ZERO, AR=AR, IDf=IDf, IDb=IDb, ULE=ULE, LGE=LGE, ONES=ONES, MASK=MASK,
                     DIFF=DIFF, POSC=POSC, ALTb=ALTb, PSL=PSL, PST=PST, PSA=PSA, PSS=PSS, PSY=PSY, PSI=PSI,
                     linear=linear, make_prep_norm=make_prep_norm, make_prep_load=make_prep_load, conv3=conv3,
                     bcast_row=bcast_row, alt_eng=alt_eng, stage_mod=stage_mod, stage_ffn=stage_ffn, LENS=LENS)
    return kb


def assemble(kb, mixers=True):
    S = kb._stuff
    if kb.dbg_level >= 1:
        S['stage_mod']()
    pending = None
    for l in range(kb.DEPTH):
        if mixers:
            if l % 2 == 0:
                S['stage_ab'](l, pending)
            else:
                S['stage_hy'](l, pending)
            pending = (l, 2)
        S['stage_ffn'](l, pending)
        pending = (l, 5)
    if kb.dbg_level >= 5:
        S['stage_final'](pending)
    kb.P.emit(kb.es)
    kb.es.close()
    return kb.nc


def make_consts(LS):
    K = {}
    j = np.arange(128)[:, None]; i = np.arange(128)[None, :]
    K['k_ident'] = np.eye(128, dtype=np.float32)
    K['k_ule'] = (j <= i).astype(np.float32)
    K['k_lge'] = (j >= i).astype(np.float32)
    K['k_maskf'] = np.where(i >= j, 0.0, -1e30).astype(np.float32)
    K['k_maskb'] = np.where(i <= j, 0.0, -1e30).astype(np.float32)
    K['k_difff'] = np.where(i >= j, (i - j).astype(np.float64), 1e9).astype(np.float32)
    K['k_diffb'] = np.where(j >= i, (j - i).astype(np.float64), 1e9).astype(np.float32)
    p = np.arange(128, dtype=np.float64)
    K['k_posc'] = np.stack([p + 1, 128 - p, 127 - p, p], axis=1).astype(np.float32)
    K['k_alt'] = np.broadcast_to(((-1.0) ** p)[:, None], (128, 128)).astype(np.float32).copy()
    n = 64
    inv = (np.float32(10000.0) ** (-np.arange(n, dtype=np.float32) / np.float32(n))).astype(np.float32)
    t = np.arange(LS)
    rows = (t // 64).astype(np.float32); cols = (t % 64).astype(np.float32)
    angr = (rows[:, None] * inv[None, :]).astype(np.float32); angc = (cols[:, None] * inv[None, :]).astype(np.float32)
    K['k_rope'] = np.stack([np.cos(angr), np.sin(angr), np.cos(angc), np.sin(angc)]).astype(np.float32)
    K['k_delta'] = np.abs(np.linspace(math.log(1e-2) / 1.5, math.log(1e-2) / 0.3, 2048, dtype=np.float32)).astype(np.float32)
    for L in sorted({256, LS}):
        tt = (np.arange(L, dtype=np.float32) / np.float32(L)).astype(np.float32)
        bands = np.linspace(1e-4, 15, 16, dtype=np.float32)
        ang = (np.float32(2 * math.pi) * tt[:, None] * bands[None, :]).astype(np.float32)
        feats = np.concatenate([tt[:, None], np.cos(ang), np.sin(ang)], axis=-1).astype(np.float32)
        K[f'k_featsT{L}'] = np.ascontiguousarray(feats.T)
        K[f'k_negt{L}'] = np.ascontiguousarray((-tt).reshape(L // 128, 128).T)
        N = 2 * L
        tf = (np.arange(L)[:, None] * np.arange(L)[None, :]) % N
        ang2 = 2.0 * np.pi * tf / N
        Cm = np.cos(ang2); Sm = -np.sin(ang2)
        Sm[:, 0] = (-1.0) ** np.arange(L)
        K[f'k_dftc{L}'] = Cm.astype(np.float32)
        K[f'k_dfts{L}'] = Sm.astype(np.float32)
        K[f'k_dftst{L}'] = np.ascontiguousarray(Sm.T).astype(np.float32)
    return K


def core_inputs(inputs, K, core, NP, b, DEPTH=4):
    f = lambda a: np.ascontiguousarray(np.asarray(a, dtype=np.float32))
    m = {}
    m['xp'] = f(inputs['x_prompt'][core * NP:(core + 1) * NP]).reshape(NP * 256, D)
    m['xs'] = f(inputs['x_sample'][b])
    m['cvec'] = f(np.stack([np.asarray(inputs['c_ctx']), np.asarray(inputs['c'])[b]]))
    m['sssd'] = f(inputs['state_ssd'][b]).reshape(2, 2, 2048, 128)
    m['sret'] = f(inputs['state_ret'][b]).reshape(2, 2, 2048, 256)
    for k in ['ada_w', 'ada_b', 'norm1_w', 'norm2_w', 'ab_w_in', 'ab_conv_w', 'ab_conv_b', 'ssd_norm_w', 'ret_norm_w',
              'ab_w_out', 'hy_w_in', 'hy_short_w', 'hy_short_b', 'hy_f_w1', 'hy_f_b1', 'hy_f_w2', 'hy_f_b2', 'hy_f_w3',
              'hy_bias', 'hy_w_out', 'ffn_w_gate', 'ffn_w_up', 'ffn_conv_w', 'ffn_conv_b', 'ffn_w_down', 'final_norm_w', 'ssd_d']:
        m[k] = f(inputs[k])
    m['ssd_dt_bias'] = f(inputs['ssd_dt_bias']).reshape(2, 64)
    m['ssd_a_log'] = f(inputs['ssd_a_log']).reshape(2, 64)
    m['ret_log_gamma'] = f(inputs['ret_log_gamma']).reshape(2, 16)
    NLA = (DEPTH + 1) // 2; NLH = max(DEPTH // 2, 1)
    for k in list(m.keys()):
        if k in ('ada_w', 'ada_b', 'norm1_w', 'norm2_w', 'ffn_w_gate', 'ffn_w_up', 'ffn_conv_w', 'ffn_conv_b', 'ffn_w_down'):
            m[k] = np.ascontiguousarray(m[k][:DEPTH])
        elif k.startswith('ab_') or k.startswith('ssd_') or k.startswith('ret_') or k in ('sssd', 'sret'):
            m[k] = np.ascontiguousarray(m[k][:NLA])
        elif k.startswith('hy_'):
            m[k] = np.ascontiguousarray(m[k][:NLH])
    m.update(K)
    return m


_NC_CACHE = {}


def kernel(**inputs):
    NCORES, NP, LS, DEPTH = 2, 8, 4096, 4
    key = (NP, LS, DEPTH)
    if key not in _NC_CACHE:
        _NC_CACHE[key] = assemble(build(NP, LS, DEPTH))
    nc = _NC_CACHE[key]
    K = make_consts(LS)
    in_maps = [core_inputs(inputs, K, c, NP, c % 2) for c in range(NCORES)]
    res = run_bass_kernel_spmd(nc, in_maps, core_ids=list(range(NCORES)))
    R = res.results
    y_prompt = np.concatenate([R[c]['yp'].reshape(NP, 256, D) for c in range(NCORES)], axis=0)
    y_sample = np.stack([R[0]['ys'], R[1]['ys']], axis=0)
    nssd = np.concatenate([R[c]['nssd'].reshape(NP, 2, 2, 32, 64, 128) for c in range(NCORES)], axis=0)
    nret = np.concatenate([R[c]['nret'].reshape(NP, 2, 2, 8, 256, 256) for c in range(NCORES)], axis=0)
    return (y_prompt.astype(np.float32), y_sample.astype(np.float32), nssd.astype(np.float32), nret.astype(np.float32))
```

```python
import contextlib
import math
import numpy as np
import concourse.bass as bass
import concourse.mybir as mybir
from concourse.bass_utils import run_bass_kernel_spmd

F32 = mybir.dt.float32
BF16 = mybir.dt.bfloat16
AF = mybir.ActivationFunctionType
ALU = mybir.AluOpType
AX = mybir.AxisListType

D = 2048
AB_IN = 13376
DFF = 5632
EPS = 1e-6
NDQ = 8


class Prog:
    def __init__(self, nc):
        self.nc = nc
        self.names = ['pe', 'act', 'dve', 'pool', 'sp']
        self.ops = {e: [] for e in self.names}
        self.cnt = {e: 0 for e in self.names}
        self.dma_i = {'sp': 0, 'pool': 0}
        self.waited = {e: {} for e in self.names}
        self.lastw = {}
        self.readers = {}
        self.semnames = list(self.names) + [f"dq_{q}_{i}" for q in ('sp', 'pool') for i in range(NDQ)]
        self.semval = {s: 0 for s in self.semnames}

    def _need(self, eng, deps):
        out = []
        wd = self.waited[eng]
        for s, v in deps.items():
            if v > wd.get(s, 0):
                wd[s] = v
                out.append((s, v))
        return out

    def _collect(self, r, w):
        deps = {}

        def add(ev):
            if ev is None:
                return
            s, v = ev
            if v > deps.get(s, 0):
                deps[s] = v
        for k in list(r) + list(w):
            add(self.lastw.get(k))
        for k in w:
            for s, v in self.readers.get(k, {}).items():
                add((s, v))
        return deps

    def _commit(self, ev, r, w):
        for k in w:
            self.lastw[k] = ev
            self.readers[k] = {}
        for k in r:
            if k in w:
                continue
            d = self.readers.setdefault(k, {})
            if ev[1] > d.get(ev[0], 0):
                d[ev[0]] = ev[1]

    def op(self, eng, fn, r=(), w=()):
        deps = self._collect(r, w)
        waits = self._need(eng, deps)
        self.cnt[eng] += 1
        ev = (eng, self.cnt[eng])
        self.semval[eng] = self.cnt[eng]
        self.ops[eng].append((waits, fn, (eng, 1)))
        self._commit(ev, r, w)

    def dma(self, q, out, in_, r=(), w=(), **kw):
        deps = self._collect(r, w)
        i = self.dma_i[q]
        self.dma_i[q] += 1
        s = f"dq_{q}_{i % NDQ}"
        prev = 16 * (i // NDQ)
        if prev > 0:
            deps[s] = max(deps.get(s, 0), prev)
        waits = self._need(q, deps)
        ev = (s, prev + 16)
        self.semval[s] = prev + 16

        def fn(e, out=out, in_=in_, kw=kw):
            return e.dma_start(out=out, in_=in_, **kw)
        self.ops[q].append((waits, fn, (s, 16)))
        self._commit(ev, r, w)

    def barrier(self):
        allv = {s: v for s, v in self.semval.items() if v > 0}
        for e in self.names:
            waits = self._need(e, dict(allv))
            if waits:
                self.ops[e].append((waits, None, None))
        self.lastw = {}
        self.readers = {}

    def emit(self, es):
        nc = self.nc
        sems = {s: es.enter_context(nc.semaphore(s)) for s in self.semnames}
        block = es.enter_context(nc.Block())
        engs = {'pe': block.tensor, 'act': block.scalar, 'dve': block.vector, 'pool': block.gpsimd, 'sp': block.sync}
        final = {s: v for s, v in self.semval.items() if v > 0}
        for name in self.names:
            ops = self.ops[name]
            last = (name == 'sp')

            def body(e, ops=ops, last=last):
                for waits, fn, inc in ops:
                    for s, v in waits:
                        e.wait_ge(sems[s], v)
                    if fn is not None:
                        ins = fn(e)
                        ins.then_inc(sems[inc[0]], inc[1])
                if last:
                    for s, v in final.items():
                        e.wait_ge(sems[s], v)
            engs[name](body)


class KB:
    def __init__(self, NP, LS, DEPTH, dbg=None):
        self.NP, self.LS, self.DEPTH = NP, LS, DEPTH
        self.T = NP * 256 + LS
        self.NCH = self.T // 128
        self.seqs = [(2 * p, 2, 0) for p in range(NP)] + [(2 * NP, LS // 128, 1)]
        self.cond_of = []
        for c0, n, cd in self.seqs:
            self.cond_of += [cd] * n
        self.nc = bass.Bass("TRN2", target_bir_lowering=False)
        self.P = Prog(self.nc)
        self.es = contextlib.ExitStack()
        self.inputs = {}
        self.dbg = dbg or {}
        self.dbg_level = 99
        self._uid = 0

    def inp(self, name, shape, dtype=F32):
        ap = self.nc.dram_tensor(name, list(shape), dtype, kind="ExternalInput").ap()
        self.inputs[name] = ap
        return ap

    def outp(self, name, shape, dtype=F32):
        return self.nc.dram_tensor(name, list(shape), dtype, kind="ExternalOutput").ap()

    def scr(self, name, shape, dtype=F32):
        return self.nc.dram_tensor(name, list(shape), dtype, kind="Internal").ap()

    def sb(self, name, shape, dtype=F32):
        return self.es.enter_context(self.nc.sbuf_tensor(name, list(shape), dtype))

    def ps(self, name, shape, dtype=F32):
        return self.es.enter_context(self.nc.psum_tensor(name, list(shape), dtype))

    def act(self, out, in_, func, r, w, bias=None, scale=None, accum_out=None):
        kw = {}
        if bias is not None:
            kw['bias'] = bias
        if scale is not None:
            kw['scale'] = scale
        if accum_out is not None:
            kw['accum_out'] = accum_out
        self.P.op('act', lambda e: e.activation(out=out, in_=in_, func=func, **kw), r, w)

    def tt(self, out, in0, in1, op, r, w, eng='dve'):
        self.P.op(eng, lambda e: e.tensor_tensor(out=out, in0=in0, in1=in1, op=op), r, w)

    def ts(self, out, in0, s1, op0, r, w, s2=None, op1=None, eng='dve'):
        if op1 is None:
            self.P.op(eng, lambda e: e.tensor_scalar(out=out, in0=in0, scalar1=s1, scalar2=None, op0=op0), r, w)
        else:
            self.P.op(eng, lambda e: e.tensor_scalar(out=out, in0=in0, scalar1=s1, scalar2=s2, op0=op0, op1=op1), r, w)

    def stt(self, out, in0, scalar, in1, op0, op1, r, w):
        self.P.op('dve', lambda e: e.scalar_tensor_tensor(out=out, in0=in0, scalar=scalar, in1=in1, op0=op0, op1=op1), r, w)

    def cp(self, out, in_, r, w, eng='dve'):
        if eng == 'act':
            self.P.op('act', lambda e: e.copy(out=out, in_=in_), r, w)
        else:
            self.P.op(eng, lambda e: e.tensor_copy(out=out, in_=in_), r, w)

    def memset(self, out, val, w):
        self.P.op('dve', lambda e: e.memset(out, val), (), w)

    def mm(self, out, pairs, r, w, transpose=False):
        n = len(pairs)

        def fn(e):
            ins = None
            for i, (l, rr) in enumerate(pairs):
                ins = e.matmul(out, lhsT=l, rhs=rr, start=(i == 0), stop=(i == n - 1))
            return ins
        self.P.op('pe', fn, r, w)

    def mmx(self, items, r, w):
        def fn(e):
            ins = None
            for (o, l, rr, st, sp) in items:
                ins = e.matmul(o, lhsT=l, rhs=rr, start=st, stop=sp)
            return ins
        self.P.op('pe', fn, r, w)

    def tr(self, outs_ins, ident, r, w):
        def fn(e):
            ins = None
            for o, i in outs_ins:
                ins = e.transpose(o, i, ident)
            return ins
        self.P.op('pe', fn, r, w)

    def ld(self, out, in_, r, w, q='sp', **kw):
        self.P.dma(q, out, in_, r, w, **kw)


class Arena:
    def __init__(self, kb, sizes):
        self.regs = [kb.sb(f"arena{i}", [128, n], F32) for i, n in enumerate(sizes)]
        self.sizes = sizes
        self.reset()
        self.uid = 0

    def reset(self):
        self.off = [0] * len(self.regs)

    def tile(self, shape, dtype=F32):
        n = 1
        for s in shape[1:]:
            n *= s
        nw = n if dtype == F32 else (n + 1) // 2
        for i, reg in enumerate(self.regs):
            if self.off[i] + nw <= self.sizes[i]:
                ap = reg[:, self.off[i]:self.off[i] + nw]
                self.off[i] += nw
                if dtype != F32:
                    ap = ap.bitcast(dtype)
                    if n != nw * 2:
                        ap = ap[:, :n]
                if len(shape) == 3:
                    ap = ap.rearrange("p (a b) -> p a b", b=shape[2])
                elif len(shape) == 4:
                    ap = ap.rearrange("p (a b c) -> p a b c", b=shape[2], c=shape[3])
                self.uid += 1
                return ap, f"t{self.uid}"
        raise RuntimeError(f"arena full for {shape}")


def build(NP, LS, DEPTH, dbg=False):
    kb = KB(NP, LS, DEPTH)
    nc, P = kb.nc, kb.P
    T, NCH = kb.T, kb.NCH
    NLA = (DEPTH + 1) // 2
    NLC = DEPTH // 2
    LENS = sorted({256, LS})

    I = {}
    def inp(name, shape):
        I[name] = kb.inp(name, shape)
    inp('xp', [NP * 256, D]); inp('xs', [LS, D]); inp('cvec', [2, D])
    inp('sssd', [NLA, 2, 2048, 128]); inp('sret', [NLA, 2, 2048, 256])
    inp('ada_w', [DEPTH, D, 6 * D]); inp('ada_b', [DEPTH, 6 * D]); inp('norm1_w', [DEPTH, D]); inp('norm2_w', [DEPTH, D])
    inp('ab_w_in', [NLA, D, AB_IN]); inp('ab_conv_w', [NLA, 3, 3072]); inp('ab_conv_b', [NLA, 3072])
    inp('ssd_dt_bias', [NLA, 64]); inp('ssd_a_log', [NLA, 64]); inp('ssd_d', [NLA, 32]); inp('ssd_norm_w', [NLA, D])
    inp('ret_log_gamma', [NLA, 16]); inp('ret_norm_w', [NLA, D]); inp('ab_w_out', [NLA, 2 * D, D])
    NLH = max(NLC, 1)
    inp('hy_w_in', [NLH, D, 3 * D]); inp('hy_short_w', [NLH, 3, 3 * D]); inp('hy_short_b', [NLH, 3 * D])
    inp('hy_f_w1', [NLH, 33, 64]); inp('hy_f_b1', [NLH, 64]); inp('hy_f_w2', [NLH, 64, 64]); inp('hy_f_b2', [NLH, 64])
    inp('hy_f_w3', [NLH, 64, 4 * D]); inp('hy_bias', [NLH, 2, D]); inp('hy_w_out', [NLH, D, D])
    inp('ffn_w_gate', [DEPTH, D, DFF]); inp('ffn_w_up', [DEPTH, D, DFF]); inp('ffn_conv_w', [DEPTH, 3, DFF])
    inp('ffn_conv_b', [DEPTH, DFF]); inp('ffn_w_down', [DEPTH, DFF, D]); inp('final_norm_w', [D])
    inp('k_ident', [128, 128]); inp('k_ule', [128, 128]); inp('k_lge', [128, 128])
    inp('k_maskf', [128, 128]); inp('k_maskb', [128, 128]); inp('k_difff', [128, 128]); inp('k_diffb', [128, 128])
    inp('k_posc', [128, 4]); inp('k_alt', [128, 128])
    inp('k_rope', [4, LS, 64]); inp('k_delta', [D])
    for L in LENS:
        inp(f'k_featsT{L}', [33, L]); inp(f'k_negt{L}', [128, L // 128])
        inp(f'k_dftc{L}', [L, L]); inp(f'k_dfts{L}', [L, L]); inp(f'k_dftst{L}', [L, L])

    O = {}
    O['yp'] = kb.outp('yp', [NP * 256, D]); O['ys'] = kb.outp('ys', [LS, D])
    O['nssd'] = kb.outp('nssd', [NP, NLA, 2, 2048, 128]); O['nret'] = kb.outp('nret', [NP, NLA, 2, 2048, 256])

    X = kb.scr('X', [T, D]); MODD = kb.scr('MODD', [8, 6 * D])
    PROJA = kb.scr('PROJA', [T, 5184]); PROJB = kb.scr('PROJB', [T, 8192]); XBC = kb.scr('XBC', [T, 3072]); QK = kb.scr('QK', [T, 2 * D], BF16)
    YF = kb.scr('YF', [T, 2 * D]); YG = kb.scr('YG', [T, 2 * D], BF16); MIXO = kb.scr('MIXO', [T, D])
    FA = kb.scr('FA', [T, DFF]); FU = kb.scr('FU', [T, DFF]); FH = kb.scr('FH', [T, DFF], BF16)
    XG = kb.scr('XG', [T, 3 * D]); Z1 = kb.scr('Z1', [T, D]); Z2 = kb.scr('Z2', [T, D])
    FILT = {L: kb.scr(f'FILT{L}', [L, 4 * D]) for L in LENS}
    RN = {L: kb.scr(f'RN{L}', [4 * D]) for L in LENS}
    FSP = {L: kb.scr(f'FSP{L}', [2, 2, L, D]) for L in LENS}
    ZERO = kb.scr('ZERO', [1, 6144])
    FWT = 256
    TAB = {L: {nm: kb.scr(f'TAB{nm}{L}', [L // FWT, 128, (L // 128) * FWT], BF16) for nm in ('c', 's', 'st')} for L in LENS}

    IDf = kb.sb('IDf', [128, 128]); IDb = kb.sb('IDb', [128, 128], BF16)
    ULE = kb.sb('ULE', [128, 128]); LGE = kb.sb('LGE', [128, 128]); ONES = kb.sb('ONES', [128, 128])
    MASK = [kb.sb('MASKF', [128, 128]), kb.sb('MASKB', [128, 128])]
    DIFF = [kb.sb('DIFFF', [128, 128]), kb.sb('DIFFB', [128, 128])]
    POSC = kb.sb('POSC', [128, 4]); ALTb = kb.sb('ALTb', [128, 128], BF16)
    AR = Arena(kb, [16384, 16384, 12288, 3072])
    PSL = [kb.ps('PSL0', [128, 512]), kb.ps('PSL1', [128, 512])]
    PST = kb.ps('PST', [128, 1024], BF16)
    PSA = [kb.ps('PSA0', [128, 512]), kb.ps('PSA1', [128, 512])]
    PSS = kb.ps('PSS', [128, 512]); PSY = kb.ps('PSY', [128, 512]); PSI = kb.ps('PSI', [128, 512])

    def ldc(dst, key, src, cast=False):
        kb.ld(dst, src, [], [key], q='pool' if cast else 'sp')
    ldc(IDf[:], 'IDf', I['k_ident']); ldc(IDb[:], 'IDb', I['k_ident'], True)
    ldc(ULE[:], 'ULE', I['k_ule']); ldc(LGE[:], 'LGE', I['k_lge'])
    ldc(MASK[0][:], 'MASKF', I['k_maskf']); ldc(MASK[1][:], 'MASKB', I['k_maskb'])
    ldc(DIFF[0][:], 'DIFFF', I['k_difff']); ldc(DIFF[1][:], 'DIFFB', I['k_diffb'])
    ldc(POSC[:], 'POSC', I['k_posc']); ldc(ALTb[:], 'ALTb', I['k_alt'], True)
    kb.memset(ONES[:], 1.0, ['ONES'])
    zt, zk = AR.tile([128, 6144])
    kb.memset(zt[0:1, :], 0.0, [zk])
    kb.ld(ZERO[:, :], zt[0:1, :], [zk], ['ZERO'])
    for c in range(NCH):
        r0 = c * 128
        if r0 < NP * 256:
            src = I['xp'][r0:r0 + 128, :]
        else:
            src = I['xs'][r0 - NP * 256:r0 - NP * 256 + 128, :]
        kb.ld(X[r0:r0 + 128, :], src, [], [('X', c)])
    P.barrier(); AR.reset()
    tbufs = [AR.tile([128, 2048], BF16) for _ in range(4)]
    ti = 0
    for L in LENS:
        for nm in ('c', 's', 'st'):
            src = I[f'k_dft{nm}{L}']
            dstv = TAB[L][nm].rearrange("fb p (tc fw) -> p fb tc fw", fw=FWT)
            for tc in range(L // 128):
                for h0 in range(0, L, 2048):
                    w_ = min(2048, L - h0)
                    tb_, tbk_ = tbufs[ti % 4]; ti += 1
                    kb.ld(tb_[:, :w_], src[tc * 128:(tc + 1) * 128, h0:h0 + w_], [], [tbk_], q='pool')
                    kb.ld(dstv[:, h0 // FWT:(h0 + w_) // FWT, tc, :], tb_[:, :w_].rearrange("p (fb fw) -> p fb fw", fw=FWT), [tbk_], ['TAB'])
    P.barrier(); AR.reset()

    rr = {'i': 0}
    def alt_eng():
        rr['i'] += 1
        return 'act' if rr['i'] % 2 else 'dve'

    def bcast_row(dst, key, vec_ap, r=()):
        kb.ld(dst, vec_ap.partition_broadcast(128), list(r), [key])

    def stage_mod():
        cT, ck = AR.tile([128, 2, 16]); sT, sk = AR.tile([128, 2, 16])
        lhs, lk = AR.tile([128, 2, 16, 128], BF16)
        for cd in range(2):
            kb.ld(cT[:, cd, :], I['cvec'][cd, :].rearrange("(kc p) -> p kc", p=128), [], [ck], allow_slow_non_contiguous=True)
        kb.act(sT, cT, AF.Silu, [ck], [sk])
        for cd in range(2):
            kb.cp(lhs[:, cd, :, :], sT[:, cd, :].unsqueeze(2).to_broadcast([128, 16, 128]), [sk], [lk])
        wbs = [AR.tile([128, 16, 512], BF16) for _ in range(2)]
        abs_ = [AR.tile([128, 512]) for _ in range(2)]
        mos = [AR.tile([128, 512]) for _ in range(2)]
        n = 0
        for l in range(DEPTH):
            wv = I['ada_w'][l].rearrange("(kc p) f -> p kc f", p=128)
            for blk in range(24):
                wb, wk = wbs[n % 2]; ab, ak = abs_[n % 2]
                kb.ld(wb, wv[:, :, blk * 512:(blk + 1) * 512], [], [wk], q='pool')
                bcast_row(ab, ak, I['ada_b'][l, blk * 512:(blk + 1) * 512])
                for cd in range(2):
                    ps = PSL[cd]
                    kb.mm(ps[:, :], [(lhs[:, cd, kc, :], wb[:, kc, :]) for kc in range(16)], [lk, wk], [f'PSL{cd}'])
                    mo, mk = mos[cd]
                    kb.tt(mo, ps[:, :], ab, ALU.add, [f'PSL{cd}', ak], [mk])
                    kb.ld(MODD[l * 2 + cd:l * 2 + cd + 1, blk * 512:(blk + 1) * 512], mo[0:1, :], [mk], ['MODD'])
                n += 1
        P.barrier(); AR.reset()

    def linear(prep_setup, Fin, outs, rkeys_of):
        kch = Fin // 128
        GC = {16: 12, 32: 6, 44: 4}[kch]
        BLK = 512 if kch <= 32 else 256
        HT, hk = AR.tile([128, kch, GC * 128], BF16)
        wbs = [AR.tile([128, kch, BLK], BF16) for _ in range(2)]
        ots = [AR.tile([128, 512]) for _ in range(2)]
        prep = prep_setup()
        chunks = list(range(NCH))
        wi = 0; oi = 0
        for g0 in range(0, NCH, GC):
            grp = chunks[g0:g0 + GC]
            for gi, c in enumerate(grp):
                src, skey = prep(c)
                for k0 in range(0, kch, 8):
                    nb = min(8, kch - k0)
                    kb.tr([(PST[:, j * 128:(j + 1) * 128], src[:, (k0 + j) * 128:(k0 + j + 1) * 128]) for j in range(nb)],
                          IDb[:], [skey, 'IDb'], ['PST'])
                    kb.cp(HT[:, k0:k0 + nb, gi * 128:(gi + 1) * 128],
                          PST[:, :nb * 128].rearrange("p (a b) -> p a b", b=128), ['PST'], [(hk, gi)], eng=alt_eng())
            for (W, c_lo, c_hi, dst, dname, dcol) in outs:
                wv = W.rearrange("(kc p) f -> p kc f", p=128)
                for b0 in range(c_lo, c_hi, BLK):
                    ncol = min(BLK, c_hi - b0)
                    wb, wk = wbs[wi % 2]; wi += 1
                    kb.ld(wb[:, :, :ncol], wv[:, :, b0:b0 + ncol], [], [wk], q='pool')
                    for gi, c in enumerate(grp):
                        ps = PSL[oi % 2]; pk = f'PSL{oi % 2}'
                        ot, ok = ots[oi % 2]; oi += 1
                        kb.mm(ps[:, :ncol], [(HT[:, kc, gi * 128:(gi + 1) * 128], wb[:, kc, :ncol]) for kc in range(kch)],
                              [(hk, gi), wk], [pk])
                        kb.cp(ot[:, :ncol], ps[:, :ncol], [pk], [ok], eng=alt_eng())
                        kb.ld(dst[c * 128:(c + 1) * 128, dcol + b0 - c_lo:dcol + b0 - c_lo + ncol], ot[:, :ncol], [ok], [(dname, c)])
        P.barrier(); AR.reset()

    def make_prep_norm(l, which, normw_ap, resid_from=None):
        def setup():
            arow = []; srow = []; grow = []
            nw, nk = AR.tile([128, D])
            bcast_row(nw, nk, normw_ap)
            for cd in range(2):
                a, ak = AR.tile([128, D]); s, sk = AR.tile([128, D])
                bcast_row(s, sk, MODD[l * 2 + cd, (3 * which) * D:(3 * which + 1) * D], ['MODD'])
                bcast_row(a, ak, MODD[l * 2 + cd, (3 * which + 1) * D:(3 * which + 2) * D], ['MODD'])
                kb.stt(a, a, 1.0, nw, ALU.add, ALU.mult, [ak, nk], [ak])
                arow.append((a, ak)); srow.append((s, sk))
                if resid_from is not None:
                    gl, gidx = resid_from
                    g, gk = AR.tile([128, D])
                    bcast_row(g, gk, MODD[gl * 2 + cd, gidx * D:(gidx + 1) * D], ['MODD'])
                    grow.append((g, gk))
            xts = [AR.tile([128, D]) for _ in range(2)]
            mts = [AR.tile([128, D]) for _ in range(2)]
            hbs = [AR.tile([128, D], BF16) for _ in range(2)]
            sts = [AR.tile([128, 4]) for _ in range(2)]
            st = {'i': 0}

            def prep(c):
                i = st['i'] % 2; st['i'] += 1
                cd = kb.cond_of[c]
                xt, xk = xts[i]; mt, mk = mts[i]; hb, hk2 = hbs[i]; s4, s4k = sts[i]
                kb.ld(xt, X[c * 128:(c + 1) * 128, :], [('X', c)], [xk])
                if resid_from is not None:
                    kb.ld(mt, MIXO[c * 128:(c + 1) * 128, :], [('MIXO', c)], [mk])
                    g, gk = grow[cd]
                    kb.tt(mt, mt, g, ALU.mult, [mk, gk], [mk])
                    kb.tt(xt, xt, mt, ALU.add, [xk, mk], [xk])
                    kb.ld(X[c * 128:(c + 1) * 128, :], xt, [xk], [('X', c)])
                kb.tt(mt, xt, xt, ALU.mult, [xk], [mk])
                P.op('dve', lambda e, s4=s4, mt=mt: e.reduce_sum(out=s4[:, 0:1], in_=mt, axis=AX.X), [mk], [s4k])
                kb.ts(s4[:, 1:2], s4[:, 0:1], 1.0 / D, ALU.mult, [s4k], [s4k], s2=EPS, op1=ALU.add)
                kb.act(s4[:, 2:3], s4[:, 1:2], AF.Sqrt, [s4k], [s4k])
                P.op('dve', lambda e, s4=s4: e.reciprocal(out=s4[:, 3:4], in_=s4[:, 2:3]), [s4k], [s4k])
                a, ak = arow[cd]; s, sk = srow[cd]
                kb.stt(mt, xt, s4[:, 3:4], a, ALU.mult, ALU.mult, [xk, s4k, ak], [mk])
                kb.tt(hb, mt, s, ALU.add, [mk, sk], [hk2])
                return hb, hk2
            return prep
        return setup

    def make_prep_load(src, sname, Fin, is_bf16):
        def setup():
            if is_bf16:
                hbs = [AR.tile([128, Fin], BF16) for _ in range(2)]
            else:
                fts = [AR.tile([128, Fin]) for _ in range(2)]
                hbs = [AR.tile([128, Fin], BF16) for _ in range(2)]
            st = {'i': 0}

            def prep(c):
                i = st['i'] % 2; st['i'] += 1
                hb, hk2 = hbs[i]
                if is_bf16:
                    kb.ld(hb, src[c * 128:(c + 1) * 128, :], [(sname, c)], [hk2])
                else:
                    ft, fk = fts[i]
                    kb.ld(ft, src[c * 128:(c + 1) * 128, :], [(sname, c)], [fk])
                    kb.cp(hb, ft, [fk], [hk2], eng=alt_eng())
                return hb, hk2
            return prep
        return setup

    def conv3(src, sname, scol, F, w_ap, b_ap, post_setup, CB=1024):
        for col0 in range(0, F, CB):
            ncol = min(CB, F - col0)
            wts = []
            for k in range(3):
                wt, wk = AR.tile([128, CB]); bcast_row(wt[:, :ncol], wk, w_ap[k, col0:col0 + ncol]); wts.append((wt, wk))
            bt, bk = AR.tile([128, CB]); bcast_row(bt[:, :ncol], bk, b_ap[col0:col0 + ncol])
            bufs = [[AR.tile([128, CB]) for _ in range(3)] for _ in range(2)]
            accs = [AR.tile([128, CB]) for _ in range(2)]
            post = post_setup()
            n = 0
            for (c0, nchk, cd) in kb.seqs:
                for c in range(c0, c0 + nchk):
                    (pv, pk), (cu, ck), (nx, nk) = bufs[n % 2]; acc, akey = accs[n % 2]; n += 1
                    r0 = c * 128
                    cs = slice(scol + col0, scol + col0 + ncol)
                    kb.ld(cu[:, :ncol], src[r0:r0 + 128, cs], [(sname, c)], [ck])
                    if c == c0:
                        kb.ld(pv[0:1, :ncol], ZERO[:, :ncol], ['ZERO'], [pk])
                        kb.ld(pv[1:128, :ncol], src[r0:r0 + 127, cs], [(sname, c)], [pk])
                    else:
                        kb.ld(pv[:, :ncol], src[r0 - 1:r0 + 127, cs], [(sname, c), (sname, c - 1)], [pk])
                    if c == c0 + nchk - 1:
                        kb.ld(nx[127:128, :ncol], ZERO[:, :ncol], ['ZERO'], [nk])
                        kb.ld(nx[0:127, :ncol], src[r0 + 1:r0 + 128, cs], [(sname, c)], [nk])
                    else:
                        kb.ld(nx[:, :ncol], src[r0 + 1:r0 + 129, cs], [(sname, c), (sname, c + 1)], [nk])
                    kb.tt(acc[:, :ncol], cu[:, :ncol], wts[1][0][:, :ncol], ALU.mult, [ck, wts[1][1]], [akey])
                    kb.tt(pv[:, :ncol], pv[:, :ncol], wts[0][0][:, :ncol], ALU.mult, [pk, wts[0][1]], [pk])
                    kb.tt(nx[:, :ncol], nx[:, :ncol], wts[2][0][:, :ncol], ALU.mult, [nk, wts[2][1]], [nk])
                    kb.tt(acc[:, :ncol], acc[:, :ncol], pv[:, :ncol], ALU.add, [akey, pk], [akey])
                    kb.tt(acc[:, :ncol], acc[:, :ncol], nx[:, :ncol], ALU.add, [akey, nk], [akey])
                    kb.tt(acc[:, :ncol], acc[:, :ncol], bt[:, :ncol], ALU.add, [akey, bk], [akey])
                    post(c, col0, ncol, acc, akey)
            P.barrier(); AR.reset()

    def stage_ffn(l, resid_from):
        if kb.dbg_level < 2:
            return
        linear(make_prep_norm(l, 1, I['norm2_w'][l, :], resid_from), D,
               [(I['ffn_w_gate'][l], 0, DFF, FA, 'FA', 0), (I['ffn_w_up'][l], 0, DFF, FU, 'FU', 0)], None)
        def post_setup():
            us = [AR.tile([128, 1024]) for _ in range(2)]
            tsx = [AR.tile([128, 1024]) for _ in range(2)]
            hs = [AR.tile([128, 1024], BF16) for _ in range(2)]
            st = {'i': 0}

            def post(c, col0, ncol, acc, akey):
                i = st['i'] % 2; st['i'] += 1
                u, uk = us[i]; t, tk = tsx[i]; h, hk = hs[i]
                kb.ld(u[:, :ncol], FU[c * 128:(c + 1) * 128, col0:col0 + ncol], [('FU', c)], [uk])
                a = acc[:, :ncol]; tt_ = t[:, :ncol]
                kb.tt(tt_, a, a, ALU.mult, [akey], [tk])
                kb.ts(tt_, tt_, 0.044715, ALU.mult, [tk], [tk], s2=1.0, op1=ALU.add)
                kb.tt(tt_, tt_, a, ALU.mult, [tk, akey], [tk])
                kb.act(tt_, tt_, AF.Sigmoid, [tk], [tk], scale=1.5957691216057308)
                kb.tt(tt_, tt_, a, ALU.mult, [tk, akey], [tk])
                kb.tt(h[:, :ncol], tt_, u[:, :ncol], ALU.mult, [tk, uk], [hk])
                kb.ld(FH[c * 128:(c + 1) * 128, col0:col0 + ncol], h[:, :ncol], [hk], [('FH', c)])
            return post
        if kb.dbg_level < 3:
            return
        conv3(FA, 'FA', 0, DFF, I['ffn_conv_w'][l], I['ffn_conv_b'][l], post_setup)
        if kb.dbg_level < 4:
            return
        linear(make_prep_load(FH, 'FH', DFF, True), DFF, [(I['ffn_w_down'][l], 0, D, MIXO, 'MIXO', 0)], None)


    def stage_ab(l, resid_from):
        li = l // 2
        linear(make_prep_norm(l, 0, I['norm1_w'][l, :], resid_from), D, [(I['ab_w_in'][li], 0, 5184, PROJA, 'PROJA', 0), (I['ab_w_in'][li], 5184, AB_IN, PROJB, 'PROJB', 0)], None)

        def post_setup():
            os_ = [AR.tile([128, 1024]) for _ in range(2)]
            st = {'i': 0}

            def post(c, col0, ncol, acc, akey):
                o, ok = os_[st['i'] % 2]; st['i'] += 1
                kb.act(o[:, :ncol], acc[:, :ncol], AF.Silu, [akey], [ok])
                kb.ld(XBC[c * 128:(c + 1) * 128, col0:col0 + ncol], o[:, :ncol], [ok], [('XBC', c)])
            return post
        conv3(PROJA, 'PROJA', 2048, 3072, I['ab_conv_w'][li], I['ab_conv_b'][li], post_setup)

        qk, qkk = AR.tile([128, 4096]); ob, obk = AR.tile([128, 4096], BF16)
        ta, tak = AR.tile([128, 16, 64]); tb, tbk = AR.tile([128, 16, 64])
        rt = [AR.tile([128, 64]) for _ in range(4)]
        for (c0, nchk, cd) in kb.seqs:
            for c in range(c0, c0 + nchk):
                kb.ld(qk, PROJB[c * 128:(c + 1) * 128, 0:4096], [('PROJB', c)], [qkk])
                if cd == 0:
                    kb.cp(ob, qk, [qkk], [obk])
                else:
                    p0 = (c - c0) * 128
                    for j in range(4):
                        kb.ld(rt[j][0], I['k_rope'][j, p0:p0 + 128, :], [], [rt[j][1]])
                    qv = qk.rearrange("p (h q f) -> p h q f", q=4, f=64)
                    ov = ob.rearrange("p (h q f) -> p h q f", q=4, f=64)
                    for half in range(2):
                        cs_, csk = rt[2 * half]; sn, snk = rt[2 * half + 1]
                        cb = cs_.unsqueeze(1).to_broadcast([128, 16, 64]); sbb = sn.unsqueeze(1).to_broadcast([128, 16, 64])
                        t1 = qv[:, :, 2 * half, :]; t2 = qv[:, :, 2 * half + 1, :]
                        kb.tt(ta, t1, cb, ALU.mult, [qkk, csk], [tak])
                        kb.tt(tb, t2, sbb, ALU.mult, [qkk, snk], [tbk])
                        kb.tt(ov[:, :, 2 * half, :], ta, tb, ALU.subtract, [tak, tbk], [obk])
                        kb.tt(ta, t1, sbb, ALU.mult, [qkk, snk], [tak])
                        kb.tt(tb, t2, cb, ALU.mult, [qkk, csk], [tbk])
                        kb.tt(ov[:, :, 2 * half + 1, :], ta, tb, ALU.add, [tak, tbk], [obk])
                kb.ts(ob[:, 2048:], ob[:, 2048:], 0.0625, ALU.mult, [obk], [obk])
                kb.ld(QK[c * 128:(c + 1) * 128, :], ob, [obk], [('QK', c)])
        P.barrier(); AR.reset()

        T_ = AR.tile
        arow, ark = T_([128, 64]); dtb, dtbk = T_([128, 64]); Drow, Dk = T_([128, 32]); lgb, lgk = T_([128, 16])
        bcast_row(arow, ark, I['ssd_a_log'][li, :]); bcast_row(dtb, dtbk, I['ssd_dt_bias'][li, :])
        bcast_row(Drow, Dk, I['ssd_d'][li, :]); bcast_row(lgb, lgk, I['ret_log_gamma'][li, :])
        kb.act(arow, arow, AF.Exp, [ark], [ark])
        kb.ts(arow, arow, -1.0, ALU.mult, [ark], [ark])
        DM = [T_([128, 8, 128]) for _ in range(2)]
        GP = [T_([128, 8]) for _ in range(2)]; TE = [T_([128, 8]) for _ in range(2)]; CDr = [T_([128, 8]) for _ in range(2)]
        for d in range(2):
            for h in range(8):
                kb.act(DM[d][0][:, h, :], DIFF[d][:], AF.Exp, ['DIFFF' if d == 0 else 'DIFFB', lgk], [DM[d][1]], scale=lgb[:, d * 8 + h:d * 8 + h + 1])
            kb.act(GP[d][0], lgb[:, d * 8:d * 8 + 8], AF.Exp, [lgk, 'POSC'], [GP[d][1]], scale=POSC[:, d:d + 1])
            kb.act(TE[d][0], lgb[:, d * 8:d * 8 + 8], AF.Exp, [lgk, 'POSC'], [TE[d][1]], scale=POSC[:, 2 + d:3 + d])
            kb.act(CDr[d][0], lgb[:, d * 8:d * 8 + 8], AF.Exp, [lgk], [CDr[d][1]], scale=128.0)
        nws, nwsk = T_([128, D]); nwr, nwrk = T_([128, D])
        bcast_row(nws, nwsk, I['ssd_norm_w'][li, :]); bcast_row(nwr, nwrk, I['ret_norm_w'][li, :])
        HS, HSk = T_([128, 4, 512]); HSb, HSbk = T_([128, 4, 512], BF16)
        HR, HRk = T_([128, 8, 2, 256]); HRb, HRbk = T_([128, 8, 2, 256], BF16)
        xbc, xbck = T_([128, 3072]); bcb, bcbk = T_([128, 1024], BF16)
        vd, vdk = T_([128, 32, 64], BF16); vte, vtek = T_([128, 32, 64], BF16)
        sm, smk = T_([128, 12, 32])
        BCT, BCTk = T_([128, 8, 128], BF16)
        seg, segk = T_([128, 4, 128]); MT, MTk = T_([128, 4, 128], BF16)
        qkb, qkbk = T_([128, 4096], BF16); QKT, QKTk = T_([128, 32, 128], BF16)
        kte, ktek = T_([128, 2048], BF16)
        vt, vtk = T_([128, D]); vb, vbk = T_([128, D], BF16)
        ych, ychk = T_([128, 4096]); yfl, yflk = T_([128, 4096])
        tmp, tmpk = T_([128, 512])
        yg, ygk = T_([128, 4096], BF16)
        st8, st8k = T_([128, 4, 8])
        blk, blkk = T_([128, 256])
        DTR, X0, AB_, EX, L1, DT, LA, ACM, TOT, EAC, TEv, CDv = range(12)

        def chunk(c, d, sample):
            r0 = c * 128
            tri = ULE if d == 0 else LGE
            trik = 'ULE' if d == 0 else 'LGE'
            kb.ld(xbc, XBC[r0:r0 + 128, :], [('XBC', c)], [xbck])
            kb.ld(sm[:, DTR, :], PROJA[r0:r0 + 128, 5120 + d * 32:5152 + d * 32], [('PROJA', c)], [smk])
            kb.ld(qkb, QK[r0:r0 + 128, :], [('QK', c)], [qkbk])
            kb.ld(vt, PROJB[r0:r0 + 128, 4096:6144], [('PROJB', c)], [vtk])
            kb.cp(bcb, xbc[:, 2048:3072], [xbck], [bcbk])
            kb.cp(vb, vt, [vtk], [vbk], eng='act')
            kb.tt(sm[:, X0, :], sm[:, DTR, :], dtb[:, d * 32:d * 32 + 32], ALU.add, [smk, dtbk], [smk])
            kb.ts(sm[:, AB_, :], sm[:, X0, :], -1.0, ALU.mult, [smk], [smk])
            kb.tt(sm[:, AB_, :], sm[:, AB_, :], sm[:, X0, :], ALU.max, [smk], [smk])
            kb.act(sm[:, EX, :], sm[:, AB_, :], AF.Exp, [smk], [smk], scale=-1.0)
            kb.ts(sm[:, EX, :], sm[:, EX, :], 1.0, ALU.add, [smk], [smk])
            kb.act(sm[:, L1, :], sm[:, EX, :], AF.Ln, [smk], [smk])
            kb.ts(sm[:, X0, :], sm[:, X0, :], 0.0, ALU.max, [smk], [smk])
            kb.tt(sm[:, DT, :], sm[:, X0, :], sm[:, L1, :], ALU.add, [smk], [smk])
            kb.tt(sm[:, LA, :], sm[:, DT, :], arow[:, d * 32:d * 32 + 32], ALU.mult, [smk, ark], [smk])
            kb.mmx([(PSI[:, 0:32], tri[:], sm[:, LA, :], True, True), (PSI[:, 32:64], ONES[:], sm[:, LA, :], True, True)],
                   [trik, 'ONES', smk], ['PSI'])
            kb.cp(sm[:, ACM:ACM + 2, :], PSI[:, 0:64].rearrange("p (a b) -> p a b", b=32), ['PSI'], [smk])
            kb.act(sm[:, EAC, :], sm[:, ACM, :], AF.Exp, [smk], [smk])
            kb.tt(sm[:, TEv, :], sm[:, TOT, :], sm[:, ACM, :], ALU.subtract, [smk], [smk])
            kb.act(sm[:, TEv, :], sm[:, TEv, :], AF.Exp, [smk], [smk])
            kb.act(sm[:, CDv, :], sm[:, TOT, :], AF.Exp, [smk], [smk])
            xsv = xbc[:, 0:2048].rearrange("p (h f) -> p h f", f=64)
            kb.tt(vd, xsv, sm[:, DT, :].unsqueeze(2).to_broadcast([128, 32, 64]), ALU.mult, [xbck, smk], [vdk])
            kb.tt(sm[:, X0, :], sm[:, DT, :], sm[:, TEv, :], ALU.mult, [smk], [smk])
            kb.tt(vte, xsv, sm[:, X0, :].unsqueeze(2).to_broadcast([128, 32, 64]), ALU.mult, [xbck, smk], [vtek])
            kb.tr([(PST[:, j * 128:(j + 1) * 128], bcb[:, j * 128:(j + 1) * 128]) for j in range(8)], IDb[:], [bcbk, 'IDb'], ['PST'])
            kb.cp(BCT, PST[:, :].rearrange("p (a b) -> p a b", b=128), ['PST'], [BCTk])
            kb.mmx([(PSS[:, g * 128:(g + 1) * 128], BCT[:, g, :], BCT[:, 4 + g, :], True, True) for g in range(4)], [BCTk], ['PSS'])
            kb.cp(HSb, HS, [HSk], [HSbk])
            for g in range(4):
                for hh in (0, 4):
                    pa = PSA[(hh // 4) % 2]; pak = f'PSA{(hh // 4) % 2}'
                    kb.mmx([(pa[:, j * 128:(j + 1) * 128], sm[:, LA, g * 8 + hh + j:g * 8 + hh + j + 1].to_broadcast([128, 128]), tri[:], True, True)
                            for j in range(4)], [smk, trik], [pak])
                    for j in range(4):
                        h = g * 8 + hh + j
                        kb.stt(seg[:, j, :], pa[:, j * 128:(j + 1) * 128], sm[:, ACM, h:h + 1], MASK[d][:], ALU.subtract, ALU.add,
                               [pak, smk, 'MASKF' if d == 0 else 'MASKB'], [segk])
                    kb.act(seg, seg, AF.Exp, [segk], [segk])
                    kb.tt(MT, seg, PSS[:, g * 128:(g + 1) * 128].unsqueeze(1).to_broadcast([128, 4, 128]), ALU.mult, [segk, 'PSS'], [MTk])
                    kb.mmx([(PSY[:, (hh + j) * 64:(hh + j + 1) * 64], MT[:, j, :], vd[:, g * 8 + hh + j, :], True, True) for j in range(4)],
                           [MTk, vdk], ['PSY'])
                kb.mmx([(PSI[:, :], BCT[:, 4 + g, :], HSb[:, g, :], True, True)], [BCTk, HSbk], ['PSI'])
                kb.tt(tmp.rearrange("p (h f) -> p h f", f=64), PSI[:, :].rearrange("p (h f) -> p h f", f=64),
                      sm[:, EAC, g * 8:g * 8 + 8].unsqueeze(2).to_broadcast([128, 8, 64]), ALU.mult, ['PSI', smk], [tmpk])
                kb.tt(ych[:, g * 512:(g + 1) * 512], PSY[:, :], tmp, ALU.add, ['PSY', tmpk], [ychk])
                kb.mmx([(PSI[:, :], bcb[:, g * 128:(g + 1) * 128], vte[:, g * 8:g * 8 + 8, :].rearrange("p h f -> p (h f)"), True, True)],
                       [bcbk, vtek], ['PSI'])
                hsg = HS[:, g, :].rearrange("p (h f) -> p h f", f=64)
                kb.tt(hsg, hsg, sm[:, CDv, g * 8:g * 8 + 8].unsqueeze(2).to_broadcast([128, 8, 64]), ALU.mult, [HSk, smk], [HSk])
                kb.tt(HS[:, g, :], HS[:, g, :], PSI[:, :], ALU.add, [HSk, 'PSI'], [HSk])
            for rnd in range(4):
                kb.tr([(PST[:, j * 128:(j + 1) * 128], qkb[:, (rnd * 8 + j) * 128:(rnd * 8 + j + 1) * 128]) for j in range(8)],
                      IDb[:], [qkbk, 'IDb'], ['PST'])
                kb.cp(QKT[:, rnd * 8:rnd * 8 + 8, :], PST[:, :].rearrange("p (a b) -> p a b", b=128), ['PST'], [QKTk], eng=alt_eng())
            kb.cp(HRb, HR, [HRk], [HRbk])
            for hq in (0, 4):
                items = []
                for j in range(4):
                    h = hq + j
                    for kc in range(2):
                        items.append((PSS[:, j * 128:(j + 1) * 128], QKT[:, 16 + 2 * h + kc, :], QKT[:, 2 * h + kc, :], kc == 0, kc == 1))
                kb.mmx(items, [QKTk], ['PSS'])
                kb.tt(MT, PSS[:, :].rearrange("p (a b) -> p a b", b=128), DM[d][0][:, hq:hq + 4, :], ALU.mult, ['PSS', DM[d][1]], [MTk])
                for jp in (0, 2):
                    h0 = hq + jp
                    kb.mmx([(PSY[:, jj * 256:(jj + 1) * 256], MT[:, jp + jj, :], vb[:, (h0 + jj) * 256:(h0 + jj + 1) * 256], True, True)
                            for jj in range(2)], [MTk, vbk], ['PSY'])
                    items = []
                    for jj in range(2):
                        for kc in range(2):
                            items.append((PSI[:, jj * 256:(jj + 1) * 256], QKT[:, 2 * (h0 + jj) + kc, :], HRb[:, h0 + jj, kc, :], kc == 0, kc == 1))
                    kb.mmx(items, [QKTk, HRbk], ['PSI'])
                    for jj in range(2):
                        kb.tt(tmp[:, jj * 256:(jj + 1) * 256], PSI[:, jj * 256:(jj + 1) * 256],
                              GP[d][0][:, h0 + jj:h0 + jj + 1].to_broadcast([128, 256]), ALU.mult, ['PSI', GP[d][1]], [tmpk])
                    kb.tt(ych[:, 2048 + h0 * 256:2048 + (h0 + 2) * 256], PSY[:, :], tmp, ALU.add, ['PSY', tmpk], [ychk])
            for h in range(8):
                kb.tt(kte[:, h * 256:(h + 1) * 256], qkb[:, 2048 + h * 256:2048 + (h + 1) * 256],
                      TE[d][0][:, h:h + 1].to_broadcast([128, 256]), ALU.mult, [qkbk, TE[d][1]], [ktek])
                for kc in range(2):
                    kb.mmx([(PSI[:, 0:256], kte[:, h * 256 + kc * 128:h * 256 + (kc + 1) * 128], vb[:, h * 256:(h + 1) * 256], True, True)],
                           [ktek, vbk], ['PSI'])
                    kb.stt(HR[:, h, kc, :], HR[:, h, kc, :], CDr[d][0][:, h:h + 1], PSI[:, 0:256], ALU.mult, ALU.add, [HRk, CDr[d][1], 'PSI'], [HRk])
            if d == 0:
                kb.ld(YF[r0:r0 + 128, :], ych, [ychk], [('YF', c)])
                return
            kb.ld(yfl, YF[r0:r0 + 128, :], [('YF', c)], [yflk])
            kb.tt(ych, ych, yfl, ALU.add, [ychk, yflk], [ychk])
            zt_ = yfl[:, 0:2048]; gt_ = yfl[:, 2048:4096]
            kb.ld(zt_, PROJA[r0:r0 + 128, 0:2048], [('PROJA', c)], [yflk])
            kb.ld(gt_, PROJB[r0:r0 + 128, 6144:8192], [('PROJB', c)], [yflk])
            ys_ = ych[:, 0:2048]
            kb.tt(vt.rearrange("p (h f) -> p h f", f=64), xsv, Drow.unsqueeze(2).to_broadcast([128, 32, 64]), ALU.mult, [xbck, Dk], [vtk])
            kb.tt(ys_, ys_, vt, ALU.add, [ychk, vtk], [ychk])
            kb.act(zt_, zt_, AF.Silu, [yflk], [yflk])
            kb.tt(ys_, ys_, zt_, ALU.mult, [ychk, yflk], [ychk])
            kb.tt(vt, ys_, ys_, ALU.mult, [ychk], [vtk])
            P.op('dve', lambda e: e.reduce_sum(out=st8[:, 0, 0:4], in_=vt.rearrange("p (g f) -> p g f", f=512), axis=AX.X), [vtk], [st8k])
            kb.ts(st8[:, 0, 0:4], st8[:, 0, 0:4], 1.0 / 512, ALU.mult, [st8k], [st8k], s2=EPS, op1=ALU.add)
            kb.act(st8[:, 0, 0:4], st8[:, 0, 0:4], AF.Sqrt, [st8k], [st8k])
            P.op('dve', lambda e: e.reciprocal(out=st8[:, 1, 0:4], in_=st8[:, 0, 0:4]), [st8k], [st8k])
            for g in range(4):
                kb.stt(yg[:, g * 512:(g + 1) * 512], ys_[:, g * 512:(g + 1) * 512], st8[:, 1, g:g + 1], nws[:, g * 512:(g + 1) * 512],
                       ALU.mult, ALU.mult, [ychk, st8k, nwsk], [ygk])
            yr3 = ych[:, 2048:4096].rearrange("p (h f) -> p h f", f=256)
            vt3 = vt.rearrange("p (h f) -> p h f", f=256)
            P.op('dve', lambda e: e.reduce_sum(out=st8[:, 2, :], in_=yr3, axis=AX.X), [ychk], [st8k])
            kb.ts(st8[:, 2, :], st8[:, 2, :], 1.0 / 256, ALU.mult, [st8k], [st8k])
            kb.tt(yr3, yr3, st8[:, 2, :].unsqueeze(2).to_broadcast([128, 8, 256]), ALU.subtract, [ychk, st8k], [ychk])
            kb.tt(vt3, yr3, yr3, ALU.mult, [ychk], [vtk])
            P.op('dve', lambda e: e.reduce_sum(out=st8[:, 3, :], in_=vt3, axis=AX.X), [vtk], [st8k])
            kb.ts(st8[:, 3, :], st8[:, 3, :], 1.0 / 256, ALU.mult, [st8k], [st8k], s2=EPS, op1=ALU.add)
            kb.act(st8[:, 3, :], st8[:, 3, :], AF.Sqrt, [st8k], [st8k])
            P.op('dve', lambda e: e.reciprocal(out=st8[:, 2, :], in_=st8[:, 3, :]), [st8k], [st8k])
            kb.tt(yr3, yr3, st8[:, 2, :].unsqueeze(2).to_broadcast([128, 8, 256]), ALU.mult, [ychk, st8k], [ychk])
            kb.tt(ych[:, 2048:4096], ych[:, 2048:4096], nwr, ALU.mult, [ychk, nwrk], [ychk])
            kb.act(gt_, gt_, AF.Silu, [yflk], [yflk])
            kb.tt(yg[:, 2048:4096], ych[:, 2048:4096], gt_, ALU.mult, [ychk, yflk], [ygk])
            kb.ld(YG[r0:r0 + 128, :], yg, [ygk], [('YG', c)])

        for si, (c0, nchk, cd) in enumerate(kb.seqs):
            sample = (cd == 1)
            for d in range(2):
                if not sample:
                    kb.memset(HS, 0.0, [HSk]); kb.memset(HR, 0.0, [HRk])
                else:
                    for rb in range(16):
                        kb.ld(blk[:, 0:128], I['sssd'][li, d, rb * 128:(rb + 1) * 128, :], [], [blkk])
                        kb.tr([(PSI[:, 0:128], blk[:, 0:128])], IDf[:], [blkk, 'IDf'], ['PSI'])
                        kb.cp(HS[:, rb // 4, (rb % 4) * 128:(rb % 4 + 1) * 128], PSI[:, 0:128], ['PSI'], [HSk])
                    for h in range(8):
                        for dvb in range(2):
                            kb.ld(blk, I['sret'][li, d, h * 256 + dvb * 128:h * 256 + (dvb + 1) * 128, :], [], [blkk])
                            for kc in range(2):
                                kb.tr([(PSI[:, 0:128], blk[:, kc * 128:(kc + 1) * 128])], IDf[:], [blkk, 'IDf'], ['PSI'])
                                kb.cp(HR[:, h, kc, dvb * 128:(dvb + 1) * 128], PSI[:, 0:128], ['PSI'], [HRk])
                order = range(c0, c0 + nchk) if d == 0 else range(c0 + nchk - 1, c0 - 1, -1)
                for c in order:
                    chunk(c, d, sample)
                if not sample:
                    for rb in range(16):
                        kb.tr([(PSI[:, 0:128], HS[:, rb // 4, (rb % 4) * 128:(rb % 4 + 1) * 128])], IDf[:], [HSk, 'IDf'], ['PSI'])
                        kb.cp(blk[:, 0:128], PSI[:, 0:128], ['PSI'], [blkk])
                        kb.ld(O['nssd'][si, li, d, rb * 128:(rb + 1) * 128, :], blk[:, 0:128], [blkk], [('NS', si, li, d, rb)])
                    for h in range(8):
                        for dvb in range(2):
                            for kc in range(2):
                                kb.tr([(PSI[:, 0:128], HR[:, h, kc, dvb * 128:(dvb + 1) * 128])], IDf[:], [HRk, 'IDf'], ['PSI'])
                                kb.cp(blk[:, kc * 128:(kc + 1) * 128], PSI[:, 0:128], ['PSI'], [blkk])
                            kb.ld(O['nret'][si, li, d, h * 256 + dvb * 128:h * 256 + (dvb + 1) * 128, :], blk, [blkk], [('NR', si, li, d, h, dvb)])
        P.barrier(); AR.reset()
        linear(make_prep_load(YG, 'YG', 2 * D, True), 2 * D, [(I['ab_w_out'][li], 0, D, MIXO, 'MIXO', 0)], None)


    def stage_hy(l, resid_from):
        li = l // 2
        T_ = AR.tile
        linear(make_prep_norm(l, 0, I['norm1_w'][l, :], resid_from), D, [(I['hy_w_in'][li], 0, 3 * D, PROJB, 'PROJB', 0)], None)

        def post_setup():
            def post(c, col0, ncol, acc, akey):
                kb.ld(XG[c * 128:(c + 1) * 128, col0:col0 + ncol], acc[:, :ncol], [akey], [('XG', c)])
            return post
        conv3(PROJB, 'PROJB', 0, 3 * D, I['hy_short_w'][li], I['hy_short_b'][li], post_setup)
        MAGIC = 12582912.0
        TWO_PI = 2.0 * math.pi
        for L in LENS:
            nch = L // 128
            N2 = 2 * L
            ft, ftk = T_([128, L]); H1, H1k = T_([128, L]); H2, H2k = T_([128, L])
            w1, w1k = T_([128, 64]); w2, w2k = T_([128, 64]); w3, w3k = T_([128, 4 * D])
            bb, bbk = T_([128, 2]); u, uk = T_([128, 512]); r_, rk = T_([128, 512])
            dl, dlk = T_([128, D]); ng, ngk = T_([128, nch])
            kb.ld(ft[0:33, :], I[f'k_featsT{L}'], [], [ftk])
            kb.ld(w1[0:33, :], I['hy_f_w1'][li], [], [w1k]); kb.ld(w2[0:64, :], I['hy_f_w2'][li], [], [w2k])
            kb.ld(w3[0:64, :], I['hy_f_w3'][li], [], [w3k])
            kb.ld(bb[0:64, 0:1], I['hy_f_b1'][li, :].rearrange("(p o) -> p o", o=1), [], [bbk])
            kb.ld(bb[0:64, 1:2], I['hy_f_b2'][li, :].rearrange("(p o) -> p o", o=1), [], [bbk])
            bcast_row(dl, dlk, I['k_delta']); kb.ld(ng, I[f'k_negt{L}'], [], [ngk])
            for layer in range(2):
                src, srck, kk = (ft, ftk, 33) if layer == 0 else (H1, H1k, 64)
                wt, wtk = (w1, w1k) if layer == 0 else (w2, w2k)
                dstT, dstk = (H1, H1k) if layer == 0 else (H2, H2k)
                for b0 in range(0, L, 512):
                    bw = min(512, L - b0)
                    kb.mmx([(PSL[0][0:64, :bw], wt[0:kk, :], src[0:kk, b0:b0 + bw], True, True)], [wtk, srck], ['PSL0'])
                    kb.act(u[0:64, :bw], PSL[0][0:64, :bw], AF.Identity, ['PSL0', bbk], [uk], bias=bb[0:64, layer:layer + 1])
                    kb.ts(r_[0:64, :bw], u[0:64, :bw], 1.0 / TWO_PI, ALU.mult, [uk], [rk], s2=MAGIC, op1=ALU.add)
                    kb.ts(r_[0:64, :bw], r_[0:64, :bw], -MAGIC, ALU.add, [rk], [rk])
                    kb.stt(u[0:64, :bw], r_[0:64, :bw], -TWO_PI, u[0:64, :bw], ALU.mult, ALU.add, [rk, uk], [uk])
                    kb.ts(u[0:64, :bw], u[0:64, :bw], -3.14159, ALU.max, [uk], [uk], s2=3.14159, op1=ALU.min)
                    kb.act(dstT[0:64, b0:b0 + bw], u[0:64, :bw], AF.Sin, [uk], [dstk])
            win, wink = T_([128, 512]); hw, hwk = T_([128, 512]); ab, abk = T_([128, 512]); rn, rnk = T_([128, 512])
            for cbk in range(16):
                ccol = (cbk * 512) % D
                for tch in range(nch):
                    kb.mmx([(PSL[0][:, :], H2[0:64, tch * 128:(tch + 1) * 128], w3[0:64, cbk * 512:(cbk + 1) * 512], True, True)],
                           [H2k, w3k], ['PSL0'])
                    kb.act(win, dl[:, ccol:ccol + 512], AF.Exp, [dlk, ngk], [wink], scale=ng[:, tch:tch + 1])
                    kb.tt(hw, PSL[0][:, :], win, ALU.mult, ['PSL0', wink], [hwk])
                    kb.ld(FILT[L][tch * 128:(tch + 1) * 128, cbk * 512:(cbk + 1) * 512], hw, [hwk], [('FILT', tch, cbk)])
                    kb.ts(ab, hw, -1.0, ALU.mult, [hwk], [abk])
                    kb.tt(ab, ab, hw, ALU.max, [abk, hwk], [abk])
                    kb.mmx([(PSA[0][:, :], ONES[:], ab, tch == 0, tch == nch - 1)], ['ONES', abk], ['PSA0'])
                kb.ts(rn, PSA[0][:, :], EPS, ALU.add, ['PSA0'], [rnk])
                P.op('dve', lambda e, rn=rn: e.reciprocal(out=rn, in_=rn), [rnk], [rnk])
                kb.ld(RN[L][cbk * 512:(cbk + 1) * 512].rearrange("(o f) -> o f", o=1), rn[0:1, :], [rnk], [('RN', cbk)])
            P.barrier(); AR.reset()
            CW = 512
            FW = FWT
            hp, hpk = T_([128, nch, CW], BF16); hm, hmk = T_([128, nch, CW], BF16)
            CBS = [T_([128, nch, FW], BF16) for _ in range(2)]; SBS = [T_([128, nch, FW], BF16) for _ in range(2)]
            tbi = {'i': 0}
            hf, hfk = T_([128, CW]); hb_, hbk = T_([128, CW]); rf, rfk = T_([128, CW]); rb, rbk = T_([128, CW])
            fr, frk = T_([128, CW]); fi, fik = T_([128, CW])
            cv = I[f'k_dftc{L}'].rearrange("(tc p) f -> p tc f", p=128)
            sv = I[f'k_dfts{L}'].rearrange("(tc p) f -> p tc f", p=128)
            stv = I[f'k_dftst{L}'].rearrange("(tc p) f -> p tc f", p=128)
            for o in range(2):
                for cb in range(D // CW):
                    colf = (0 * 2 + o) * D + cb * CW; colb = (1 * 2 + o) * D + cb * CW
                    bcast_row(rf, rfk, RN[L][colf:colf + CW], [('RN', colf // 512)])
                    bcast_row(rb, rbk, RN[L][colb:colb + CW], [('RN', colb // 512)])
                    for tch in range(nch):
                        kb.ld(hf, FILT[L][tch * 128:(tch + 1) * 128, colf:colf + CW], [('FILT', tch, colf // 512)], [hfk])
                        kb.ld(hb_, FILT[L][tch * 128:(tch + 1) * 128, colb:colb + CW], [('FILT', tch, colb // 512)], [hbk])
                        if tch == 0:
                            kb.ld(hb_[0:1, :], ZERO[:, :CW], ['ZERO'], [hbk])
                        kb.tt(hf, hf, rf, ALU.mult, [hfk, rfk], [hfk])
                        kb.tt(hb_, hb_, rb, ALU.mult, [hbk, rbk], [hbk])
                        kb.tt(hp[:, tch, :], hf, hb_, ALU.add, [hfk, hbk], [hpk])
                        kb.tt(hm[:, tch, :], hf, hb_, ALU.subtract, [hfk, hbk], [hmk])
                    for f0 in range(0, L, FW):
                        (Cb, Cbk), (Sb, Sbk) = CBS[tbi['i'] % 2], SBS[tbi['i'] % 2]; tbi['i'] += 1
                        kb.ld(Cb.rearrange("p a b -> p (a b)"), TAB[L]['c'][f0 // FW], [], [Cbk], q='pool')
                        kb.ld(Sb.rearrange("p a b -> p (a b)"), TAB[L]['s'][f0 // FW], [], [Sbk], q='pool')
                        for fc in range(FW // 128):
                            fch = f0 // 128 + fc
                            kb.mmx([(PSL[0][:, :CW], Cb[:, t_, fc * 128:(fc + 1) * 128], hp[:, t_, :], t_ == 0, t_ == nch - 1) for t_ in range(nch)],
                                   [Cbk, hpk], ['PSL0'])
                            kb.mmx([(PSL[1][:, :CW], Sb[:, t_, fc * 128:(fc + 1) * 128], hm[:, t_, :], t_ == 0, t_ == nch - 1) for t_ in range(nch)],
                                   [Sbk, hmk], ['PSL1'])
                            kb.ts(fr, PSL[0][:, :CW], 2.0 / N2, ALU.mult, ['PSL0'], [frk])
                            kb.ts(fi, PSL[1][:, :CW], 2.0 / N2, ALU.mult, ['PSL1'], [fik])
                            if fch == 0:
                                kb.mmx([(PSA[1][:, :CW], ALTb[:], hp[:, t_, :], t_ == 0, t_ == nch - 1) for t_ in range(nch)], ['ALTb', hpk], ['PSA1'])
                                kb.ts(fi[0:1, :], PSA[1][0:1, :CW], 1.0 / N2, ALU.mult, ['PSA1', fik], [fik])
                                kb.ts(fr[0:1, :], fr[0:1, :], 0.5, ALU.mult, [frk], [frk])
                            kb.ld(FSP[L][o, 0, fch * 128:(fch + 1) * 128, cb * CW:(cb + 1) * CW], fr, [frk], [('FSP', o, 0, fch, cb)])
                            kb.ld(FSP[L][o, 1, fch * 128:(fch + 1) * 128, cb * CW:(cb + 1) * CW], fi, [fik], [('FSP', o, 1, fch, cb)])
            P.barrier(); AR.reset()
            Yb, Ybk = T_([128, nch, 2, CW], BF16); zb, zbk = T_([128, nch, CW], BF16)
            CBS = [T_([128, nch, FW], BF16) for _ in range(2)]; SBS = [T_([128, nch, FW], BF16) for _ in range(2)]
            zf, zfk = T_([128, CW]); gt, gtk = T_([128, CW]); brow, browk = T_([128, CW])
            fr, frk = T_([128, CW]); fi, fik = T_([128, CW]); t1, t1k = T_([128, CW]); t2, t2k = T_([128, CW])
            for (c0, nchk, cd) in kb.seqs:
                if nchk != nch:
                    continue
                for o in range(2):
                    zsrc, zname, zcol = (XG, 'XG', 2 * D) if o == 0 else (Z1, 'Z1', 0)
                    gcol = o * D
                    zdst, zdname = (Z1, 'Z1') if o == 0 else (Z2, 'Z2')
                    for cb in range(D // CW):
                        bcast_row(brow, browk, I['hy_bias'][li, o, cb * CW:(cb + 1) * CW])
                        for tch in range(nch):
                            c = c0 + tch
                            kb.ld(zf, zsrc[c * 128:(c + 1) * 128, zcol + cb * CW:zcol + (cb + 1) * CW], [(zname, c)], [zfk])
                            kb.cp(zb[:, tch, :], zf, [zfk], [zbk], eng=alt_eng())
                        for f0 in range(0, L, FW):
                            (Cb, Cbk), (Sb, Sbk) = CBS[tbi['i'] % 2], SBS[tbi['i'] % 2]; tbi['i'] += 1
                            kb.ld(Cb.rearrange("p a b -> p (a b)"), TAB[L]['c'][f0 // FW], [], [Cbk], q='pool')
                            kb.ld(Sb.rearrange("p a b -> p (a b)"), TAB[L]['s'][f0 // FW], [], [Sbk], q='pool')
                            for fc in range(FW // 128):
                                fch = f0 // 128 + fc
                                kb.mmx([(PSL[0][:, :CW], Cb[:, t_, fc * 128:(fc + 1) * 128], zb[:, t_, :], t_ == 0, t_ == nch - 1) for t_ in range(nch)],
                                       [Cbk, zbk], ['PSL0'])
                                kb.mmx([(PSL[1][:, :CW], Sb[:, t_, fc * 128:(fc + 1) * 128], zb[:, t_, :], t_ == 0, t_ == nch - 1) for t_ in range(nch)],
                                       [Sbk, zbk], ['PSL1'])
                                kb.ld(fr, FSP[L][o, 0, fch * 128:(fch + 1) * 128, cb * CW:(cb + 1) * CW], [('FSP', o, 0, fch, cb)], [frk])
                                kb.ld(fi, FSP[L][o, 1, fch * 128:(fch + 1) * 128, cb * CW:(cb + 1) * CW], [('FSP', o, 1, fch, cb)], [fik])
                                kb.tt(t1, PSL[0][:, :CW], fr, ALU.mult, ['PSL0', frk], [t1k])
                                kb.tt(t2, PSL[1][:, :CW], fi, ALU.mult, ['PSL1', fik], [t2k])
                                kb.tt(Yb[:, fch, 0, :], t1, t2, ALU.subtract, [t1k, t2k], [Ybk])
                                if fch == 0:
                                    kb.cp(Yb[0:1, 0, 0, :], t1[0:1, :], [t1k], [Ybk])
                                    kb.cp(Yb[0:1, 0, 1, :], t2[0:1, :], [t2k], [Ybk])
                                kb.tt(t1, PSL[0][:, :CW], fi, ALU.mult, ['PSL0', fik], [t1k])
                                kb.tt(t2, PSL[1][:, :CW], fr, ALU.mult, ['PSL1', frk], [t2k])
                                if fch == 0:
                                    kb.tt(Yb[1:128, fch, 1, :], t1[1:128, :], t2[1:128, :], ALU.add, [t1k, t2k], [Ybk]) if False else None
                                    kb.tt(t1, t1, t2, ALU.add, [t1k, t2k], [t1k])
                                    kb.cp(t1[0:1, :], Yb[0:1, 0, 1, :], [Ybk], [t1k])
                                    kb.cp(Yb[:, fch, 1, :], t1, [t1k], [Ybk])
                                else:
                                    kb.tt(Yb[:, fch, 1, :], t1, t2, ALU.add, [t1k, t2k], [Ybk])
                        for t0 in range(0, L, FW):
                            (Cb, Cbk), (Sb, Sbk) = CBS[tbi['i'] % 2], SBS[tbi['i'] % 2]; tbi['i'] += 1
                            kb.ld(Cb.rearrange("p a b -> p (a b)"), TAB[L]['c'][t0 // FW], [], [Cbk], q='pool')
                            kb.ld(Sb.rearrange("p a b -> p (a b)"), TAB[L]['st'][t0 // FW], [], [Sbk], q='pool')
                            for tc in range(FW // 128):
                                tch = t0 // 128 + tc
                                c = c0 + tch
                                items = []
                                for f_ in range(nch):
                                    items.append((PSL[0][:, :CW], Cb[:, f_, tc * 128:(tc + 1) * 128], Yb[:, f_, 0, :], f_ == 0, False))
                                    items.append((PSL[0][:, :CW], Sb[:, f_, tc * 128:(tc + 1) * 128], Yb[:, f_, 1, :], False, f_ == nch - 1))
                                kb.mmx(items, [Cbk, Sbk, Ybk], ['PSL0'])
                                kb.ld(zf, zsrc[c * 128:(c + 1) * 128, zcol + cb * CW:zcol + (cb + 1) * CW], [(zname, c)], [zfk])
                                kb.ld(gt, XG[c * 128:(c + 1) * 128, gcol + cb * CW:gcol + (cb + 1) * CW], [('XG', c)], [gtk])
                                kb.tt(zf, zf, brow, ALU.mult, [zfk, browk], [zfk])
                                kb.tt(zf, zf, PSL[0][:, :CW], ALU.add, [zfk, 'PSL0'], [zfk])
                                kb.tt(zf, zf, gt, ALU.mult, [zfk, gtk], [zfk])
                                kb.ld(zdst[c * 128:(c + 1) * 128, cb * CW:(cb + 1) * CW], zf, [zfk], [(zdname, c)])
            P.barrier(); AR.reset()
        linear(make_prep_load(Z2, 'Z2', D, False), D, [(I['hy_w_out'][li], 0, D, MIXO, 'MIXO', 0)], None)

    def stage_final(resid_from):
        nw, nk = AR.tile([128, D]); bcast_row(nw, nk, I['final_norm_w'])
        grow = []
        for cd in range(2):
            g, gk = AR.tile([128, D])
            if resid_from is not None:
                gl, gidx = resid_from
                bcast_row(g, gk, MODD[gl * 2 + cd, gidx * D:(gidx + 1) * D], ['MODD'])
            grow.append((g, gk))
        xts = [AR.tile([128, D]) for _ in range(2)]
        mts = [AR.tile([128, D]) for _ in range(2)]
        sts = [AR.tile([128, 4]) for _ in range(2)]
        for c in range(NCH):
            i = c % 2
            cd = kb.cond_of[c]
            xt, xk = xts[i]; mt, mk = mts[i]; s4, s4k = sts[i]
            kb.ld(xt, X[c * 128:(c + 1) * 128, :], [('X', c)], [xk])
            if resid_from is not None:
                kb.ld(mt, MIXO[c * 128:(c + 1) * 128, :], [('MIXO', c)], [mk])
                g, gk = grow[cd]
                kb.tt(mt, mt, g, ALU.mult, [mk, gk], [mk])
                kb.tt(xt, xt, mt, ALU.add, [xk, mk], [xk])
            kb.tt(mt, xt, xt, ALU.mult, [xk], [mk])
            P.op('dve', lambda e, s4=s4, mt=mt: e.reduce_sum(out=s4[:, 0:1], in_=mt, axis=AX.X), [mk], [s4k])
            kb.ts(s4[:, 1:2], s4[:, 0:1], 1.0 / D, ALU.mult, [s4k], [s4k], s2=EPS, op1=ALU.add)
            kb.act(s4[:, 2:3], s4[:, 1:2], AF.Sqrt, [s4k], [s4k])
            P.op('dve', lambda e, s4=s4: e.reciprocal(out=s4[:, 3:4], in_=s4[:, 2:3]), [s4k], [s4k])
            kb.stt(mt, xt, s4[:, 3:4], nw, ALU.mult, ALU.mult, [xk, s4k, nk], [mk])
            r0 = c * 128
            if r0 < NP * 256:
                dst = O['yp'][r0:r0 + 128, :]
            else:
                dst = O['ys'][r0 - NP * 256:r0 - NP * 256 + 128, :]
            kb.ld(dst, mt, [mk], [('OUT', c)])
        P.barrier(); AR.reset()

    kb._stuff = dict(stage_final=stage_final, stage_ab=stage_ab, stage_hy=stage_hy, I=I, O=O, X=X, MODD=MODD, XBC=XBC, QK=QK, YF=YF, YG=YG, MIXO=MIXO, XG=XG, Z1=Z1, Z2=Z2,
                     FILT=FILT, RN=RN, FSP=FSP, ZERO=ZERO, AR=AR, IDf=IDf, IDb=IDb, ULE=ULE, LGE=LGE, ONES=ONES, MASK=MASK,
                     DIFF=DIFF, POSC=POSC, ALTb=ALTb, PSL=PSL, PST=PST, PSA=PSA, PSS=PSS, PSY=PSY, PSI=PSI,
                     linear=linear, make_prep_norm=make_prep_norm, make_prep_load=make_prep_load, conv3=conv3,
                     bcast_row=bcast_row, alt_eng=alt_eng, stage_mod=stage_mod, stage_ffn=stage_ffn, LENS=LENS)
    return kb


def assemble(kb, mixers=True):
    S = kb._stuff
    if kb.dbg_level >= 1:
        S['stage_mod']()
    pending = None
    for l in range(kb.DEPTH):
        if mixers:
            if l % 2 == 0:
                S['stage_ab'](l, pending)
            else:
                S['stage_hy'](l, pending)
            pending = (l, 2)
        S['stage_ffn'](l, pending)
        pending = (l, 5)
    if kb.dbg_level >= 5:
        S['stage_final'](pending)
    kb.P.emit(kb.es)
    kb.es.close()
    return kb.nc


def make_consts(LS):
    K = {}
    j = np.arange(128)[:, None]; i = np.arange(128)[None, :]
    K['k_ident'] = np.eye(128, dtype=np.float32)
    K['k_ule'] = (j <= i).astype(np.float32)
    K['k_lge'] = (j >= i).astype(np.float32)
    K['k_maskf'] = np.where(i >= j, 0.0, -1e30).astype(np.float32)
    K['k_maskb'] = np.where(i <= j, 0.0, -1e30).astype(np.float32)
    K['k_difff'] = np.where(i >= j, (i - j).astype(np.float64), 1e9).astype(np.float32)
    K['k_diffb'] = np.where(j >= i, (j - i).astype(np.float64), 1e9).astype(np.float32)
    p = np.arange(128, dtype=np.float64)
    K['k_posc'] = np.stack([p + 1, 128 - p, 127 - p, p], axis=1).astype(np.float32)
    K['k_alt'] = np.broadcast_to(((-1.0) ** p)[:, None], (128, 128)).astype(np.float32).copy()
    n = 64
    inv = (np.float32(10000.0) ** (-np.arange(n, dtype=np.float32) / np.float32(n))).astype(np.float32)
    t = np.arange(LS)
    rows = (t // 64).astype(np.float32); cols = (t % 64).astype(np.float32)
    angr = (rows[:, None] * inv[None, :]).astype(np.float32); angc = (cols[:, None] * inv[None, :]).astype(np.float32)
    K['k_rope'] = np.stack([np.cos(angr), np.sin(angr), np.cos(angc), np.sin(angc)]).astype(np.float32)
    K['k_delta'] = np.abs(np.linspace(math.log(1e-2) / 1.5, math.log(1e-2) / 0.3, 2048, dtype=np.float32)).astype(np.float32)
    for L in sorted({256, LS}):
        tt = (np.arange(L, dtype=np.float32) / np.float32(L)).astype(np.float32)
        bands = np.linspace(1e-4, 15, 16, dtype=np.float32)
        ang = (np.float32(2 * math.pi) * tt[:, None] * bands[None, :]).astype(np.float32)
        feats = np.concatenate([tt[:, None], np.cos(ang), np.sin(ang)], axis=-1).astype(np.float32)
        K[f'k_featsT{L}'] = np.ascontiguousarray(feats.T)
        K[f'k_negt{L}'] = np.ascontiguousarray((-tt).reshape(L // 128, 128).T)
        N = 2 * L
        tf = (np.arange(L)[:, None] * np.arange(L)[None, :]) % N
        ang2 = 2.0 * np.pi * tf / N
        Cm = np.cos(ang2); Sm = -np.sin(ang2)
        Sm[:, 0] = (-1.0) ** np.arange(L)
        K[f'k_dftc{L}'] = Cm.astype(np.float32)
        K[f'k_dfts{L}'] = Sm.astype(np.float32)
        K[f'k_dftst{L}'] = np.ascontiguousarray(Sm.T).astype(np.float32)
    return K


def core_inputs(inputs, K, core, NP, b, DEPTH=4):
    f = lambda a: np.ascontiguousarray(np.asarray(a, dtype=np.float32))
    m = {}
    m['xp'] = f(inputs['x_prompt'][core * NP:(core + 1) * NP]).reshape(NP * 256, D)
    m['xs'] = f(inputs['x_sample'][b])
    m['cvec'] = f(np.stack([np.asarray(inputs['c_ctx']), np.asarray(inputs['c'])[b]]))
    m['sssd'] = f(inputs['state_ssd'][b]).reshape(2, 2, 2048, 128)
    m['sret'] = f(inputs['state_ret'][b]).reshape(2, 2, 2048, 256)
    for k in ['ada_w', 'ada_b', 'norm1_w', 'norm2_w', 'ab_w_in', 'ab_conv_w', 'ab_conv_b', 'ssd_norm_w', 'ret_norm_w',
              'ab_w_out', 'hy_w_in', 'hy_short_w', 'hy_short_b', 'hy_f_w1', 'hy_f_b1', 'hy_f_w2', 'hy_f_b2', 'hy_f_w3',
              'hy_bias', 'hy_w_out', 'ffn_w_gate', 'ffn_w_up', 'ffn_conv_w', 'ffn_conv_b', 'ffn_w_down', 'final_norm_w', 'ssd_d']:
        m[k] = f(inputs[k])
    m['ssd_dt_bias'] = f(inputs['ssd_dt_bias']).reshape(2, 64)
    m['ssd_a_log'] = f(inputs['ssd_a_log']).reshape(2, 64)
    m['ret_log_gamma'] = f(inputs['ret_log_gamma']).reshape(2, 16)
    NLA = (DEPTH + 1) // 2; NLH = max(DEPTH // 2, 1)
    for k in list(m.keys()):
        if k in ('ada_w', 'ada_b', 'norm1_w', 'norm2_w', 'ffn_w_gate', 'ffn_w_up', 'ffn_conv_w', 'ffn_conv_b', 'ffn_w_down'):
            m[k] = np.ascontiguousarray(m[k][:DEPTH])
        elif k.startswith('ab_') or k.startswith('ssd_') or k.startswith('ret_') or k in ('sssd', 'sret'):
            m[k] = np.ascontiguousarray(m[k][:NLA])
        elif k.startswith('hy_'):
            m[k] = np.ascontiguousarray(m[k][:NLH])
    m.update(K)
    return m


_NC_CACHE = {}


def kernel(**inputs):
    NCORES, NP, LS, DEPTH = 2, 8, 4096, 4
    key = (NP, LS, DEPTH)
    if key not in _NC_CACHE:
        _NC_CACHE[key] = assemble(build(NP, LS, DEPTH))
    nc = _NC_CACHE[key]
    K = make_consts(LS)
    in_maps = [core_inputs(inputs, K, c, NP, c % 2) for c in range(NCORES)]
    res = run_bass_kernel_spmd(nc, in_maps, core_ids=list(range(NCORES)))
    R = res.results
    y_prompt = np.concatenate([R[c]['yp'].reshape(NP, 256, D) for c in range(NCORES)], axis=0)
    y_sample = np.stack([R[0]['ys'], R[1]['ys']], axis=0)
    nssd = np.concatenate([R[c]['nssd'].reshape(NP, 2, 2, 32, 64, 128) for c in range(NCORES)], axis=0)
    nret = np.concatenate([R[c]['nret'].reshape(NP, 2, 2, 8, 256, 256) for c in range(NCORES)], axis=0)
    return (y_prompt.astype(np.float32), y_sample.astype(np.float32), nssd.astype(np.float32), nret.astype(np.float32))
```

```python
import contextlib
import math
import numpy as np
import concourse.bass as bass
import concourse.mybir as mybir
from concourse.bass_utils import run_bass_kernel_spmd

F32 = mybir.dt.float32
BF16 = mybir.dt.bfloat16
AF = mybir.ActivationFunctionType
ALU = mybir.AluOpType
AX = mybir.AxisListType

D = 2048
AB_IN = 13376
DFF = 5632
EPS = 1e-6
NDQ = 8


class Prog:
    def __init__(self, nc):
        self.nc = nc
        self.names = ['pe', 'act', 'dve', 'pool', 'sp']
        self.ops = {e: [] for e in self.names}
        self.cnt = {e: 0 for e in self.names}
        self.dma_i = {'sp': 0, 'pool': 0}
        self.waited = {e: {} for e in self.names}
        self.lastw = {}
        self.readers = {}
        self.semnames = list(self.names) + [f"dq_{q}_{i}" for q in ('sp', 'pool') for i in range(NDQ)]
        self.semval = {s: 0 for s in self.semnames}

    def _need(self, eng, deps):
        out = []
        wd = self.waited[eng]
        for s, v in deps.items():
            if v > wd.get(s, 0):
                wd[s] = v
                out.append((s, v))
        return out

    def _collect(self, r, w):
        deps = {}

        def add(ev):
            if ev is None:
                return
            s, v = ev
            if v > deps.get(s, 0):
                deps[s] = v
        for k in list(r) + list(w):
            add(self.lastw.get(k))
        for k in w:
            for s, v in self.readers.get(k, {}).items():
                add((s, v))
        return deps

    def _commit(self, ev, r, w):
        for k in w:
            self.lastw[k] = ev
            self.readers[k] = {}
        for k in r:
            if k in w:
                continue
            d = self.readers.setdefault(k, {})
            if ev[1] > d.get(ev[0], 0):
                d[ev[0]] = ev[1]

    def op(self, eng, fn, r=(), w=()):
        deps = self._collect(r, w)
        waits = self._need(eng, deps)
        self.cnt[eng] += 1
        ev = (eng, self.cnt[eng])
        self.semval[eng] = self.cnt[eng]
        self.ops[eng].append((waits, fn, (eng, 1)))
        self._commit(ev, r, w)

    def dma(self, q, out, in_, r=(), w=(), **kw):
        deps = self._collect(r, w)
        i = self.dma_i[q]
        self.dma_i[q] += 1
        s = f"dq_{q}_{i % NDQ}"
        prev = 16 * (i // NDQ)
        if prev > 0:
            deps[s] = max(deps.get(s, 0), prev)
        waits = self._need(q, deps)
        ev = (s, prev + 16)
        self.semval[s] = prev + 16

        def fn(e, out=out, in_=in_, kw=kw):
            return e.dma_start(out=out, in_=in_, **kw)
        self.ops[q].append((waits, fn, (s, 16)))
        self._commit(ev, r, w)

    def barrier(self):
        allv = {s: v for s, v in self.semval.items() if v > 0}
        for e in self.names:
            waits = self._need(e, dict(allv))
            if waits:
                self.ops[e].append((waits, None, None))
        self.lastw = {}
        self.readers = {}

    def emit(self, es):
        nc = self.nc
        sems = {s: es.enter_context(nc.semaphore(s)) for s in self.semnames}
        block = es.enter_context(nc.Block())
        engs = {'pe': block.tensor, 'act': block.scalar, 'dve': block.vector, 'pool': block.gpsimd, 'sp': block.sync}
        final = {s: v for s, v in self.semval.items() if v > 0}
        for name in self.names:
            ops = self.ops[name]
            last = (name == 'sp')

            def body(e, ops=ops, last=last):
                for waits, fn, inc in ops:
                    for s, v in waits:
                        e.wait_ge(sems[s], v)
                    if fn is not None:
                        ins = fn(e)
                        ins.then_inc(sems[inc[0]], inc[1])
                if last:
                    for s, v in final.items():
                        e.wait_ge(sems[s], v)
            engs[name](body)


class KB:
    def __init__(self, NP, LS, DEPTH, dbg=None):
        self.NP, self.LS, self.DEPTH = NP, LS, DEPTH
        self.T = NP * 256 + LS
        self.NCH = self.T // 128
        self.seqs = [(2 * p, 2, 0) for p in range(NP)] + [(2 * NP, LS // 128, 1)]
        self.cond_of = []
        for c0, n, cd in self.seqs:
            self.cond_of += [cd] * n
        self.nc = bass.Bass("TRN2", target_bir_lowering=False)
        self.P = Prog(self.nc)
        self.es = contextlib.ExitStack()
        self.inputs = {}
        self.dbg = dbg or {}
        self.dbg_level = 99
        self._uid = 0

    def inp(self, name, shape, dtype=F32):
        ap = self.nc.dram_tensor(name, list(shape), dtype, kind="ExternalInput").ap()
        self.inputs[name] = ap
        return ap

    def outp(self, name, shape, dtype=F32):
        return self.nc.dram_tensor(name, list(shape), dtype, kind="ExternalOutput").ap()

    def scr(self, name, shape, dtype=F32):
        return self.nc.dram_tensor(name, list(shape), dtype, kind="Internal").ap()

    def sb(self, name, shape, dtype=F32):
        return self.es.enter_context(self.nc.sbuf_tensor(name, list(shape), dtype))

    def ps(self, name, shape, dtype=F32):
        return self.es.enter_context(self.nc.psum_tensor(name, list(shape), dtype))

    def act(self, out, in_, func, r, w, bias=None, scale=None, accum_out=None):
        kw = {}
        if bias is not None:
            kw['bias'] = bias
        if scale is not None:
            kw['scale'] = scale
        if accum_out is not None:
            kw['accum_out'] = accum_out
        self.P.op('act', lambda e: e.activation(out=out, in_=in_, func=func, **kw), r, w)

    def tt(self, out, in0, in1, op, r, w, eng='dve'):
        self.P.op(eng, lambda e: e.tensor_tensor(out=out, in0=in0, in1=in1, op=op), r, w)

    def ts(self, out, in0, s1, op0, r, w, s2=None, op1=None, eng='dve'):
        if op1 is None:
            self.P.op(eng, lambda e: e.tensor_scalar(out=out, in0=in0, scalar1=s1, scalar2=None, op0=op0), r, w)
        else:
            self.P.op(eng, lambda e: e.tensor_scalar(out=out, in0=in0, scalar1=s1, scalar2=s2, op0=op0, op1=op1), r, w)

    def stt(self, out, in0, scalar, in1, op0, op1, r, w):
        self.P.op('dve', lambda e: e.scalar_tensor_tensor(out=out, in0=in0, scalar=scalar, in1=in1, op0=op0, op1=op1), r, w)

    def cp(self, out, in_, r, w, eng='dve'):
        if eng == 'act':
            self.P.op('act', lambda e: e.copy(out=out, in_=in_), r, w)
        else:
            self.P.op(eng, lambda e: e.tensor_copy(out=out, in_=in_), r, w)

    def memset(self, out, val, w):
        self.P.op('dve', lambda e: e.memset(out, val), (), w)

    def mm(self, out, pairs, r, w, transpose=False):
        n = len(pairs)

        def fn(e):
            ins = None
            for i, (l, rr) in enumerate(pairs):
                ins = e.matmul(out, lhsT=l, rhs=rr, start=(i == 0), stop=(i == n - 1))
            return ins
        self.P.op('pe', fn, r, w)

    def mmx(self, items, r, w):
        def fn(e):
            ins = None
            for (o, l, rr, st, sp) in items:
                ins = e.matmul(o, lhsT=l, rhs=rr, start=st, stop=sp)
            return ins
        self.P.op('pe', fn, r, w)

    def tr(self, outs_ins, ident, r, w):
        def fn(e):
            ins = None
            for o, i in outs_ins:
                ins = e.transpose(o, i, ident)
            return ins
        self.P.op('pe', fn, r, w)

    def ld(self, out, in_, r, w, q='sp', **kw):
        self.P.dma(q, out, in_, r, w, **kw)


class Arena:
    def __init__(self, kb, sizes):
        self.regs = [kb.sb(f"arena{i}", [128, n], F32) for i, n in enumerate(sizes)]
        self.sizes = sizes
        self.reset()
        self.uid = 0

    def reset(self):
        self.off = [0] * len(self.regs)

    def tile(self, shape, dtype=F32):
        n = 1
        for s in shape[1:]:
            n *= s
        nw = n if dtype == F32 else (n + 1) // 2
        for i, reg in enumerate(self.regs):
            if self.off[i] + nw <= self.sizes[i]:
                ap = reg[:, self.off[i]:self.off[i] + nw]
                self.off[i] += nw
                if dtype != F32:
                    ap = ap.bitcast(dtype)
                    if n != nw * 2:
                        ap = ap[:, :n]
                if len(shape) == 3:
                    ap = ap.rearrange("p (a b) -> p a b", b=shape[2])
                elif len(shape) == 4:
                    ap = ap.rearrange("p (a b c) -> p a b c", b=shape[2], c=shape[3])
                self.uid += 1
                return ap, f"t{self.uid}"
        raise RuntimeError(f"arena full for {shape}")


def build(NP, LS, DEPTH, dbg=False):
    kb = KB(NP, LS, DEPTH)
    nc, P = kb.nc, kb.P
    T, NCH = kb.T, kb.NCH
    NLA = (DEPTH + 1) // 2
    NLC = DEPTH // 2
    LENS = sorted({256, LS})

    I = {}
    def inp(name, shape):
        I[name] = kb.inp(name, shape)
    inp('xp', [NP * 256, D]); inp('xs', [LS, D]); inp('cvec', [2, D])
    inp('sssd', [NLA, 2, 2048, 128]); inp('sret', [NLA, 2, 2048, 256])
    inp('ada_w', [DEPTH, D, 6 * D]); inp('ada_b', [DEPTH, 6 * D]); inp('norm1_w', [DEPTH, D]); inp('norm2_w', [DEPTH, D])
    inp('ab_w_in', [NLA, D, AB_IN]); inp('ab_conv_w', [NLA, 3, 3072]); inp('ab_conv_b', [NLA, 3072])
    inp('ssd_dt_bias', [NLA, 64]); inp('ssd_a_log', [NLA, 64]); inp('ssd_d', [NLA, 32]); inp('ssd_norm_w', [NLA, D])
    inp('ret_log_gamma', [NLA, 16]); inp('ret_norm_w', [NLA, D]); inp('ab_w_out', [NLA, 2 * D, D])
    NLH = max(NLC, 1)
    inp('hy_w_in', [NLH, D, 3 * D]); inp('hy_short_w', [NLH, 3, 3 * D]); inp('hy_short_b', [NLH, 3 * D])
    inp('hy_f_w1', [NLH, 33, 64]); inp('hy_f_b1', [NLH, 64]); inp('hy_f_w2', [NLH, 64, 64]); inp('hy_f_b2', [NLH, 64])
    inp('hy_f_w3', [NLH, 64, 4 * D]); inp('hy_bias', [NLH, 2, D]); inp('hy_w_out', [NLH, D, D])
    inp('ffn_w_gate', [DEPTH, D, DFF]); inp('ffn_w_up', [DEPTH, D, DFF]); inp('ffn_conv_w', [DEPTH, 3, DFF])
    inp('ffn_conv_b', [DEPTH, DFF]); inp('ffn_w_down', [DEPTH, DFF, D]); inp('final_norm_w', [D])
    inp('k_ident', [128, 128]); inp('k_ule', [128, 128]); inp('k_lge', [128, 128])
    inp('k_maskf', [128, 128]); inp('k_maskb', [128, 128]); inp('k_difff', [128, 128]); inp('k_diffb', [128, 128])
    inp('k_posc', [128, 4]); inp('k_alt', [128, 128])
    inp('k_rope', [4, LS, 64]); inp('k_delta', [D])
    for L in LENS:
        inp(f'k_featsT{L}', [33, L]); inp(f'k_negt{L}', [128, L // 128])
        inp(f'k_dftc{L}', [L, L]); inp(f'k_dfts{L}', [L, L]); inp(f'k_dftst{L}', [L, L])

    O = {}
    O['yp'] = kb.outp('yp', [NP * 256, D]); O['ys'] = kb.outp('ys', [LS, D])
    O['nssd'] = kb.outp('nssd', [NP, NLA, 2, 2048, 128]); O['nret'] = kb.outp('nret', [NP, NLA, 2, 2048, 256])

    X = kb.scr('X', [T, D]); MODD = kb.scr('MODD', [8, 6 * D])
    PROJA = kb.scr('PROJA', [T, 5184]); PROJB = kb.scr('PROJB', [T, 8192]); XBC = kb.scr('XBC', [T, 3072]); QK = kb.scr('QK', [T, 2 * D], BF16)
    YF = kb.scr('YF', [T, 2 * D]); YG = kb.scr('YG', [T, 2 * D], BF16); MIXO = kb.scr('MIXO', [T, D])
    FA = kb.scr('FA', [T, DFF]); FU = kb.scr('FU', [T, DFF]); FH = kb.scr('FH', [T, DFF], BF16)
    XG = kb.scr('XG', [T, 3 * D]); Z1 = kb.scr('Z1', [T, D]); Z2 = kb.scr('Z2', [T, D])
    FILT = {L: kb.scr(f'FILT{L}', [L, 4 * D]) for L in LENS}
    RN = {L: kb.scr(f'RN{L}', [4 * D]) for L in LENS}
    FSP = {L: kb.scr(f'FSP{L}', [2, 2, L, D]) for L in LENS}
    ZERO = kb.scr('ZERO', [1, 6144])
    FWT = 256
    TAB = {L: {nm: kb.scr(f'TAB{nm}{L}', [L // FWT, 128, (L // 128) * FWT], BF16) for nm in ('c', 's', 'st')} for L in LENS}

    IDf = kb.sb('IDf', [128, 128]); IDb = kb.sb('IDb', [128, 128], BF16)
    ULE = kb.sb('ULE', [128, 128]); LGE = kb.sb('LGE', [128, 128]); ONES = kb.sb('ONES', [128, 128])
    MASK = [kb.sb('MASKF', [128, 128]), kb.sb('MASKB', [128, 128])]
    DIFF = [kb.sb('DIFFF', [128, 128]), kb.sb('DIFFB', [128, 128])]
    POSC = kb.sb('POSC', [128, 4]); ALTb = kb.sb('ALTb', [128, 128], BF16)
    AR = Arena(kb, [16384, 16384, 12288, 3072])
    PSL = [kb.ps('PSL0', [128, 512]), kb.ps('PSL1', [128, 512])]
    PST = kb.ps('PST', [128, 1024], BF16)
    PSA = [kb.ps('PSA0', [128, 512]), kb.ps('PSA1', [128, 512])]
    PSS = kb.ps('PSS', [128, 512]); PSY = kb.ps('PSY', [128, 512]); PSI = kb.ps('PSI', [128, 512])

    def ldc(dst, key, src, cast=False):
        kb.ld(dst, src, [], [key], q='pool' if cast else 'sp')
    ldc(IDf[:], 'IDf', I['k_ident']); ldc(IDb[:], 'IDb', I['k_ident'], True)
    ldc(ULE[:], 'ULE', I['k_ule']); ldc(LGE[:], 'LGE', I['k_lge'])
    ldc(MASK[0][:], 'MASKF', I['k_maskf']); ldc(MASK[1][:], 'MASKB', I['k_maskb'])
    ldc(DIFF[0][:], 'DIFFF', I['k_difff']); ldc(DIFF[1][:], 'DIFFB', I['k_diffb'])
    ldc(POSC[:], 'POSC', I['k_posc']); ldc(ALTb[:], 'ALTb', I['k_alt'], True)
    kb.memset(ONES[:], 1.0, ['ONES'])
    zt, zk = AR.tile([128, 6144])
    kb.memset(zt[0:1, :], 0.0, [zk])
    kb.ld(ZERO[:, :], zt[0:1, :], [zk], ['ZERO'])
    for c in range(NCH):
        r0 = c * 128
        if r0 < NP * 256:
            src = I['xp'][r0:r0 + 128, :]
        else:
            src = I['xs'][r0 - NP * 256:r0 - NP * 256 + 128, :]
        kb.ld(X[r0:r0 + 128, :], src, [], [('X', c)])
    P.barrier(); AR.reset()
    tbufs = [AR.tile([128, 2048], BF16) for _ in range(4)]
    ti = 0
    for L in LENS:
        for nm in ('c', 's', 'st'):
            src = I[f'k_dft{nm}{L}']
            dstv = TAB[L][nm].rearrange("fb p (tc fw) -> p fb tc fw", fw=FWT)
            for tc in range(L // 128):
                for h0 in range(0, L, 2048):
                    w_ = min(2048, L - h0)
                    tb_, tbk_ = tbufs[ti % 4]; ti += 1
                    kb.ld(tb_[:, :w_], src[tc * 128:(tc + 1) * 128, h0:h0 + w_], [], [tbk_], q='pool')
                    kb.ld(dstv[:, h0 // FWT:(h0 + w_) // FWT, tc, :], tb_[:, :w_].rearrange("p (fb fw) -> p fb fw", fw=FWT), [tbk_], ['TAB'])
    P.barrier(); AR.reset()

    rr = {'i': 0}
    def alt_eng():
        rr['i'] += 1
        return 'act' if rr['i'] % 2 else 'dve'

    def bcast_row(dst, key, vec_ap, r=()):
        kb.ld(dst, vec_ap.partition_broadcast(128), list(r), [key])

    def stage_mod():
        cT, ck = AR.tile([128, 2, 16]); sT, sk = AR.tile([128, 2, 16])
        lhs, lk = AR.tile([128, 2, 16, 128], BF16)
        for cd in range(2):
            kb.ld(cT[:, cd, :], I['cvec'][cd, :].rearrange("(kc p) -> p kc", p=128), [], [ck], allow_slow_non_contiguous=True)
        kb.act(sT, cT, AF.Silu, [ck], [sk])
        for cd in range(2):
            kb.cp(lhs[:, cd, :, :], sT[:, cd, :].unsqueeze(2).to_broadcast([128, 16, 128]), [sk], [lk])
        wbs = [AR.tile([128, 16, 512], BF16) for _ in range(2)]
        abs_ = [AR.tile([128, 512]) for _ in range(2)]
        mos = [AR.tile([128, 512]) for _ in range(2)]
        n = 0
        for l in range(DEPTH):
            wv = I['ada_w'][l].rearrange("(kc p) f -> p kc f", p=128)
            for blk in range(24):
                wb, wk = wbs[n % 2]; ab, ak = abs_[n % 2]
                kb.ld(wb, wv[:, :, blk * 512:(blk + 1) * 512], [], [wk], q='pool')
                bcast_row(ab, ak, I['ada_b'][l, blk * 512:(blk + 1) * 512])
                for cd in range(2):
                    ps = PSL[cd]
                    kb.mm(ps[:, :], [(lhs[:, cd, kc, :], wb[:, kc, :]) for kc in range(16)], [lk, wk], [f'PSL{cd}'])
                    mo, mk = mos[cd]
                    kb.tt(mo, ps[:, :], ab, ALU.add, [f'PSL{cd}', ak], [mk])
                    kb.ld(MODD[l * 2 + cd:l * 2 + cd + 1, blk * 512:(blk + 1) * 512], mo[0:1, :], [mk], ['MODD'])
                n += 1
        P.barrier(); AR.reset()

    def linear(prep_setup, Fin, outs, rkeys_of):
        kch = Fin // 128
        GC = {16: 12, 32: 6, 44: 4}[kch]
        BLK = 512 if kch <= 32 else 256
        HT, hk = AR.tile([128, kch, GC * 128], BF16)
        wbs = [AR.tile([128, kch, BLK], BF16) for _ in range(2)]
        ots = [AR.tile([128, 512]) for _ in range(2)]
        prep = prep_setup()
        chunks = list(range(NCH))
        wi = 0; oi = 0
        for g0 in range(0, NCH, GC):
            grp = chunks[g0:g0 + GC]
            for gi, c in enumerate(grp):
                src, skey = prep(c)
                for k0 in range(0, kch, 8):
                    nb = min(8, kch - k0)
                    kb.tr([(PST[:, j * 128:(j + 1) * 128], src[:, (k0 + j) * 128:(k0 + j + 1) * 128]) for j in range(nb)],
                          IDb[:], [skey, 'IDb'], ['PST'])
                    kb.cp(HT[:, k0:k0 + nb, gi * 128:(gi + 1) * 128],
                          PST[:, :nb * 128].rearrange("p (a b) -> p a b", b=128), ['PST'], [(hk, gi)], eng=alt_eng())
            for (W, c_lo, c_hi, dst, dname, dcol) in outs:
                wv = W.rearrange("(kc p) f -> p kc f", p=128)
                for b0 in range(c_lo, c_hi, BLK):
                    ncol = min(BLK, c_hi - b0)
                    wb, wk = wbs[wi % 2]; wi += 1
                    kb.ld(wb[:, :, :ncol], wv[:, :, b0:b0 + ncol], [], [wk], q='pool')
                    for gi, c in enumerate(grp):
                        ps = PSL[oi % 2]; pk = f'PSL{oi % 2}'
                        ot, ok = ots[oi % 2]; oi += 1
                        kb.mm(ps[:, :ncol], [(HT[:, kc, gi * 128:(gi + 1) * 128], wb[:, kc, :ncol]) for kc in range(kch)],
                              [(hk, gi), wk], [pk])
                        kb.cp(ot[:, :ncol], ps[:, :ncol], [pk], [ok], eng=alt_eng())
                        kb.ld(dst[c * 128:(c + 1) * 128, dcol + b0 - c_lo:dcol + b0 - c_lo + ncol], ot[:, :ncol], [ok], [(dname, c)])
        P.barrier(); AR.reset()

    def make_prep_norm(l, which, normw_ap, resid_from=None):
        def setup():
            arow = []; srow = []; grow = []
            nw, nk = AR.tile([128, D])
            bcast_row(nw, nk, normw_ap)
            for cd in range(2):
                a, ak = AR.tile([128, D]); s, sk = AR.tile([128, D])
                bcast_row(s, sk, MODD[l * 2 + cd, (3 * which) * D:(3 * which + 1) * D], ['MODD'])
                bcast_row(a, ak, MODD[l * 2 + cd, (3 * which + 1) * D:(3 * which + 2) * D], ['MODD'])
                kb.stt(a, a, 1.0, nw, ALU.add, ALU.mult, [ak, nk], [ak])
                arow.append((a, ak)); srow.append((s, sk))
                if resid_from is not None:
                    gl, gidx = resid_from
                    g, gk = AR.tile([128, D])
                    bcast_row(g, gk, MODD[gl * 2 + cd, gidx * D:(gidx + 1) * D], ['MODD'])
                    grow.append((g, gk))
            xts = [AR.tile([128, D]) for _ in range(2)]
            mts = [AR.tile([128, D]) for _ in range(2)]
            hbs = [AR.tile([128, D], BF16) for _ in range(2)]
            sts = [AR.tile([128, 4]) for _ in range(2)]
            st = {'i': 0}

            def prep(c):
                i = st['i'] % 2; st['i'] += 1
                cd = kb.cond_of[c]
                xt, xk = xts[i]; mt, mk = mts[i]; hb, hk2 = hbs[i]; s4, s4k = sts[i]
                kb.ld(xt, X[c * 128:(c + 1) * 128, :], [('X', c)], [xk])
                if resid_from is not None:
                    kb.ld(mt, MIXO[c * 128:(c + 1) * 128, :], [('MIXO', c)], [mk])
                    g, gk = grow[cd]
                    kb.tt(mt, mt, g, ALU.mult, [mk, gk], [mk])
                    kb.tt(xt, xt, mt, ALU.add, [xk, mk], [xk])
                    kb.ld(X[c * 128:(c + 1) * 128, :], xt, [xk], [('X', c)])
                kb.tt(mt, xt, xt, ALU.mult, [xk], [mk])
                P.op('dve', lambda e, s4=s4, mt=mt: e.reduce_sum(out=s4[:, 0:1], in_=mt, axis=AX.X), [mk], [s4k])
                kb.ts(s4[:, 1:2], s4[:, 0:1], 1.0 / D, ALU.mult, [s4k], [s4k], s2=EPS, op1=ALU.add)
                kb.act(s4[:, 2:3], s4[:, 1:2], AF.Sqrt, [s4k], [s4k])
                P.op('dve', lambda e, s4=s4: e.reciprocal(out=s4[:, 3:4], in_=s4[:, 2:3]), [s4k], [s4k])
                a, ak = arow[cd]; s, sk = srow[cd]
                kb.stt(mt, xt, s4[:, 3:4], a, ALU.mult, ALU.mult, [xk, s4k, ak], [mk])
                kb.tt(hb, mt, s, ALU.add, [mk, sk], [hk2])
                return hb, hk2
            return prep
        return setup

    def make_prep_load(src, sname, Fin, is_bf16):
        def setup():
            if is_bf16:
                hbs = [AR.tile([128, Fin], BF16) for _ in range(2)]
            else:
                fts = [AR.tile([128, Fin]) for _ in range(2)]
                hbs = [AR.tile([128, Fin], BF16) for _ in range(2)]
            st = {'i': 0}

            def prep(c):
                i = st['i'] % 2; st['i'] += 1
                hb, hk2 = hbs[i]
                if is_bf16:
                    kb.ld(hb, src[c * 128:(c + 1) * 128, :], [(sname, c)], [hk2])
                else:
                    ft, fk = fts[i]
                    kb.ld(ft, src[c * 128:(c + 1) * 128, :], [(sname, c)], [fk])
                    kb.cp(hb, ft, [fk], [hk2], eng=alt_eng())
                return hb, hk2
            return prep
        return setup

    def conv3(src, sname, scol, F, w_ap, b_ap, post_setup, CB=1024):
        for col0 in range(0, F, CB):
            ncol = min(CB, F - col0)
            wts = []
            for k in range(3):
                wt, wk = AR.tile([128, CB]); bcast_row(wt[:, :ncol], wk, w_ap[k, col0:col0 + ncol]); wts.append((wt, wk))
            bt, bk = AR.tile([128, CB]); bcast_row(bt[:, :ncol], bk, b_ap[col0:col0 + ncol])
            bufs = [[AR.tile([128, CB]) for _ in range(3)] for _ in range(2)]
            accs = [AR.tile([128, CB]) for _ in range(2)]
            post = post_setup()
            n = 0
            for (c0, nchk, cd) in kb.seqs:
                for c in range(c0, c0 + nchk):
                    (pv, pk), (cu, ck), (nx, nk) = bufs[n % 2]; acc, akey = accs[n % 2]; n += 1
                    r0 = c * 128
                    cs = slice(scol + col0, scol + col0 + ncol)
                    kb.ld(cu[:, :ncol], src[r0:r0 + 128, cs], [(sname, c)], [ck])
                    if c == c0:
                        kb.ld(pv[0:1, :ncol], ZERO[:, :ncol], ['ZERO'], [pk])
                        kb.ld(pv[1:128, :ncol], src[r0:r0 + 127, cs], [(sname, c)], [pk])
                    else:
                        kb.ld(pv[:, :ncol], src[r0 - 1:r0 + 127, cs], [(sname, c), (sname, c - 1)], [pk])
                    if c == c0 + nchk - 1:
                        kb.ld(nx[127:128, :ncol], ZERO[:, :ncol], ['ZERO'], [nk])
                        kb.ld(nx[0:127, :ncol], src[r0 + 1:r0 + 128, cs], [(sname, c)], [nk])
                    else:
                        kb.ld(nx[:, :ncol], src[r0 + 1:r0 + 129, cs], [(sname, c), (sname, c + 1)], [nk])
                    kb.tt(acc[:, :ncol], cu[:, :ncol], wts[1][0][:, :ncol], ALU.mult, [ck, wts[1][1]], [akey])
                    kb.tt(pv[:, :ncol], pv[:, :ncol], wts[0][0][:, :ncol], ALU.mult, [pk, wts[0][1]], [pk])
                    kb.tt(nx[:, :ncol], nx[:, :ncol], wts[2][0][:, :ncol], ALU.mult, [nk, wts[2][1]], [nk])
                    kb.tt(acc[:, :ncol], acc[:, :ncol], pv[:, :ncol], ALU.add, [akey, pk], [akey])
                    kb.tt(acc[:, :ncol], acc[:, :ncol], nx[:, :ncol], ALU.add, [akey, nk], [akey])
                    kb.tt(acc[:, :ncol], acc[:, :ncol], bt[:, :ncol], ALU.add, [akey, bk], [akey])
                    post(c, col0, ncol, acc, akey)
            P.barrier(); AR.reset()

    def stage_ffn(l, resid_from):
        if kb.dbg_level < 2:
            return
        linear(make_prep_norm(l, 1, I['norm2_w'][l, :], resid_from), D,
               [(I['ffn_w_gate'][l], 0, DFF, FA, 'FA', 0), (I['ffn_w_up'][l], 0, DFF, FU, 'FU', 0)], None)
        def post_setup():
            us = [AR.tile([128, 1024]) for _ in range(2)]
            tsx = [AR.tile([128, 1024]) for _ in range(2)]
            hs = [AR.tile([128, 1024], BF16) for _ in range(2)]
            st = {'i': 0}

            def post(c, col0, ncol, acc, akey):
                i = st['i'] % 2; st['i'] += 1
                u, uk = us[i]; t, tk = tsx[i]; h, hk = hs[i]
                kb.ld(u[:, :ncol], FU[c * 128:(c + 1) * 128, col0:col0 + ncol], [('FU', c)], [uk])
                a = acc[:, :ncol]; tt_ = t[:, :ncol]
                kb.tt(tt_, a, a, ALU.mult, [akey], [tk])
                kb.ts(tt_, tt_, 0.044715, ALU.mult, [tk], [tk], s2=1.0, op1=ALU.add)
                kb.tt(tt_, tt_, a, ALU.mult, [tk, akey], [tk])
                kb.act(tt_, tt_, AF.Sigmoid, [tk], [tk], scale=1.5957691216057308)
                kb.tt(tt_, tt_, a, ALU.mult, [tk, akey], [tk])
                kb.tt(h[:, :ncol], tt_, u[:, :ncol], ALU.mult, [tk, uk], [hk])
                kb.ld(FH[c * 128:(c + 1) * 128, col0:col0 + ncol], h[:, :ncol], [hk], [('FH', c)], q='pool')
            return post
        if kb.dbg_level < 3:
            return
        conv3(FA, 'FA', 0, DFF, I['ffn_conv_w'][l], I['ffn_conv_b'][l], post_setup)
        if kb.dbg_level < 4:
            return
        linear(make_prep_load(FH, 'FH', DFF, True), DFF, [(I['ffn_w_down'][l], 0, D, MIXO, 'MIXO', 0)], None)


    def stage_ab(l, resid_from):
        li = l // 2
        linear(make_prep_norm(l, 0, I['norm1_w'][l, :], resid_from), D, [(I['ab_w_in'][li], 0, 5184, PROJA, 'PROJA', 0), (I['ab_w_in'][li], 5184, AB_IN, PROJB, 'PROJB', 0)], None)

        def post_setup():
            os_ = [AR.tile([128, 1024]) for _ in range(2)]
            st = {'i': 0}

            def post(c, col0, ncol, acc, akey):
                o, ok = os_[st['i'] % 2]; st['i'] += 1
                kb.act(o[:, :ncol], acc[:, :ncol], AF.Silu, [akey], [ok])
                kb.ld(XBC[c * 128:(c + 1) * 128, col0:col0 + ncol], o[:, :ncol], [ok], [('XBC', c)], q='pool')
            return post
        conv3(PROJA, 'PROJA', 2048, 3072, I['ab_conv_w'][li], I['ab_conv_b'][li], post_setup)

        qk, qkk = AR.tile([128, 4096]); ob, obk = AR.tile([128, 4096], BF16)
        ta, tak = AR.tile([128, 16, 64]); tb, tbk = AR.tile([128, 16, 64])
        rt = [AR.tile([128, 64]) for _ in range(4)]
        for (c0, nchk, cd) in kb.seqs:
            for c in range(c0, c0 + nchk):
                kb.ld(qk, PROJB[c * 128:(c + 1) * 128, 0:4096], [('PROJB', c)], [qkk])
                if cd == 0:
                    kb.cp(ob, qk, [qkk], [obk])
                else:
                    p0 = (c - c0) * 128
                    for j in range(4):
                        kb.ld(rt[j][0], I['k_rope'][j, p0:p0 + 128, :], [], [rt[j][1]])
                    qv = qk.rearrange("p (h q f) -> p h q f", q=4, f=64)
                    ov = ob.rearrange("p (h q f) -> p h q f", q=4, f=64)
                    for half in range(2):
                        cs_, csk = rt[2 * half]; sn, snk = rt[2 * half + 1]
                        cb = cs_.unsqueeze(1).to_broadcast([128, 16, 64]); sbb = sn.unsqueeze(1).to_broadcast([128, 16, 64])
                        t1 = qv[:, :, 2 * half, :]; t2 = qv[:, :, 2 * half + 1, :]
                        kb.tt(ta, t1, cb, ALU.mult, [qkk, csk], [tak])
                        kb.tt(tb, t2, sbb, ALU.mult, [qkk, snk], [tbk])
                        kb.tt(ov[:, :, 2 * half, :], ta, tb, ALU.subtract, [tak, tbk], [obk])
                        kb.tt(ta, t1, sbb, ALU.mult, [qkk, snk], [tak])
                        kb.tt(tb, t2, cb, ALU.mult, [qkk, csk], [tbk])
                        kb.tt(ov[:, :, 2 * half + 1, :], ta, tb, ALU.add, [tak, tbk], [obk])
                kb.ts(ob[:, 2048:], ob[:, 2048:], 0.0625, ALU.mult, [obk], [obk])
                kb.ld(QK[c * 128:(c + 1) * 128, :], ob, [obk], [('QK', c)], q='pool')
        P.barrier(); AR.reset()

        T_ = AR.tile
        arow, ark = T_([128, 64]); dtb, dtbk = T_([128, 64]); Drow, Dk = T_([128, 32]); lgb, lgk = T_([128, 16])
        bcast_row(arow, ark, I['ssd_a_log'][li, :]); bcast_row(dtb, dtbk, I['ssd_dt_bias'][li, :])
        bcast_row(Drow, Dk, I['ssd_d'][li, :]); bcast_row(lgb, lgk, I['ret_log_gamma'][li, :])
        kb.act(arow, arow, AF.Exp, [ark], [ark])
        kb.ts(arow, arow, -1.0, ALU.mult, [ark], [ark])
        DM = [T_([128, 8, 128]) for _ in range(2)]
        GP = [T_([128, 8]) for _ in range(2)]; TE = [T_([128, 8]) for _ in range(2)]; CDr = [T_([128, 8]) for _ in range(2)]
        for d in range(2):
            for h in range(8):
                kb.act(DM[d][0][:, h, :], DIFF[d][:], AF.Exp, ['DIFFF' if d == 0 else 'DIFFB', lgk], [DM[d][1]], scale=lgb[:, d * 8 + h:d * 8 + h + 1])
            kb.act(GP[d][0], lgb[:, d * 8:d * 8 + 8], AF.Exp, [lgk, 'POSC'], [GP[d][1]], scale=POSC[:, d:d + 1])
            kb.act(TE[d][0], lgb[:, d * 8:d * 8 + 8], AF.Exp, [lgk, 'POSC'], [TE[d][1]], scale=POSC[:, 2 + d:3 + d])
            kb.act(CDr[d][0], lgb[:, d * 8:d * 8 + 8], AF.Exp, [lgk], [CDr[d][1]], scale=128.0)
        nws, nwsk = T_([128, D]); nwr, nwrk = T_([128, D])
        bcast_row(nws, nwsk, I['ssd_norm_w'][li, :]); bcast_row(nwr, nwrk, I['ret_norm_w'][li, :])
        HS, HSk = T_([128, 4, 512]); HSb, HSbk = T_([128, 4, 512], BF16)
        HR, HRk = T_([128, 8, 2, 256]); HRb, HRbk = T_([128, 8, 2, 256], BF16)
        xbc, xbck = T_([128, 3072]); bcb, bcbk = T_([128, 1024], BF16)
        vd, vdk = T_([128, 32, 64], BF16); vte, vtek = T_([128, 32, 64], BF16)
        sm, smk = T_([128, 12, 32])
        BCT, BCTk = T_([128, 8, 128], BF16)
        seg, segk = T_([128, 4, 128]); MT, MTk = T_([128, 4, 128], BF16)
        qkb, qkbk = T_([128, 4096], BF16); QKT, QKTk = T_([128, 32, 128], BF16)
        kte, ktek = T_([128, 2048], BF16)
        vt, vtk = T_([128, D]); vb, vbk = T_([128, D], BF16)
        ych, ychk = T_([128, 4096]); yfl, yflk = T_([128, 4096])
        tmp, tmpk = T_([128, 512])
        yg, ygk = T_([128, 4096], BF16)
        st8, st8k = T_([128, 4, 8])
        blk, blkk = T_([128, 256])
        DTR, X0, AB_, EX, L1, DT, LA, ACM, TOT, EAC, TEv, CDv = range(12)

        def chunk(c, d, sample):
            r0 = c * 128
            tri = ULE if d == 0 else LGE
            trik = 'ULE' if d == 0 else 'LGE'
            kb.ld(xbc, XBC[r0:r0 + 128, :], [('XBC', c)], [xbck])
            kb.ld(sm[:, DTR, :], PROJA[r0:r0 + 128, 5120 + d * 32:5152 + d * 32], [('PROJA', c)], [smk])
            kb.ld(qkb, QK[r0:r0 + 128, :], [('QK', c)], [qkbk])
            kb.ld(vt, PROJB[r0:r0 + 128, 4096:6144], [('PROJB', c)], [vtk])
            kb.cp(bcb, xbc[:, 2048:3072], [xbck], [bcbk])
            kb.cp(vb, vt, [vtk], [vbk], eng='act')
            kb.tt(sm[:, X0, :], sm[:, DTR, :], dtb[:, d * 32:d * 32 + 32], ALU.add, [smk, dtbk], [smk])
            kb.ts(sm[:, AB_, :], sm[:, X0, :], -1.0, ALU.mult, [smk], [smk])
            kb.tt(sm[:, AB_, :], sm[:, AB_, :], sm[:, X0, :], ALU.max, [smk], [smk])
            kb.act(sm[:, EX, :], sm[:, AB_, :], AF.Exp, [smk], [smk], scale=-1.0)
            kb.ts(sm[:, EX, :], sm[:, EX, :], 1.0, ALU.add, [smk], [smk])
            kb.act(sm[:, L1, :], sm[:, EX, :], AF.Ln, [smk], [smk])
            kb.ts(sm[:, X0, :], sm[:, X0, :], 0.0, ALU.max, [smk], [smk])
            kb.tt(sm[:, DT, :], sm[:, X0, :], sm[:, L1, :], ALU.add, [smk], [smk])
            kb.tt(sm[:, LA, :], sm[:, DT, :], arow[:, d * 32:d * 32 + 32], ALU.mult, [smk, ark], [smk])
            kb.mmx([(PSI[:, 0:32], tri[:], sm[:, LA, :], True, True), (PSI[:, 32:64], ONES[:], sm[:, LA, :], True, True)],
                   [trik, 'ONES', smk], ['PSI'])
            kb.cp(sm[:, ACM:ACM + 2, :], PSI[:, 0:64].rearrange("p (a b) -> p a b", b=32), ['PSI'], [smk])
            kb.act(sm[:, EAC, :], sm[:, ACM, :], AF.Exp, [smk], [smk])
            kb.tt(sm[:, TEv, :], sm[:, TOT, :], sm[:, ACM, :], ALU.subtract, [smk], [smk])
            kb.act(sm[:, TEv, :], sm[:, TEv, :], AF.Exp, [smk], [smk])
            kb.act(sm[:, CDv, :], sm[:, TOT, :], AF.Exp, [smk], [smk])
            xsv = xbc[:, 0:2048].rearrange("p (h f) -> p h f", f=64)
            kb.tt(vd, xsv, sm[:, DT, :].unsqueeze(2).to_broadcast([128, 32, 64]), ALU.mult, [xbck, smk], [vdk])
            kb.tt(sm[:, X0, :], sm[:, DT, :], sm[:, TEv, :], ALU.mult, [smk], [smk])
            kb.tt(vte, xsv, sm[:, X0, :].unsqueeze(2).to_broadcast([128, 32, 64]), ALU.mult, [xbck, smk], [vtek])
            kb.tr([(PST[:, j * 128:(j + 1) * 128], bcb[:, j * 128:(j + 1) * 128]) for j in range(8)], IDb[:], [bcbk, 'IDb'], ['PST'])
            kb.cp(BCT, PST[:, :].rearrange("p (a b) -> p a b", b=128), ['PST'], [BCTk])
            kb.mmx([(PSS[:, g * 128:(g + 1) * 128], BCT[:, g, :], BCT[:, 4 + g, :], True, True) for g in range(4)], [BCTk], ['PSS'])
            kb.cp(HSb, HS, [HSk], [HSbk])
            for g in range(4):
                for hh in (0, 4):
                    pa = PSA[(hh // 4) % 2]; pak = f'PSA{(hh // 4) % 2}'
                    kb.mmx([(pa[:, j * 128:(j + 1) * 128], sm[:, LA, g * 8 + hh + j:g * 8 + hh + j + 1].to_broadcast([128, 128]), tri[:], True, True)
                            for j in range(4)], [smk, trik], [pak])
                    for j in range(4):
                        h = g * 8 + hh + j
                        kb.stt(seg[:, j, :], pa[:, j * 128:(j + 1) * 128], sm[:, ACM, h:h + 1], MASK[d][:], ALU.subtract, ALU.add,
                               [pak, smk, 'MASKF' if d == 0 else 'MASKB'], [segk])
                    kb.act(seg, seg, AF.Exp, [segk], [segk])
                    kb.tt(MT, seg, PSS[:, g * 128:(g + 1) * 128].unsqueeze(1).to_broadcast([128, 4, 128]), ALU.mult, [segk, 'PSS'], [MTk])
                    kb.mmx([(PSY[:, (hh + j) * 64:(hh + j + 1) * 64], MT[:, j, :], vd[:, g * 8 + hh + j, :], True, True) for j in range(4)],
                           [MTk, vdk], ['PSY'])
                kb.mmx([(PSI[:, :], BCT[:, 4 + g, :], HSb[:, g, :], True, True)], [BCTk, HSbk], ['PSI'])
                kb.tt(tmp.rearrange("p (h f) -> p h f", f=64), PSI[:, :].rearrange("p (h f) -> p h f", f=64),
                      sm[:, EAC, g * 8:g * 8 + 8].unsqueeze(2).to_broadcast([128, 8, 64]), ALU.mult, ['PSI', smk], [tmpk])
                kb.tt(ych[:, g * 512:(g + 1) * 512], PSY[:, :], tmp, ALU.add, ['PSY', tmpk], [ychk])
                kb.mmx([(PSI[:, :], bcb[:, g * 128:(g + 1) * 128], vte[:, g * 8:g * 8 + 8, :].rearrange("p h f -> p (h f)"), True, True)],
                       [bcbk, vtek], ['PSI'])
                hsg = HS[:, g, :].rearrange("p (h f) -> p h f", f=64)
                kb.tt(hsg, hsg, sm[:, CDv, g * 8:g * 8 + 8].unsqueeze(2).to_broadcast([128, 8, 64]), ALU.mult, [HSk, smk], [HSk])
                kb.tt(HS[:, g, :], HS[:, g, :], PSI[:, :], ALU.add, [HSk, 'PSI'], [HSk])
            for rnd in range(4):
                kb.tr([(PST[:, j * 128:(j + 1) * 128], qkb[:, (rnd * 8 + j) * 128:(rnd * 8 + j + 1) * 128]) for j in range(8)],
                      IDb[:], [qkbk, 'IDb'], ['PST'])
                kb.cp(QKT[:, rnd * 8:rnd * 8 + 8, :], PST[:, :].rearrange("p (a b) -> p a b", b=128), ['PST'], [QKTk], eng=alt_eng())
            kb.cp(HRb, HR, [HRk], [HRbk])
            for hq in (0, 4):
                items = []
                for j in range(4):
                    h = hq + j
                    for kc in range(2):
                        items.append((PSS[:, j * 128:(j + 1) * 128], QKT[:, 16 + 2 * h + kc, :], QKT[:, 2 * h + kc, :], kc == 0, kc == 1))
                kb.mmx(items, [QKTk], ['PSS'])
                kb.tt(MT, PSS[:, :].rearrange("p (a b) -> p a b", b=128), DM[d][0][:, hq:hq + 4, :], ALU.mult, ['PSS', DM[d][1]], [MTk])
                for jp in (0, 2):
                    h0 = hq + jp
                    kb.mmx([(PSY[:, jj * 256:(jj + 1) * 256], MT[:, jp + jj, :], vb[:, (h0 + jj) * 256:(h0 + jj + 1) * 256], True, True)
                            for jj in range(2)], [MTk, vbk], ['PSY'])
                    items = []
                    for jj in range(2):
                        for kc in range(2):
                            items.append((PSI[:, jj * 256:(jj + 1) * 256], QKT[:, 2 * (h0 + jj) + kc, :], HRb[:, h0 + jj, kc, :], kc == 0, kc == 1))
                    kb.mmx(items, [QKTk, HRbk], ['PSI'])
                    for jj in range(2):
                        kb.tt(tmp[:, jj * 256:(jj + 1) * 256], PSI[:, jj * 256:(jj + 1) * 256],
                              GP[d][0][:, h0 + jj:h0 + jj + 1].to_broadcast([128, 256]), ALU.mult, ['PSI', GP[d][1]], [tmpk])
                    kb.tt(ych[:, 2048 + h0 * 256:2048 + (h0 + 2) * 256], PSY[:, :], tmp, ALU.add, ['PSY', tmpk], [ychk])
            for h in range(8):
                kb.tt(kte[:, h * 256:(h + 1) * 256], qkb[:, 2048 + h * 256:2048 + (h + 1) * 256],
                      TE[d][0][:, h:h + 1].to_broadcast([128, 256]), ALU.mult, [qkbk, TE[d][1]], [ktek])
                for kc in range(2):
                    kb.mmx([(PSI[:, 0:256], kte[:, h * 256 + kc * 128:h * 256 + (kc + 1) * 128], vb[:, h * 256:(h + 1) * 256], True, True)],
                           [ktek, vbk], ['PSI'])
                    kb.stt(HR[:, h, kc, :], HR[:, h, kc, :], CDr[d][0][:, h:h + 1], PSI[:, 0:256], ALU.mult, ALU.add, [HRk, CDr[d][1], 'PSI'], [HRk])
            if d == 0:
                kb.ld(YF[r0:r0 + 128, :], ych, [ychk], [('YF', c)], q='pool')
                return
            kb.ld(yfl, YF[r0:r0 + 128, :], [('YF', c)], [yflk])
            kb.tt(ych, ych, yfl, ALU.add, [ychk, yflk], [ychk])
            zt_ = yfl[:, 0:2048]; gt_ = yfl[:, 2048:4096]
            kb.ld(zt_, PROJA[r0:r0 + 128, 0:2048], [('PROJA', c)], [yflk])
            kb.ld(gt_, PROJB[r0:r0 + 128, 6144:8192], [('PROJB', c)], [yflk])
            ys_ = ych[:, 0:2048]
            kb.tt(vt.rearrange("p (h f) -> p h f", f=64), xsv, Drow.unsqueeze(2).to_broadcast([128, 32, 64]), ALU.mult, [xbck, Dk], [vtk])
            kb.tt(ys_, ys_, vt, ALU.add, [ychk, vtk], [ychk])
            kb.act(zt_, zt_, AF.Silu, [yflk], [yflk])
            kb.tt(ys_, ys_, zt_, ALU.mult, [ychk, yflk], [ychk])
            kb.tt(vt, ys_, ys_, ALU.mult, [ychk], [vtk])
            P.op('dve', lambda e: e.reduce_sum(out=st8[:, 0, 0:4], in_=vt.rearrange("p (g f) -> p g f", f=512), axis=AX.X), [vtk], [st8k])
            kb.ts(st8[:, 0, 0:4], st8[:, 0, 0:4], 1.0 / 512, ALU.mult, [st8k], [st8k], s2=EPS, op1=ALU.add)
            kb.act(st8[:, 0, 0:4], st8[:, 0, 0:4], AF.Sqrt, [st8k], [st8k])
            P.op('dve', lambda e: e.reciprocal(out=st8[:, 1, 0:4], in_=st8[:, 0, 0:4]), [st8k], [st8k])
            for g in range(4):
                kb.stt(yg[:, g * 512:(g + 1) * 512], ys_[:, g * 512:(g + 1) * 512], st8[:, 1, g:g + 1], nws[:, g * 512:(g + 1) * 512],
                       ALU.mult, ALU.mult, [ychk, st8k, nwsk], [ygk])
            yr3 = ych[:, 2048:4096].rearrange("p (h f) -> p h f", f=256)
            vt3 = vt.rearrange("p (h f) -> p h f", f=256)
            P.op('dve', lambda e: e.reduce_sum(out=st8[:, 2, :], in_=yr3, axis=AX.X), [ychk], [st8k])
            kb.ts(st8[:, 2, :], st8[:, 2, :], 1.0 / 256, ALU.mult, [st8k], [st8k])
            kb.tt(yr3, yr3, st8[:, 2, :].unsqueeze(2).to_broadcast([128, 8, 256]), ALU.subtract, [ychk, st8k], [ychk])
            kb.tt(vt3, yr3, yr3, ALU.mult, [ychk], [vtk])
            P.op('dve', lambda e: e.reduce_sum(out=st8[:, 3, :], in_=vt3, axis=AX.X), [vtk], [st8k])
            kb.ts(st8[:, 3, :], st8[:, 3, :], 1.0 / 256, ALU.mult, [st8k], [st8k], s2=EPS, op1=ALU.add)
            kb.act(st8[:, 3, :], st8[:, 3, :], AF.Sqrt, [st8k], [st8k])
            P.op('dve', lambda e: e.reciprocal(out=st8[:, 2, :], in_=st8[:, 3, :]), [st8k], [st8k])
            kb.tt(yr3, yr3, st8[:, 2, :].unsqueeze(2).to_broadcast([128, 8, 256]), ALU.mult, [ychk, st8k], [ychk])
            kb.tt(ych[:, 2048:4096], ych[:, 2048:4096], nwr, ALU.mult, [ychk, nwrk], [ychk])
            kb.act(gt_, gt_, AF.Silu, [yflk], [yflk])
            kb.tt(yg[:, 2048:4096], ych[:, 2048:4096], gt_, ALU.mult, [ychk, yflk], [ygk])
            kb.ld(YG[r0:r0 + 128, :], yg, [ygk], [('YG', c)], q='pool')

        for si, (c0, nchk, cd) in enumerate(kb.seqs):
            sample = (cd == 1)
            for d in range(2):
                if not sample:
                    kb.memset(HS, 0.0, [HSk]); kb.memset(HR, 0.0, [HRk])
                else:
                    for rb in range(16):
                        kb.ld(blk[:, 0:128], I['sssd'][li, d, rb * 128:(rb + 1) * 128, :], [], [blkk])
                        kb.tr([(PSI[:, 0:128], blk[:, 0:128])], IDf[:], [blkk, 'IDf'], ['PSI'])
                        kb.cp(HS[:, rb // 4, (rb % 4) * 128:(rb % 4 + 1) * 128], PSI[:, 0:128], ['PSI'], [HSk])
                    for h in range(8):
                        for dvb in range(2):
                            kb.ld(blk, I['sret'][li, d, h * 256 + dvb * 128:h * 256 + (dvb + 1) * 128, :], [], [blkk])
                            for kc in range(2):
                                kb.tr([(PSI[:, 0:128], blk[:, kc * 128:(kc + 1) * 128])], IDf[:], [blkk, 'IDf'], ['PSI'])
                                kb.cp(HR[:, h, kc, dvb * 128:(dvb + 1) * 128], PSI[:, 0:128], ['PSI'], [HRk])
                order = range(c0, c0 + nchk) if d == 0 else range(c0 + nchk - 1, c0 - 1, -1)
                for c in order:
                    chunk(c, d, sample)
                if not sample:
                    for rb in range(16):
                        kb.tr([(PSI[:, 0:128], HS[:, rb // 4, (rb % 4) * 128:(rb % 4 + 1) * 128])], IDf[:], [HSk, 'IDf'], ['PSI'])
                        kb.cp(blk[:, 0:128], PSI[:, 0:128], ['PSI'], [blkk])
                        kb.ld(O['nssd'][si, li, d, rb * 128:(rb + 1) * 128, :], blk[:, 0:128], [blkk], [('NS', si, li, d, rb)])
                    for h in range(8):
                        for dvb in range(2):
                            for kc in range(2):
                                kb.tr([(PSI[:, 0:128], HR[:, h, kc, dvb * 128:(dvb + 1) * 128])], IDf[:], [HRk, 'IDf'], ['PSI'])
                                kb.cp(blk[:, kc * 128:(kc + 1) * 128], PSI[:, 0:128], ['PSI'], [blkk])
                            kb.ld(O['nret'][si, li, d, h * 256 + dvb * 128:h * 256 + (dvb + 1) * 128, :], blk, [blkk], [('NR', si, li, d, h, dvb)])
        P.barrier(); AR.reset()
        linear(make_prep_load(YG, 'YG', 2 * D, True), 2 * D, [(I['ab_w_out'][li], 0, D, MIXO, 'MIXO', 0)], None)


    def stage_hy(l, resid_from):
        li = l // 2
        T_ = AR.tile
        linear(make_prep_norm(l, 0, I['norm1_w'][l, :], resid_from), D, [(I['hy_w_in'][li], 0, 3 * D, PROJB, 'PROJB', 0)], None)

        def post_setup():
            def post(c, col0, ncol, acc, akey):
                kb.ld(XG[c * 128:(c + 1) * 128, col0:col0 + ncol], acc[:, :ncol], [akey], [('XG', c)], q='pool')
            return post
        conv3(PROJB, 'PROJB', 0, 3 * D, I['hy_short_w'][li], I['hy_short_b'][li], post_setup)
        MAGIC = 12582912.0
        TWO_PI = 2.0 * math.pi
        for L in LENS:
            nch = L // 128
            N2 = 2 * L
            ft, ftk = T_([128, L]); H1, H1k = T_([128, L]); H2, H2k = T_([128, L])
            w1, w1k = T_([128, 64]); w2, w2k = T_([128, 64]); w3, w3k = T_([128, 4 * D])
            bb, bbk = T_([128, 2]); u, uk = T_([128, 512]); r_, rk = T_([128, 512])
            dl, dlk = T_([128, D]); ng, ngk = T_([128, nch])
            kb.ld(ft[0:33, :], I[f'k_featsT{L}'], [], [ftk])
            kb.ld(w1[0:33, :], I['hy_f_w1'][li], [], [w1k]); kb.ld(w2[0:64, :], I['hy_f_w2'][li], [], [w2k])
            kb.ld(w3[0:64, :], I['hy_f_w3'][li], [], [w3k])
            kb.ld(bb[0:64, 0:1], I['hy_f_b1'][li, :].rearrange("(p o) -> p o", o=1), [], [bbk])
            kb.ld(bb[0:64, 1:2], I['hy_f_b2'][li, :].rearrange("(p o) -> p o", o=1), [], [bbk])
            bcast_row(dl, dlk, I['k_delta']); kb.ld(ng, I[f'k_negt{L}'], [], [ngk])
            for layer in range(2):
                src, srck, kk = (ft, ftk, 33) if layer == 0 else (H1, H1k, 64)
                wt, wtk = (w1, w1k) if layer == 0 else (w2, w2k)
                dstT, dstk = (H1, H1k) if layer == 0 else (H2, H2k)
                for b0 in range(0, L, 512):
                    bw = min(512, L - b0)
                    kb.mmx([(PSL[0][0:64, :bw], wt[0:kk, :], src[0:kk, b0:b0 + bw], True, True)], [wtk, srck], ['PSL0'])
                    kb.act(u[0:64, :bw], PSL[0][0:64, :bw], AF.Identity, ['PSL0', bbk], [uk], bias=bb[0:64, layer:layer + 1])
                    kb.ts(r_[0:64, :bw], u[0:64, :bw], 1.0 / TWO_PI, ALU.mult, [uk], [rk], s2=MAGIC, op1=ALU.add)
                    kb.ts(r_[0:64, :bw], r_[0:64, :bw], -MAGIC, ALU.add, [rk], [rk])
                    kb.stt(u[0:64, :bw], r_[0:64, :bw], -TWO_PI, u[0:64, :bw], ALU.mult, ALU.add, [rk, uk], [uk])
                    kb.ts(u[0:64, :bw], u[0:64, :bw], -3.14159, ALU.max, [uk], [uk], s2=3.14159, op1=ALU.min)
                    kb.act(dstT[0:64, b0:b0 + bw], u[0:64, :bw], AF.Sin, [uk], [dstk])
            win, wink = T_([128, 512]); hw, hwk = T_([128, 512]); ab, abk = T_([128, 512]); rn, rnk = T_([128, 512])
            for cbk in range(16):
                ccol = (cbk * 512) % D
                for tch in range(nch):
                    kb.mmx([(PSL[0][:, :], H2[0:64, tch * 128:(tch + 1) * 128], w3[0:64, cbk * 512:(cbk + 1) * 512], True, True)],
                           [H2k, w3k], ['PSL0'])
                    kb.act(win, dl[:, ccol:ccol + 512], AF.Exp, [dlk, ngk], [wink], scale=ng[:, tch:tch + 1])
                    kb.tt(hw, PSL[0][:, :], win, ALU.mult, ['PSL0', wink], [hwk])
                    kb.ld(FILT[L][tch * 128:(tch + 1) * 128, cbk * 512:(cbk + 1) * 512], hw, [hwk], [('FILT', tch, cbk)])
                    kb.ts(ab, hw, -1.0, ALU.mult, [hwk], [abk])
                    kb.tt(ab, ab, hw, ALU.max, [abk, hwk], [abk])
                    kb.mmx([(PSA[0][:, :], ONES[:], ab, tch == 0, tch == nch - 1)], ['ONES', abk], ['PSA0'])
                kb.ts(rn, PSA[0][:, :], EPS, ALU.add, ['PSA0'], [rnk])
                P.op('dve', lambda e, rn=rn: e.reciprocal(out=rn, in_=rn), [rnk], [rnk])
                kb.ld(RN[L][cbk * 512:(cbk + 1) * 512].rearrange("(o f) -> o f", o=1), rn[0:1, :], [rnk], [('RN', cbk)])
            P.barrier(); AR.reset()
            CW = 512
            FW = FWT
            hp, hpk = T_([128, nch, CW], BF16); hm, hmk = T_([128, nch, CW], BF16)
            CBS = [T_([128, nch, FW], BF16) for _ in range(2)]; SBS = [T_([128, nch, FW], BF16) for _ in range(2)]
            tbi = {'i': 0}
            hf, hfk = T_([128, CW]); hb_, hbk = T_([128, CW]); rf, rfk = T_([128, CW]); rb, rbk = T_([128, CW])
            fr, frk = T_([128, CW]); fi, fik = T_([128, CW])
            cv = I[f'k_dftc{L}'].rearrange("(tc p) f -> p tc f", p=128)
            sv = I[f'k_dfts{L}'].rearrange("(tc p) f -> p tc f", p=128)
            stv = I[f'k_dftst{L}'].rearrange("(tc p) f -> p tc f", p=128)
            for o in range(2):
                for cb in range(D // CW):
                    colf = (0 * 2 + o) * D + cb * CW; colb = (1 * 2 + o) * D + cb * CW
                    bcast_row(rf, rfk, RN[L][colf:colf + CW], [('RN', colf // 512)])
                    bcast_row(rb, rbk, RN[L][colb:colb + CW], [('RN', colb // 512)])
                    for tch in range(nch):
                        kb.ld(hf, FILT[L][tch * 128:(tch + 1) * 128, colf:colf + CW], [('FILT', tch, colf // 512)], [hfk])
                        kb.ld(hb_, FILT[L][tch * 128:(tch + 1) * 128, colb:colb + CW], [('FILT', tch, colb // 512)], [hbk])
                        if tch == 0:
                            kb.ld(hb_[0:1, :], ZERO[:, :CW], ['ZERO'], [hbk])
                        kb.tt(hf, hf, rf, ALU.mult, [hfk, rfk], [hfk])
                        kb.tt(hb_, hb_, rb, ALU.mult, [hbk, rbk], [hbk])
                        kb.tt(hp[:, tch, :], hf, hb_, ALU.add, [hfk, hbk], [hpk])
                        kb.tt(hm[:, tch, :], hf, hb_, ALU.subtract, [hfk, hbk], [hmk])
                    for f0 in range(0, L, FW):
                        (Cb, Cbk), (Sb, Sbk) = CBS[tbi['i'] % 2], SBS[tbi['i'] % 2]; tbi['i'] += 1
                        kb.ld(Cb.rearrange("p a b -> p (a b)"), TAB[L]['c'][f0 // FW], [], [Cbk], q='pool')
                        kb.ld(Sb.rearrange("p a b -> p (a b)"), TAB[L]['s'][f0 // FW], [], [Sbk], q='pool')
                        for fc in range(FW // 128):
                            fch = f0 // 128 + fc
                            kb.mmx([(PSL[0][:, :CW], Cb[:, t_, fc * 128:(fc + 1) * 128], hp[:, t_, :], t_ == 0, t_ == nch - 1) for t_ in range(nch)],
                                   [Cbk, hpk], ['PSL0'])
                            kb.mmx([(PSL[1][:, :CW], Sb[:, t_, fc * 128:(fc + 1) * 128], hm[:, t_, :], t_ == 0, t_ == nch - 1) for t_ in range(nch)],
                                   [Sbk, hmk], ['PSL1'])
                            kb.ts(fr, PSL[0][:, :CW], 2.0 / N2, ALU.mult, ['PSL0'], [frk])
                            kb.ts(fi, PSL[1][:, :CW], 2.0 / N2, ALU.mult, ['PSL1'], [fik])
                            if fch == 0:
                                kb.mmx([(PSA[1][:, :CW], ALTb[:], hp[:, t_, :], t_ == 0, t_ == nch - 1) for t_ in range(nch)], ['ALTb', hpk], ['PSA1'])
                                kb.ts(fi[0:1, :], PSA[1][0:1, :CW], 1.0 / N2, ALU.mult, ['PSA1', fik], [fik])
                                kb.ts(fr[0:1, :], fr[0:1, :], 0.5, ALU.mult, [frk], [frk])
                            kb.ld(FSP[L][o, 0, fch * 128:(fch + 1) * 128, cb * CW:(cb + 1) * CW], fr, [frk], [('FSP', o, 0, fch, cb)])
                            kb.ld(FSP[L][o, 1, fch * 128:(fch + 1) * 128, cb * CW:(cb + 1) * CW], fi, [fik], [('FSP', o, 1, fch, cb)])
            P.barrier(); AR.reset()
            Yb, Ybk = T_([128, nch, 2, CW], BF16); zb, zbk = T_([128, nch, CW], BF16)
            CBS = [T_([128, nch, FW], BF16) for _ in range(2)]; SBS = [T_([128, nch, FW], BF16) for _ in range(2)]
            zf, zfk = T_([128, CW]); gt, gtk = T_([128, CW]); brow, browk = T_([128, CW])
            fr, frk = T_([128, CW]); fi, fik = T_([128, CW]); t1, t1k = T_([128, CW]); t2, t2k = T_([128, CW])
            for (c0, nchk, cd) in kb.seqs:
                if nchk != nch:
                    continue
                for o in range(2):
                    zsrc, zname, zcol = (XG, 'XG', 2 * D) if o == 0 else (Z1, 'Z1', 0)
                    gcol = o * D
                    zdst, zdname = (Z1, 'Z1') if o == 0 else (Z2, 'Z2')
                    for cb in range(D // CW):
                        bcast_row(brow, browk, I['hy_bias'][li, o, cb * CW:(cb + 1) * CW])
                        for tch in range(nch):
                            c = c0 + tch
                            kb.ld(zf, zsrc[c * 128:(c + 1) * 128, zcol + cb * CW:zcol + (cb + 1) * CW], [(zname, c)], [zfk])
                            kb.cp(zb[:, tch, :], zf, [zfk], [zbk], eng=alt_eng())
                        for f0 in range(0, L, FW):
                            (Cb, Cbk), (Sb, Sbk) = CBS[tbi['i'] % 2], SBS[tbi['i'] % 2]; tbi['i'] += 1
                            kb.ld(Cb.rearrange("p a b -> p (a b)"), TAB[L]['c'][f0 // FW], [], [Cbk], q='pool')
                            kb.ld(Sb.rearrange("p a b -> p (a b)"), TAB[L]['s'][f0 // FW], [], [Sbk], q='pool')
                            for fc in range(FW // 128):
                                fch = f0 // 128 + fc
                                kb.mmx([(PSL[0][:, :CW], Cb[:, t_, fc * 128:(fc + 1) * 128], zb[:, t_, :], t_ == 0, t_ == nch - 1) for t_ in range(nch)],
                                       [Cbk, zbk], ['PSL0'])
                                kb.mmx([(PSL[1][:, :CW], Sb[:, t_, fc * 128:(fc + 1) * 128], zb[:, t_, :], t_ == 0, t_ == nch - 1) for t_ in range(nch)],
                                       [Sbk, zbk], ['PSL1'])
                                kb.ld(fr, FSP[L][o, 0, fch * 128:(fch + 1) * 128, cb * CW:(cb + 1) * CW], [('FSP', o, 0, fch, cb)], [frk])
                                kb.ld(fi, FSP[L][o, 1, fch * 128:(fch + 1) * 128, cb * CW:(cb + 1) * CW], [('FSP', o, 1, fch, cb)], [fik])
                                kb.tt(t1, PSL[0][:, :CW], fr, ALU.mult, ['PSL0', frk], [t1k])
                                kb.tt(t2, PSL[1][:, :CW], fi, ALU.mult, ['PSL1', fik], [t2k])
                                kb.tt(Yb[:, fch, 0, :], t1, t2, ALU.subtract, [t1k, t2k], [Ybk])
                                if fch == 0:
                                    kb.cp(Yb[0:1, 0, 0, :], t1[0:1, :], [t1k], [Ybk])
                                    kb.cp(Yb[0:1, 0, 1, :], t2[0:1, :], [t2k], [Ybk])
                                kb.tt(t1, PSL[0][:, :CW], fi, ALU.mult, ['PSL0', fik], [t1k])
                                kb.tt(t2, PSL[1][:, :CW], fr, ALU.mult, ['PSL1', frk], [t2k])
                                if fch == 0:
                                    kb.tt(Yb[1:128, fch, 1, :], t1[1:128, :], t2[1:128, :], ALU.add, [t1k, t2k], [Ybk]) if False else None
                                    kb.tt(t1, t1, t2, ALU.add, [t1k, t2k], [t1k])
                                    kb.cp(t1[0:1, :], Yb[0:1, 0, 1, :], [Ybk], [t1k])
                                    kb.cp(Yb[:, fch, 1, :], t1, [t1k], [Ybk])
                                else:
                                    kb.tt(Yb[:, fch, 1, :], t1, t2, ALU.add, [t1k, t2k], [Ybk])
                        for t0 in range(0, L, FW):
                            (Cb, Cbk), (Sb, Sbk) = CBS[tbi['i'] % 2], SBS[tbi['i'] % 2]; tbi['i'] += 1
                            kb.ld(Cb.rearrange("p a b -> p (a b)"), TAB[L]['c'][t0 // FW], [], [Cbk], q='pool')
                            kb.ld(Sb.rearrange("p a b -> p (a b)"), TAB[L]['st'][t0 // FW], [], [Sbk], q='pool')
                            for tc in range(FW // 128):
                                tch = t0 // 128 + tc
                                c = c0 + tch
                                items = []
                                for f_ in range(nch):
                                    items.append((PSL[0][:, :CW], Cb[:, f_, tc * 128:(tc + 1) * 128], Yb[:, f_, 0, :], f_ == 0, False))
                                    items.append((PSL[0][:, :CW], Sb[:, f_, tc * 128:(tc + 1) * 128], Yb[:, f_, 1, :], False, f_ == nch - 1))
                                kb.mmx(items, [Cbk, Sbk, Ybk], ['PSL0'])
                                kb.ld(zf, zsrc[c * 128:(c + 1) * 128, zcol + cb * CW:zcol + (cb + 1) * CW], [(zname, c)], [zfk])
                                kb.ld(gt, XG[c * 128:(c + 1) * 128, gcol + cb * CW:gcol + (cb + 1) * CW], [('XG', c)], [gtk])
                                kb.tt(zf, zf, brow, ALU.mult, [zfk, browk], [zfk])
                                kb.tt(zf, zf, PSL[0][:, :CW], ALU.add, [zfk, 'PSL0'], [zfk])
                                kb.tt(zf, zf, gt, ALU.mult, [zfk, gtk], [zfk])
                                kb.ld(zdst[c * 128:(c + 1) * 128, cb * CW:(cb + 1) * CW], zf, [zfk], [(zdname, c)])
            P.barrier(); AR.reset()
        linear(make_prep_load(Z2, 'Z2', D, False), D, [(I['hy_w_out'][li], 0, D, MIXO, 'MIXO', 0)], None)

    def stage_final(resid_from):
        nw, nk = AR.tile([128, D]); bcast_row(nw, nk, I['final_norm_w'])
        grow = []
        for cd in range(2):
            g, gk = AR.tile([128, D])
            if resid_from is not None:
                gl, gidx = resid_from
                bcast_row(g, gk, MODD[gl * 2 + cd, gidx * D:(gidx + 1) * D], ['MODD'])
            grow.append((g, gk))
        xts = [AR.tile([128, D]) for _ in range(2)]
        mts = [AR.tile([128, D]) for _ in range(2)]
        sts = [AR.tile([128, 4]) for _ in range(2)]
        for c in range(NCH):
            i = c % 2
            cd = kb.cond_of[c]
            xt, xk = xts[i]; mt, mk = mts[i]; s4, s4k = sts[i]
            kb.ld(xt, X[c * 128:(c + 1) * 128, :], [('X', c)], [xk])
            if resid_from is not None:
                kb.ld(mt, MIXO[c * 128:(c + 1) * 128, :], [('MIXO', c)], [mk])
                g, gk = grow[cd]
                kb.tt(mt, mt, g, ALU.mult, [mk, gk], [mk])
                kb.tt(xt, xt, mt, ALU.add, [xk, mk], [xk])
            kb.tt(mt, xt, xt, ALU.mult, [xk], [mk])
            P.op('dve', lambda e, s4=s4, mt=mt: e.reduce_sum(out=s4[:, 0:1], in_=mt, axis=AX.X), [mk], [s4k])
            kb.ts(s4[:, 1:2], s4[:, 0:1], 1.0 / D, ALU.mult, [s4k], [s4k], s2=EPS, op1=ALU.add)
            kb.act(s4[:, 2:3], s4[:, 1:2], AF.Sqrt, [s4k], [s4k])
            P.op('dve', lambda e, s4=s4: e.reciprocal(out=s4[:, 3:4], in_=s4[:, 2:3]), [s4k], [s4k])
            kb.stt(mt, xt, s4[:, 3:4], nw, ALU.mult, ALU.mult, [xk, s4k, nk], [mk])
            r0 = c * 128
            if r0 < NP * 256:
                dst = O['yp'][r0:r0 + 128, :]
            else:
                dst = O['ys'][r0 - NP * 256:r0 - NP * 256 + 128, :]
            kb.ld(dst, mt, [mk], [('OUT', c)])
        P.barrier(); AR.reset()

    kb._stuff = dict(stage_final=stage_final, stage_ab=stage_ab, stage_hy=stage_hy, I=I, O=O, X=X, MODD=MODD, XBC=XBC, QK=QK, YF=YF, YG=YG, MIXO=MIXO, XG=XG, Z1=Z1, Z2=Z2,
                     FILT=FILT, RN=RN, FSP=FSP, ZERO=ZERO, AR=AR, IDf=IDf, IDb=IDb, ULE=ULE, LGE=LGE, ONES=ONES, MASK=MASK,
                     DIFF=DIFF, POSC=POSC, ALTb=ALTb, PSL=PSL, PST=PST, PSA=PSA, PSS=PSS, PSY=PSY, PSI=PSI,
                     linear=linear, make_prep_norm=make_prep_norm, make_prep_load=make_prep_load, conv3=conv3,
                     bcast_row=bcast_row, alt_eng=alt_eng, stage_mod=stage_mod, stage_ffn=stage_ffn, LENS=LENS)
    return kb


def assemble(kb, mixers=True):
    S = kb._stuff
    if kb.dbg_level >= 1:
        S['stage_mod']()
    pending = None
    for l in range(kb.DEPTH):
        if mixers:
            if l % 2 == 0:
                S['stage_ab'](l, pending)
            else:
                S['stage_hy'](l, pending)
            pending = (l, 2)
        S['stage_ffn'](l, pending)
        pending = (l, 5)
    if kb.dbg_level >= 5:
        S['stage_final'](pending)
    kb.P.emit(kb.es)
    kb.es.close()
    return kb.nc


def make_consts(LS):
    K = {}
    j = np.arange(128)[:, None]; i = np.arange(128)[None, :]
    K['k_ident'] = np.eye(128, dtype=np.float32)
    K['k_ule'] = (j <= i).astype(np.float32)
    K['k_lge'] = (j >= i).astype(np.float32)
    K['k_maskf'] = np.where(i >= j, 0.0, -1e30).astype(np.float32)
    K['k_maskb'] = np.where(i <= j, 0.0, -1e30).astype(np.float32)
    K['k_difff'] = np.where(i >= j, (i - j).astype(np.float64), 1e9).astype(np.float32)
    K['k_diffb'] = np.where(j >= i, (j - i).astype(np.float64), 1e9).astype(np.float32)
    p = np.arange(128, dtype=np.float64)
    K['k_posc'] = np.stack([p + 1, 128 - p, 127 - p, p], axis=1).astype(np.float32)
    K['k_alt'] = np.broadcast_to(((-1.0) ** p)[:, None], (128, 128)).astype(np.float32).copy()
    n = 64
    inv = (np.float32(10000.0) ** (-np.arange(n, dtype=np.float32) / np.float32(n))).astype(np.float32)
    t = np.arange(LS)
    rows = (t // 64).astype(np.float32); cols = (t % 64).astype(np.float32)
    angr = (rows[:, None] * inv[None, :]).astype(np.float32); angc = (cols[:, None] * inv[None, :]).astype(np.float32)
    K['k_rope'] = np.stack([np.cos(angr), np.sin(angr), np.cos(angc), np.sin(angc)]).astype(np.float32)
    K['k_delta'] = np.abs(np.linspace(math.log(1e-2) / 1.5, math.log(1e-2) / 0.3, 2048, dtype=np.float32)).astype(np.float32)
    for L in sorted({256, LS}):
        tt = (np.arange(L, dtype=np.float32) / np.float32(L)).astype(np.float32)
        bands = np.linspace(1e-4, 15, 16, dtype=np.float32)
        ang = (np.float32(2 * math.pi) * tt[:, None] * bands[None, :]).astype(np.float32)
        feats = np.concatenate([tt[:, None], np.cos(ang), np.sin(ang)], axis=-1).astype(np.float32)
        K[f'k_featsT{L}'] = np.ascontiguousarray(feats.T)
        K[f'k_negt{L}'] = np.ascontiguousarray((-tt).reshape(L // 128, 128).T)
        N = 2 * L
        tf = (np.arange(L)[:, None] * np.arange(L)[None, :]) % N
        ang2 = 2.0 * np.pi * tf / N
        Cm = np.cos(ang2); Sm = -np.sin(ang2)
        Sm[:, 0] = (-1.0) ** np.arange(L)
        K[f'k_dftc{L}'] = Cm.astype(np.float32)
        K[f'k_dfts{L}'] = Sm.astype(np.float32)
        K[f'k_dftst{L}'] = np.ascontiguousarray(Sm.T).astype(np.float32)
    return K


def core_inputs(inputs, K, core, NP, b, DEPTH=4):
    f = lambda a: np.ascontiguousarray(np.asarray(a, dtype=np.float32))
    m = {}
    m['xp'] = f(inputs['x_prompt'][core * NP:(core + 1) * NP]).reshape(NP * 256, D)
    m['xs'] = f(inputs['x_sample'][b])
    m['cvec'] = f(np.stack([np.asarray(inputs['c_ctx']), np.asarray(inputs['c'])[b]]))
    m['sssd'] = f(inputs['state_ssd'][b]).reshape(2, 2, 2048, 128)
    m['sret'] = f(inputs['state_ret'][b]).reshape(2, 2, 2048, 256)
    for k in ['ada_w', 'ada_b', 'norm1_w', 'norm2_w', 'ab_w_in', 'ab_conv_w', 'ab_conv_b', 'ssd_norm_w', 'ret_norm_w',
              'ab_w_out', 'hy_w_in', 'hy_short_w', 'hy_short_b', 'hy_f_w1', 'hy_f_b1', 'hy_f_w2', 'hy_f_b2', 'hy_f_w3',
              'hy_bias', 'hy_w_out', 'ffn_w_gate', 'ffn_w_up', 'ffn_conv_w', 'ffn_conv_b', 'ffn_w_down', 'final_norm_w', 'ssd_d']:
        m[k] = f(inputs[k])
    m['ssd_dt_bias'] = f(inputs['ssd_dt_bias']).reshape(2, 64)
    m['ssd_a_log'] = f(inputs['ssd_a_log']).reshape(2, 64)
    m['ret_log_gamma'] = f(inputs['ret_log_gamma']).reshape(2, 16)
    NLA = (DEPTH + 1) // 2; NLH = max(DEPTH // 2, 1)
    for k in list(m.keys()):
        if k in ('ada_w', 'ada_b', 'norm1_w', 'norm2_w', 'ffn_w_gate', 'ffn_w_up', 'ffn_conv_w', 'ffn_conv_b', 'ffn_w_down'):
            m[k] = np.ascontiguousarray(m[k][:DEPTH])
        elif k.startswith('ab_') or k.startswith('ssd_') or k.startswith('ret_') or k in ('sssd', 'sret'):
            m[k] = np.ascontiguousarray(m[k][:NLA])
        elif k.startswith('hy_'):
            m[k] = np.ascontiguousarray(m[k][:NLH])
    m.update(K)
    return m


_NC_CACHE = {}


def kernel(**inputs):
    NCORES, NP, LS, DEPTH = 4, 4, 4096, 4
    key = (NP, LS, DEPTH)
    if key not in _NC_CACHE:
        _NC_CACHE[key] = assemble(build(NP, LS, DEPTH))
    nc = _NC_CACHE[key]
    K = make_consts(LS)
    in_maps = [core_inputs(inputs, K, c, NP, c % 2) for c in range(NCORES)]
    res = run_bass_kernel_spmd(nc, in_maps, core_ids=list(range(NCORES)))
    R = res.results
    y_prompt = np.concatenate([R[c]['yp'].reshape(NP, 256, D) for c in range(NCORES)], axis=0)
    y_sample = np.stack([R[0]['ys'], R[1]['ys']], axis=0)
    nssd = np.concatenate([R[c]['nssd'].reshape(NP, 2, 2, 32, 64, 128) for c in range(NCORES)], axis=0)
    nret = np.concatenate([R[c]['nret'].reshape(NP, 2, 2, 8, 256, 256) for c in range(NCORES)], axis=0)
    return (y_prompt.astype(np.float32), y_sample.astype(np.float32), nssd.astype(np.float32), nret.astype(np.float32))
```

```python
import contextlib
import math
import numpy as np
import concourse.bass as bass
import concourse.mybir as mybir
from concourse.bass_utils import run_bass_kernel_spmd

F32 = mybir.dt.float32
BF16 = mybir.dt.bfloat16
AF = mybir.ActivationFunctionType
ALU = mybir.AluOpType
AX = mybir.AxisListType

D = 2048
AB_IN = 13376
DFF = 5632
EPS = 1e-6
NDQ = 8


class Prog:
    def __init__(self, nc):
        self.nc = nc
        self.names = ['pe', 'act', 'dve', 'pool', 'sp']
        self.ops = {e: [] for e in self.names}
        self.cnt = {e: 0 for e in self.names}
        self.dma_i = {'sp': 0, 'pool': 0}
        self.waited = {e: {} for e in self.names}
        self.lastw = {}
        self.readers = {}
        self.semnames = list(self.names) + [f"dq_{q}_{i}" for q in ('sp', 'pool') for i in range(NDQ)]
        self.semval = {s: 0 for s in self.semnames}

    def _need(self, eng, deps):
        out = []
        wd = self.waited[eng]
        for s, v in deps.items():
            if v > wd.get(s, 0):
                wd[s] = v
                out.append((s, v))
        return out

    def _collect(self, r, w):
        deps = {}

        def add(ev):
            if ev is None:
                return
            s, v = ev
            if v > deps.get(s, 0):
                deps[s] = v
        for k in list(r) + list(w):
            add(self.lastw.get(k))
        for k in w:
            for s, v in self.readers.get(k, {}).items():
                add((s, v))
        return deps

    def _commit(self, ev, r, w):
        for k in w:
            self.lastw[k] = ev
            self.readers[k] = {}
        for k in r:
            if k in w:
                continue
            d = self.readers.setdefault(k, {})
            if ev[1] > d.get(ev[0], 0):
                d[ev[0]] = ev[1]

    def op(self, eng, fn, r=(), w=()):
        deps = self._collect(r, w)
        waits = self._need(eng, deps)
        self.cnt[eng] += 1
        ev = (eng, self.cnt[eng])
        self.semval[eng] = self.cnt[eng]
        self.ops[eng].append((waits, fn, (eng, 1)))
        self._commit(ev, r, w)

    def dma(self, q, out, in_, r=(), w=(), **kw):
        deps = self._collect(r, w)
        i = self.dma_i[q]
        self.dma_i[q] += 1
        s = f"dq_{q}_{i % NDQ}"
        prev = 16 * (i // NDQ)
        if prev > 0:
            deps[s] = max(deps.get(s, 0), prev)
        waits = self._need(q, deps)
        ev = (s, prev + 16)
        self.semval[s] = prev + 16

        def fn(e, out=out, in_=in_, kw=kw):
            return e.dma_start(out=out, in_=in_, **kw)
        self.ops[q].append((waits, fn, (s, 16)))
        self._commit(ev, r, w)

    def barrier(self):
        allv = {s: v for s, v in self.semval.items() if v > 0}
        for e in self.names:
            waits = self._need(e, dict(allv))
            if waits:
                self.ops[e].append((waits, None, None))
        self.lastw = {}
        self.readers = {}

    def emit(self, es):
        nc = self.nc
        sems = {s: es.enter_context(nc.semaphore(s)) for s in self.semnames}
        block = es.enter_context(nc.Block())
        engs = {'pe': block.tensor, 'act': block.scalar, 'dve': block.vector, 'pool': block.gpsimd, 'sp': block.sync}
        final = {s: v for s, v in self.semval.items() if v > 0}
        for name in self.names:
            ops = self.ops[name]
            last = (name == 'sp')

            def body(e, ops=ops, last=last):
                for waits, fn, inc in ops:
                    for s, v in waits:
                        e.wait_ge(sems[s], v)
                    if fn is not None:
                        ins = fn(e)
                        ins.then_inc(sems[inc[0]], inc[1])
                if last:
                    for s, v in final.items():
                        e.wait_ge(sems[s], v)
            engs[name](body)


class KB:
    def __init__(self, NP, LS, DEPTH, dbg=None):
        self.NP, self.LS, self.DEPTH = NP, LS, DEPTH
        self.T = NP * 256 + LS
        self.NCH = self.T // 128
        self.seqs = [(2 * p, 2, 0) for p in range(NP)] + [(2 * NP, LS // 128, 1)]
        self.cond_of = []
        for c0, n, cd in self.seqs:
            self.cond_of += [cd] * n
        self.nc = bass.Bass("TRN2", target_bir_lowering=False)
        self.P = Prog(self.nc)
        self.es = contextlib.ExitStack()
        self.inputs = {}
        self.dbg = dbg or {}
        self.dbg_level = 99
        self._uid = 0

    def inp(self, name, shape, dtype=F32):
        ap = self.nc.dram_tensor(name, list(shape), dtype, kind="ExternalInput").ap()
        self.inputs[name] = ap
        return ap

    def outp(self, name, shape, dtype=F32):
        return self.nc.dram_tensor(name, list(shape), dtype, kind="ExternalOutput").ap()

    def scr(self, name, shape, dtype=F32):
        return self.nc.dram_tensor(name, list(shape), dtype, kind="Internal").ap()

    def sb(self, name, shape, dtype=F32):
        return self.es.enter_context(self.nc.sbuf_tensor(name, list(shape), dtype))

    def ps(self, name, shape, dtype=F32):
        return self.es.enter_context(self.nc.psum_tensor(name, list(shape), dtype))

    def act(self, out, in_, func, r, w, bias=None, scale=None, accum_out=None):
        kw = {}
        if bias is not None:
            kw['bias'] = bias
        if scale is not None:
            kw['scale'] = scale
        if accum_out is not None:
            kw['accum_out'] = accum_out
        self.P.op('act', lambda e: e.activation(out=out, in_=in_, func=func, **kw), r, w)

    def tt(self, out, in0, in1, op, r, w, eng='dve'):
        self.P.op(eng, lambda e: e.tensor_tensor(out=out, in0=in0, in1=in1, op=op), r, w)

    def ts(self, out, in0, s1, op0, r, w, s2=None, op1=None, eng='dve'):
        if op1 is None:
            self.P.op(eng, lambda e: e.tensor_scalar(out=out, in0=in0, scalar1=s1, scalar2=None, op0=op0), r, w)
        else:
            self.P.op(eng, lambda e: e.tensor_scalar(out=out, in0=in0, scalar1=s1, scalar2=s2, op0=op0, op1=op1), r, w)

    def stt(self, out, in0, scalar, in1, op0, op1, r, w):
        self.P.op('dve', lambda e: e.scalar_tensor_tensor(out=out, in0=in0, scalar=scalar, in1=in1, op0=op0, op1=op1), r, w)

    def cp(self, out, in_, r, w, eng='dve'):
        if eng == 'act':
            self.P.op('act', lambda e: e.copy(out=out, in_=in_), r, w)
        else:
            self.P.op(eng, lambda e: e.tensor_copy(out=out, in_=in_), r, w)

    def memset(self, out, val, w):
        self.P.op('dve', lambda e: e.memset(out, val), (), w)

    def mm(self, out, pairs, r, w, transpose=False):
        n = len(pairs)

        def fn(e):
            ins = None
            for i, (l, rr) in enumerate(pairs):
                ins = e.matmul(out, lhsT=l, rhs=rr, start=(i == 0), stop=(i == n - 1))
            return ins
        self.P.op('pe', fn, r, w)

    def mmx(self, items, r, w):
        def fn(e):
            ins = None
            for (o, l, rr, st, sp) in items:
                ins = e.matmul(o, lhsT=l, rhs=rr, start=st, stop=sp)
            return ins
        self.P.op('pe', fn, r, w)

    def tr(self, outs_ins, ident, r, w):
        def fn(e):
            ins = None
            for o, i in outs_ins:
                ins = e.transpose(o, i, ident)
            return ins
        self.P.op('pe', fn, r, w)

    def ld(self, out, in_, r, w, q='sp', **kw):
        self.P.dma(q, out, in_, r, w, **kw)


class Arena:
    def __init__(self, kb, sizes):
        self.regs = [kb.sb(f"arena{i}", [128, n], F32) for i, n in enumerate(sizes)]
        self.sizes = sizes
        self.reset()
        self.uid = 0

    def reset(self):
        self.off = [0] * len(self.regs)

    def tile(self, shape, dtype=F32):
        n = 1
        for s in shape[1:]:
            n *= s
        nw = n if dtype == F32 else (n + 1) // 2
        for i, reg in enumerate(self.regs):
            if self.off[i] + nw <= self.sizes[i]:
                ap = reg[:, self.off[i]:self.off[i] + nw]
                self.off[i] += nw
                if dtype != F32:
                    ap = ap.bitcast(dtype)
                    if n != nw * 2:
                        ap = ap[:, :n]
                if len(shape) == 3:
                    ap = ap.rearrange("p (a b) -> p a b", b=shape[2])
                elif len(shape) == 4:
                    ap = ap.rearrange("p (a b c) -> p a b c", b=shape[2], c=shape[3])
                self.uid += 1
                return ap, f"t{self.uid}"
        raise RuntimeError(f"arena full for {shape}")


def build(NP, LS, DEPTH, dbg=False):
    kb = KB(NP, LS, DEPTH)
    nc, P = kb.nc, kb.P
    T, NCH = kb.T, kb.NCH
    NLA = (DEPTH + 1) // 2
    NLC = DEPTH // 2
    LENS = sorted({256, LS})

    I = {}
    def inp(name, shape):
        I[name] = kb.inp(name, shape)
    inp('xp', [NP * 256, D]); inp('xs', [LS, D]); inp('cvec', [2, D])
    inp('sssd', [NLA, 2, 2048, 128]); inp('sret', [NLA, 2, 2048, 256])
    inp('ada_w', [DEPTH, D, 6 * D]); inp('ada_b', [DEPTH, 6 * D]); inp('norm1_w', [DEPTH, D]); inp('norm2_w', [DEPTH, D])
    inp('ab_w_in', [NLA, D, AB_IN]); inp('ab_conv_w', [NLA, 3, 3072]); inp('ab_conv_b', [NLA, 3072])
    inp('ssd_dt_bias', [NLA, 64]); inp('ssd_a_log', [NLA, 64]); inp('ssd_d', [NLA, 32]); inp('ssd_norm_w', [NLA, D])
    inp('ret_log_gamma', [NLA, 16]); inp('ret_norm_w', [NLA, D]); inp('ab_w_out', [NLA, 2 * D, D])
    NLH = max(NLC, 1)
    inp('hy_w_in', [NLH, D, 3 * D]); inp('hy_short_w', [NLH, 3, 3 * D]); inp('hy_short_b', [NLH, 3 * D])
    inp('hy_f_w1', [NLH, 33, 64]); inp('hy_f_b1', [NLH, 64]); inp('hy_f_w2', [NLH, 64, 64]); inp('hy_f_b2', [NLH, 64])
    inp('hy_f_w3', [NLH, 64, 4 * D]); inp('hy_bias', [NLH, 2, D]); inp('hy_w_out', [NLH, D, D])
    inp('ffn_w_gate', [DEPTH, D, DFF]); inp('ffn_w_up', [DEPTH, D, DFF]); inp('ffn_conv_w', [DEPTH, 3, DFF])
    inp('ffn_conv_b', [DEPTH, DFF]); inp('ffn_w_down', [DEPTH, DFF, D]); inp('final_norm_w', [D])
    inp('k_ident', [128, 128]); inp('k_ule', [128, 128]); inp('k_lge', [128, 128])
    inp('k_maskf', [128, 128]); inp('k_maskb', [128, 128]); inp('k_difff', [128, 128]); inp('k_diffb', [128, 128])
    inp('k_posc', [128, 4]); inp('k_alt', [128, 128])
    inp('k_rope', [4, LS, 64]); inp('k_delta', [D])
    for L in LENS:
        inp(f'k_featsT{L}', [33, L]); inp(f'k_negt{L}', [128, L // 128])
        inp(f'k_dftc{L}', [L, L]); inp(f'k_dfts{L}', [L, L]); inp(f'k_dftst{L}', [L, L])

    O = {}
    O['yp'] = kb.outp('yp', [NP * 256, D]); O['ys'] = kb.outp('ys', [LS, D])
    O['nssd'] = kb.outp('nssd', [NP, NLA, 2, 2048, 128]); O['nret'] = kb.outp('nret', [NP, NLA, 2, 2048, 256])

    X = kb.scr('X', [T, D]); MODD = kb.scr('MODD', [8, 6 * D])
    PROJA = kb.scr('PROJA', [T, 5184]); PROJB = kb.scr('PROJB', [T, 8192]); XBC = kb.scr('XBC', [T, 3072]); QK = kb.scr('QK', [T, 2 * D], BF16)
    YF = kb.scr('YF', [T, 2 * D]); YG = kb.scr('YG', [T, 2 * D], BF16); MIXO = kb.scr('MIXO', [T, D])
    FA = kb.scr('FA', [T, DFF]); FU = kb.scr('FU', [T, DFF]); FH = kb.scr('FH', [T, DFF], BF16)
    XG = kb.scr('XG', [T, 3 * D]); Z1 = kb.scr('Z1', [T, D]); Z2 = kb.scr('Z2', [T, D])
    FILT = {L: kb.scr(f'FILT{L}', [L, 4 * D]) for L in LENS}
    RN = {L: kb.scr(f'RN{L}', [4 * D]) for L in LENS}
    FSP = {L: kb.scr(f'FSP{L}', [2, 2, L, D]) for L in LENS}
    ZERO = kb.scr('ZERO', [1, 6144])
    FWT = 256
    TAB = {L: {nm: kb.scr(f'TAB{nm}{L}', [L // FWT, 128, (L // 128) * FWT], BF16) for nm in ('c', 's', 'st')} for L in LENS}

    IDf = kb.sb('IDf', [128, 128]); IDb = kb.sb('IDb', [128, 128], BF16)
    ULE = kb.sb('ULE', [128, 128]); LGE = kb.sb('LGE', [128, 128]); ONES = kb.sb('ONES', [128, 128])
    MASK = [kb.sb('MASKF', [128, 128]), kb.sb('MASKB', [128, 128])]
    DIFF = [kb.sb('DIFFF', [128, 128]), kb.sb('DIFFB', [128, 128])]
    POSC = kb.sb('POSC', [128, 4]); ALTb = kb.sb('ALTb', [128, 128], BF16)
    AR = Arena(kb, [16384, 16384, 12288, 3072])
    PSL = [kb.ps('PSL0', [128, 512]), kb.ps('PSL1', [128, 512])]
    PST = kb.ps('PST', [128, 1024], BF16)
    PSA = [kb.ps('PSA0', [128, 512]), kb.ps('PSA1', [128, 512])]
    PSS = kb.ps('PSS', [128, 512]); PSY = kb.ps('PSY', [128, 512]); PSI = kb.ps('PSI', [128, 512])

    def ldc(dst, key, src, cast=False):
        kb.ld(dst, src, [], [key], q='pool' if cast else 'sp')
    ldc(IDf[:], 'IDf', I['k_ident']); ldc(IDb[:], 'IDb', I['k_ident'], True)
    ldc(ULE[:], 'ULE', I['k_ule']); ldc(LGE[:], 'LGE', I['k_lge'])
    ldc(MASK[0][:], 'MASKF', I['k_maskf']); ldc(MASK[1][:], 'MASKB', I['k_maskb'])
    ldc(DIFF[0][:], 'DIFFF', I['k_difff']); ldc(DIFF[1][:], 'DIFFB', I['k_diffb'])
    ldc(POSC[:], 'POSC', I['k_posc']); ldc(ALTb[:], 'ALTb', I['k_alt'], True)
    kb.memset(ONES[:], 1.0, ['ONES'])
    zt, zk = AR.tile([128, 6144])
    kb.memset(zt[0:1, :], 0.0, [zk])
    kb.ld(ZERO[:, :], zt[0:1, :], [zk], ['ZERO'])
    for c in range(NCH):
        r0 = c * 128
        if r0 < NP * 256:
            src = I['xp'][r0:r0 + 128, :]
        else:
            src = I['xs'][r0 - NP * 256:r0 - NP * 256 + 128, :]
        kb.ld(X[r0:r0 + 128, :], src, [], [('X', c)])
    P.barrier(); AR.reset()
    tbufs = [AR.tile([128, 2048], BF16) for _ in range(4)]
    ti = 0
    for L in LENS:
        for nm in ('c', 's', 'st'):
            src = I[f'k_dft{nm}{L}']
            dstv = TAB[L][nm].rearrange("fb p (tc fw) -> p fb tc fw", fw=FWT)
            for tc in range(L // 128):
                for h0 in range(0, L, 2048):
                    w_ = min(2048, L - h0)
                    tb_, tbk_ = tbufs[ti % 4]; ti += 1
                    kb.ld(tb_[:, :w_], src[tc * 128:(tc + 1) * 128, h0:h0 + w_], [], [tbk_], q='pool')
                    kb.ld(dstv[:, h0 // FWT:(h0 + w_) // FWT, tc, :], tb_[:, :w_].rearrange("p (fb fw) -> p fb fw", fw=FWT), [tbk_], ['TAB'])
    P.barrier(); AR.reset()

    rr = {'i': 0}
    def alt_eng():
        rr['i'] += 1
        return 'act' if rr['i'] % 2 else 'dve'

    def bcast_row(dst, key, vec_ap, r=()):
        kb.ld(dst, vec_ap.partition_broadcast(128), list(r), [key])

    def stage_mod():
        cT, ck = AR.tile([128, 2, 16]); sT, sk = AR.tile([128, 2, 16])
        lhs, lk = AR.tile([128, 2, 16, 128], BF16)
        for cd in range(2):
            kb.ld(cT[:, cd, :], I['cvec'][cd, :].rearrange("(kc p) -> p kc", p=128), [], [ck], allow_slow_non_contiguous=True)
        kb.act(sT, cT, AF.Silu, [ck], [sk])
        for cd in range(2):
            kb.cp(lhs[:, cd, :, :], sT[:, cd, :].unsqueeze(2).to_broadcast([128, 16, 128]), [sk], [lk])
        wbs = [AR.tile([128, 16, 512], BF16) for _ in range(2)]
        abs_ = [AR.tile([128, 512]) for _ in range(2)]
        mos = [AR.tile([128, 512]) for _ in range(2)]
        n = 0
        for l in range(DEPTH):
            wv = I['ada_w'][l].rearrange("(kc p) f -> p kc f", p=128)
            for blk in range(24):
                wb, wk = wbs[n % 2]; ab, ak = abs_[n % 2]
                kb.ld(wb, wv[:, :, blk * 512:(blk + 1) * 512], [], [wk], q='pool')
                bcast_row(ab, ak, I['ada_b'][l, blk * 512:(blk + 1) * 512])
                for cd in range(2):
                    ps = PSL[cd]
                    kb.mm(ps[:, :], [(lhs[:, cd, kc, :], wb[:, kc, :]) for kc in range(16)], [lk, wk], [f'PSL{cd}'])
                    mo, mk = mos[cd]
                    kb.tt(mo, ps[:, :], ab, ALU.add, [f'PSL{cd}', ak], [mk])
                    kb.ld(MODD[l * 2 + cd:l * 2 + cd + 1, blk * 512:(blk + 1) * 512], mo[0:1, :], [mk], ['MODD'])
                n += 1
        P.barrier(); AR.reset()

    def linear(prep_setup, Fin, outs, rkeys_of):
        kch = Fin // 128
        GC = {16: 12, 32: 6, 44: 4}[kch]
        BLK = 512 if kch <= 32 else 256
        HT, hk = AR.tile([128, kch, GC * 128], BF16)
        wbs = [AR.tile([128, kch, BLK], BF16) for _ in range(2)]
        ots = [AR.tile([128, 512]) for _ in range(2)]
        prep = prep_setup()
        chunks = list(range(NCH))
        wi = 0; oi = 0
        for g0 in range(0, NCH, GC):
            grp = chunks[g0:g0 + GC]
            for gi, c in enumerate(grp):
                src, skey = prep(c)
                for k0 in range(0, kch, 8):
                    nb = min(8, kch - k0)
                    kb.tr([(PST[:, j * 128:(j + 1) * 128], src[:, (k0 + j) * 128:(k0 + j + 1) * 128]) for j in range(nb)],
                          IDb[:], [skey, 'IDb'], ['PST'])
                    kb.cp(HT[:, k0:k0 + nb, gi * 128:(gi + 1) * 128],
                          PST[:, :nb * 128].rearrange("p (a b) -> p a b", b=128), ['PST'], [(hk, gi)], eng=alt_eng())
            for (W, c_lo, c_hi, dst, dname, dcol) in outs:
                wv = W.rearrange("(kc p) f -> p kc f", p=128)
                for b0 in range(c_lo, c_hi, BLK):
                    ncol = min(BLK, c_hi - b0)
                    wb, wk = wbs[wi % 2]; wi += 1
                    kb.ld(wb[:, :, :ncol], wv[:, :, b0:b0 + ncol], [], [wk], q='pool')
                    for gi, c in enumerate(grp):
                        ps = PSL[oi % 2]; pk = f'PSL{oi % 2}'
                        ot, ok = ots[oi % 2]; oi += 1
                        kb.mm(ps[:, :ncol], [(HT[:, kc, gi * 128:(gi + 1) * 128], wb[:, kc, :ncol]) for kc in range(kch)],
                              [(hk, gi), wk], [pk])
                        kb.cp(ot[:, :ncol], ps[:, :ncol], [pk], [ok], eng=alt_eng())
                        kb.ld(dst[c * 128:(c + 1) * 128, dcol + b0 - c_lo:dcol + b0 - c_lo + ncol], ot[:, :ncol], [ok], [(dname, c)])
        P.barrier(); AR.reset()

    def make_prep_norm(l, which, normw_ap, resid_from=None):
        def setup():
            arow = []; srow = []; grow = []
            nw, nk = AR.tile([128, D])
            bcast_row(nw, nk, normw_ap)
            for cd in range(2):
                a, ak = AR.tile([128, D]); s, sk = AR.tile([128, D])
                bcast_row(s, sk, MODD[l * 2 + cd, (3 * which) * D:(3 * which + 1) * D], ['MODD'])
                bcast_row(a, ak, MODD[l * 2 + cd, (3 * which + 1) * D:(3 * which + 2) * D], ['MODD'])
                kb.stt(a, a, 1.0, nw, ALU.add, ALU.mult, [ak, nk], [ak])
                arow.append((a, ak)); srow.append((s, sk))
                if resid_from is not None:
                    gl, gidx = resid_from
                    g, gk = AR.tile([128, D])
                    bcast_row(g, gk, MODD[gl * 2 + cd, gidx * D:(gidx + 1) * D], ['MODD'])
                    grow.append((g, gk))
            xts = [AR.tile([128, D]) for _ in range(2)]
            mts = [AR.tile([128, D]) for _ in range(2)]
            hbs = [AR.tile([128, D], BF16) for _ in range(2)]
            sts = [AR.tile([128, 4]) for _ in range(2)]
            st = {'i': 0}

            def prep(c):
                i = st['i'] % 2; st['i'] += 1
                cd = kb.cond_of[c]
                xt, xk = xts[i]; mt, mk = mts[i]; hb, hk2 = hbs[i]; s4, s4k = sts[i]
                kb.ld(xt, X[c * 128:(c + 1) * 128, :], [('X', c)], [xk])
                if resid_from is not None:
                    kb.ld(mt, MIXO[c * 128:(c + 1) * 128, :], [('MIXO', c)], [mk])
                    g, gk = grow[cd]
                    kb.tt(mt, mt, g, ALU.mult, [mk, gk], [mk])
                    kb.tt(xt, xt, mt, ALU.add, [xk, mk], [xk])
                    kb.ld(X[c * 128:(c + 1) * 128, :], xt, [xk], [('X', c)])
                kb.tt(mt, xt, xt, ALU.mult, [xk], [mk])
                P.op('dve', lambda e, s4=s4, mt=mt: e.reduce_sum(out=s4[:, 0:1], in_=mt, axis=AX.X), [mk], [s4k])
                kb.ts(s4[:, 1:2], s4[:, 0:1], 1.0 / D, ALU.mult, [s4k], [s4k], s2=EPS, op1=ALU.add)
                kb.act(s4[:, 2:3], s4[:, 1:2], AF.Sqrt, [s4k], [s4k])
                P.op('dve', lambda e, s4=s4: e.reciprocal(out=s4[:, 3:4], in_=s4[:, 2:3]), [s4k], [s4k])
                a, ak = arow[cd]; s, sk = srow[cd]
                kb.stt(mt, xt, s4[:, 3:4], a, ALU.mult, ALU.mult, [xk, s4k, ak], [mk])
                kb.tt(hb, mt, s, ALU.add, [mk, sk], [hk2])
                return hb, hk2
            return prep
        return setup

    def make_prep_load(src, sname, Fin, is_bf16):
        def setup():
            if is_bf16:
                hbs = [AR.tile([128, Fin], BF16) for _ in range(2)]
            else:
                fts = [AR.tile([128, Fin]) for _ in range(2)]
                hbs = [AR.tile([128, Fin], BF16) for _ in range(2)]
            st = {'i': 0}

            def prep(c):
                i = st['i'] % 2; st['i'] += 1
                hb, hk2 = hbs[i]
                if is_bf16:
                    kb.ld(hb, src[c * 128:(c + 1) * 128, :], [(sname, c)], [hk2])
                else:
                    ft, fk = fts[i]
                    kb.ld(ft, src[c * 128:(c + 1) * 128, :], [(sname, c)], [fk])
                    kb.cp(hb, ft, [fk], [hk2], eng=alt_eng())
                return hb, hk2
            return prep
        return setup

    def conv3(src, sname, scol, F, w_ap, b_ap, post_setup, CB=1024):
        for col0 in range(0, F, CB):
            ncol = min(CB, F - col0)
            wts = []
            for k in range(3):
                wt, wk = AR.tile([128, CB]); bcast_row(wt[:, :ncol], wk, w_ap[k, col0:col0 + ncol]); wts.append((wt, wk))
            bt, bk = AR.tile([128, CB]); bcast_row(bt[:, :ncol], bk, b_ap[col0:col0 + ncol])
            bufs = [[AR.tile([128, CB]) for _ in range(3)] for _ in range(2)]
            accs = [AR.tile([128, CB]) for _ in range(2)]
            post = post_setup()
            n = 0
            for (c0, nchk, cd) in kb.seqs:
                for c in range(c0, c0 + nchk):
                    (pv, pk), (cu, ck), (nx, nk) = bufs[n % 2]; acc, akey = accs[n % 2]; n += 1
                    r0 = c * 128
                    cs = slice(scol + col0, scol + col0 + ncol)
                    kb.ld(cu[:, :ncol], src[r0:r0 + 128, cs], [(sname, c)], [ck])
                    if c == c0:
                        kb.ld(pv[0:1, :ncol], ZERO[:, :ncol], ['ZERO'], [pk])
                        kb.ld(pv[1:128, :ncol], src[r0:r0 + 127, cs], [(sname, c)], [pk])
                    else:
                        kb.ld(pv[:, :ncol], src[r0 - 1:r0 + 127, cs], [(sname, c), (sname, c - 1)], [pk])
                    if c == c0 + nchk - 1:
                        kb.ld(nx[127:128, :ncol], ZERO[:, :ncol], ['ZERO'], [nk])
                        kb.ld(nx[0:127, :ncol], src[r0 + 1:r0 + 128, cs], [(sname, c)], [nk])
                    else:
                        kb.ld(nx[:, :ncol], src[r0 + 1:r0 + 129, cs], [(sname, c), (sname, c + 1)], [nk])
                    kb.tt(acc[:, :ncol], cu[:, :ncol], wts[1][0][:, :ncol], ALU.mult, [ck, wts[1][1]], [akey])
                    kb.tt(pv[:, :ncol], pv[:, :ncol], wts[0][0][:, :ncol], ALU.mult, [pk, wts[0][1]], [pk])
                    kb.tt(nx[:, :ncol], nx[:, :ncol], wts[2][0][:, :ncol], ALU.mult, [nk, wts[2][1]], [nk])
                    kb.tt(acc[:, :ncol], acc[:, :ncol], pv[:, :ncol], ALU.add, [akey, pk], [akey])
                    kb.tt(acc[:, :ncol], acc[:, :ncol], nx[:, :ncol], ALU.add, [akey, nk], [akey])
                    kb.tt(acc[:, :ncol], acc[:, :ncol], bt[:, :ncol], ALU.add, [akey, bk], [akey])
                    post(c, col0, ncol, acc, akey)
            P.barrier(); AR.reset()

    def stage_ffn(l, resid_from):
        if kb.dbg_level < 2:
            return
        linear(make_prep_norm(l, 1, I['norm2_w'][l, :], resid_from), D,
               [(I['ffn_w_gate'][l], 0, DFF, FA, 'FA', 0), (I['ffn_w_up'][l], 0, DFF, FU, 'FU', 0)], None)
        def post_setup():
            us = [AR.tile([128, 1024]) for _ in range(2)]
            tsx = [AR.tile([128, 1024]) for _ in range(2)]
            hs = [AR.tile([128, 1024], BF16) for _ in range(2)]
            st = {'i': 0}

            def post(c, col0, ncol, acc, akey):
                i = st['i'] % 2; st['i'] += 1
                u, uk = us[i]; t, tk = tsx[i]; h, hk = hs[i]
                kb.ld(u[:, :ncol], FU[c * 128:(c + 1) * 128, col0:col0 + ncol], [('FU', c)], [uk])
                a = acc[:, :ncol]; tt_ = t[:, :ncol]
                kb.tt(tt_, a, a, ALU.mult, [akey], [tk])
                kb.ts(tt_, tt_, 0.044715, ALU.mult, [tk], [tk], s2=1.0, op1=ALU.add)
                kb.tt(tt_, tt_, a, ALU.mult, [tk, akey], [tk])
                kb.act(tt_, tt_, AF.Sigmoid, [tk], [tk], scale=1.5957691216057308)
                kb.tt(tt_, tt_, a, ALU.mult, [tk, akey], [tk])
                kb.tt(h[:, :ncol], tt_, u[:, :ncol], ALU.mult, [tk, uk], [hk])
                kb.ld(FH[c * 128:(c + 1) * 128, col0:col0 + ncol], h[:, :ncol], [hk], [('FH', c)], q='pool')
            return post
        if kb.dbg_level < 3:
            return
        conv3(FA, 'FA', 0, DFF, I['ffn_conv_w'][l], I['ffn_conv_b'][l], post_setup)
        if kb.dbg_level < 4:
            return
        linear(make_prep_load(FH, 'FH', DFF, True), DFF, [(I['ffn_w_down'][l], 0, D, MIXO, 'MIXO', 0)], None)


    def stage_ab(l, resid_from):
        li = l // 2
        linear(make_prep_norm(l, 0, I['norm1_w'][l, :], resid_from), D, [(I['ab_w_in'][li], 0, 5184, PROJA, 'PROJA', 0), (I['ab_w_in'][li], 5184, AB_IN, PROJB, 'PROJB', 0)], None)

        def post_setup():
            os_ = [AR.tile([128, 1024]) for _ in range(2)]
            st = {'i': 0}

            def post(c, col0, ncol, acc, akey):
                o, ok = os_[st['i'] % 2]; st['i'] += 1
                kb.act(o[:, :ncol], acc[:, :ncol], AF.Silu, [akey], [ok])
                kb.ld(XBC[c * 128:(c + 1) * 128, col0:col0 + ncol], o[:, :ncol], [ok], [('XBC', c)], q='pool')
            return post
        conv3(PROJA, 'PROJA', 2048, 3072, I['ab_conv_w'][li], I['ab_conv_b'][li], post_setup)

        qk, qkk = AR.tile([128, 4096]); ob, obk = AR.tile([128, 4096], BF16)
        ta, tak = AR.tile([128, 16, 64]); tb, tbk = AR.tile([128, 16, 64])
        rt = [AR.tile([128, 64]) for _ in range(4)]
        for (c0, nchk, cd) in kb.seqs:
            for c in range(c0, c0 + nchk):
                kb.ld(qk, PROJB[c * 128:(c + 1) * 128, 0:4096], [('PROJB', c)], [qkk])
                if cd == 0:
                    kb.cp(ob, qk, [qkk], [obk])
                else:
                    p0 = (c - c0) * 128
                    for j in range(4):
                        kb.ld(rt[j][0], I['k_rope'][j, p0:p0 + 128, :], [], [rt[j][1]])
                    qv = qk.rearrange("p (h q f) -> p h q f", q=4, f=64)
                    ov = ob.rearrange("p (h q f) -> p h q f", q=4, f=64)
                    for half in range(2):
                        cs_, csk = rt[2 * half]; sn, snk = rt[2 * half + 1]
                        cb = cs_.unsqueeze(1).to_broadcast([128, 16, 64]); sbb = sn.unsqueeze(1).to_broadcast([128, 16, 64])
                        t1 = qv[:, :, 2 * half, :]; t2 = qv[:, :, 2 * half + 1, :]
                        kb.tt(ta, t1, cb, ALU.mult, [qkk, csk], [tak])
                        kb.tt(tb, t2, sbb, ALU.mult, [qkk, snk], [tbk])
                        kb.tt(ov[:, :, 2 * half, :], ta, tb, ALU.subtract, [tak, tbk], [obk])
                        kb.tt(ta, t1, sbb, ALU.mult, [qkk, snk], [tak])
                        kb.tt(tb, t2, cb, ALU.mult, [qkk, csk], [tbk])
                        kb.tt(ov[:, :, 2 * half + 1, :], ta, tb, ALU.add, [tak, tbk], [obk])
                kb.ts(ob[:, 2048:], ob[:, 2048:], 0.0625, ALU.mult, [obk], [obk])
                kb.ld(QK[c * 128:(c + 1) * 128, :], ob, [obk], [('QK', c)], q='pool')
        P.barrier(); AR.reset()

        T_ = AR.tile
        arow, ark = T_([128, 64]); dtb, dtbk = T_([128, 64]); Drow, Dk = T_([128, 32]); lgb, lgk = T_([128, 16])
        bcast_row(arow, ark, I['ssd_a_log'][li, :]); bcast_row(dtb, dtbk, I['ssd_dt_bias'][li, :])
        bcast_row(Drow, Dk, I['ssd_d'][li, :]); bcast_row(lgb, lgk, I['ret_log_gamma'][li, :])
        kb.act(arow, arow, AF.Exp, [ark], [ark])
        kb.ts(arow, arow, -1.0, ALU.mult, [ark], [ark])
        DM = [T_([128, 8, 128]) for _ in range(2)]
        GP = [T_([128, 8]) for _ in range(2)]; TE = [T_([128, 8]) for _ in range(2)]; CDr = [T_([128, 8]) for _ in range(2)]
        for d in range(2):
            for h in range(8):
                kb.act(DM[d][0][:, h, :], DIFF[d][:], AF.Exp, ['DIFFF' if d == 0 else 'DIFFB', lgk], [DM[d][1]], scale=lgb[:, d * 8 + h:d * 8 + h + 1])
            kb.act(GP[d][0], lgb[:, d * 8:d * 8 + 8], AF.Exp, [lgk, 'POSC'], [GP[d][1]], scale=POSC[:, d:d + 1])
            kb.act(TE[d][0], lgb[:, d * 8:d * 8 + 8], AF.Exp, [lgk, 'POSC'], [TE[d][1]], scale=POSC[:, 2 + d:3 + d])
            kb.act(CDr[d][0], lgb[:, d * 8:d * 8 + 8], AF.Exp, [lgk], [CDr[d][1]], scale=128.0)
        nws, nwsk = T_([128, D]); nwr, nwrk = T_([128, D])
        bcast_row(nws, nwsk, I['ssd_norm_w'][li, :]); bcast_row(nwr, nwrk, I['ret_norm_w'][li, :])
        HS, HSk = T_([128, 4, 512]); HSb, HSbk = T_([128, 4, 512], BF16)
        HR, HRk = T_([128, 8, 2, 256]); HRb, HRbk = T_([128, 8, 2, 256], BF16)
        xbc, xbck = T_([128, 3072]); bcb, bcbk = T_([128, 1024], BF16)
        vd, vdk = T_([128, 32, 64], BF16); vte, vtek = T_([128, 32, 64], BF16)
        sm, smk = T_([128, 12, 32])
        BCT, BCTk = T_([128, 8, 128], BF16)
        seg, segk = T_([128, 4, 128]); MT, MTk = T_([128, 4, 128], BF16)
        qkb, qkbk = T_([128, 4096], BF16); QKT, QKTk = T_([128, 32, 128], BF16)
        kte, ktek = T_([128, 2048], BF16)
        vt, vtk = T_([128, D]); vb, vbk = T_([128, D], BF16)
        ych, ychk = T_([128, 4096]); yfl, yflk = T_([128, 4096])
        tmp, tmpk = T_([128, 512])
        yg, ygk = T_([128, 4096], BF16)
        st8, st8k = T_([128, 4, 8])
        blk, blkk = T_([128, 256])
        DTR, X0, AB_, EX, L1, DT, LA, ACM, TOT, EAC, TEv, CDv = range(12)

        def chunk(c, d, sample):
            r0 = c * 128
            tri = ULE if d == 0 else LGE
            trik = 'ULE' if d == 0 else 'LGE'
            kb.ld(xbc, XBC[r0:r0 + 128, :], [('XBC', c)], [xbck])
            kb.ld(sm[:, DTR, :], PROJA[r0:r0 + 128, 5120 + d * 32:5152 + d * 32], [('PROJA', c)], [smk])
            kb.ld(qkb, QK[r0:r0 + 128, :], [('QK', c)], [qkbk])
            kb.ld(vt, PROJB[r0:r0 + 128, 4096:6144], [('PROJB', c)], [vtk])
            kb.cp(bcb, xbc[:, 2048:3072], [xbck], [bcbk])
            kb.cp(vb, vt, [vtk], [vbk], eng='act')
            kb.tt(sm[:, X0, :], sm[:, DTR, :], dtb[:, d * 32:d * 32 + 32], ALU.add, [smk, dtbk], [smk])
            kb.ts(sm[:, AB_, :], sm[:, X0, :], -1.0, ALU.mult, [smk], [smk])
            kb.tt(sm[:, AB_, :], sm[:, AB_, :], sm[:, X0, :], ALU.max, [smk], [smk])
            kb.act(sm[:, EX, :], sm[:, AB_, :], AF.Exp, [smk], [smk], scale=-1.0)
            kb.ts(sm[:, EX, :], sm[:, EX, :], 1.0, ALU.add, [smk], [smk])
            kb.act(sm[:, L1, :], sm[:, EX, :], AF.Ln, [smk], [smk])
            kb.ts(sm[:, X0, :], sm[:, X0, :], 0.0, ALU.max, [smk], [smk])
            kb.tt(sm[:, DT, :], sm[:, X0, :], sm[:, L1, :], ALU.add, [smk], [smk])
            kb.tt(sm[:, LA, :], sm[:, DT, :], arow[:, d * 32:d * 32 + 32], ALU.mult, [smk, ark], [smk])
            kb.mmx([(PSI[:, 0:32], tri[:], sm[:, LA, :], True, True), (PSI[:, 32:64], ONES[:], sm[:, LA, :], True, True)],
                   [trik, 'ONES', smk], ['PSI'])
            kb.cp(sm[:, ACM:ACM + 2, :], PSI[:, 0:64].rearrange("p (a b) -> p a b", b=32), ['PSI'], [smk])
            kb.act(sm[:, EAC, :], sm[:, ACM, :], AF.Exp, [smk], [smk])
            kb.tt(sm[:, TEv, :], sm[:, TOT, :], sm[:, ACM, :], ALU.subtract, [smk], [smk])
            kb.act(sm[:, TEv, :], sm[:, TEv, :], AF.Exp, [smk], [smk])
            kb.act(sm[:, CDv, :], sm[:, TOT, :], AF.Exp, [smk], [smk])
            xsv = xbc[:, 0:2048].rearrange("p (h f) -> p h f", f=64)
            kb.tt(vd, xsv, sm[:, DT, :].unsqueeze(2).to_broadcast([128, 32, 64]), ALU.mult, [xbck, smk], [vdk])
            kb.tt(sm[:, X0, :], sm[:, DT, :], sm[:, TEv, :], ALU.mult, [smk], [smk])
            kb.tt(vte, xsv, sm[:, X0, :].unsqueeze(2).to_broadcast([128, 32, 64]), ALU.mult, [xbck, smk], [vtek])
            kb.tr([(PST[:, j * 128:(j + 1) * 128], bcb[:, j * 128:(j + 1) * 128]) for j in range(8)], IDb[:], [bcbk, 'IDb'], ['PST'])
            kb.cp(BCT, PST[:, :].rearrange("p (a b) -> p a b", b=128), ['PST'], [BCTk])
            kb.mmx([(PSS[:, g * 128:(g + 1) * 128], BCT[:, g, :], BCT[:, 4 + g, :], True, True) for g in range(4)], [BCTk], ['PSS'])
            kb.cp(HSb, HS, [HSk], [HSbk])
            for g in range(4):
                for hh in (0, 4):
                    pa = PSA[(hh // 4) % 2]; pak = f'PSA{(hh // 4) % 2}'
                    kb.mmx([(pa[:, j * 128:(j + 1) * 128], sm[:, LA, g * 8 + hh + j:g * 8 + hh + j + 1].to_broadcast([128, 128]), tri[:], True, True)
                            for j in range(4)], [smk, trik], [pak])
                    for j in range(4):
                        h = g * 8 + hh + j
                        kb.stt(seg[:, j, :], pa[:, j * 128:(j + 1) * 128], sm[:, ACM, h:h + 1], MASK[d][:], ALU.subtract, ALU.add,
                               [pak, smk, 'MASKF' if d == 0 else 'MASKB'], [segk])
                    kb.act(seg, seg, AF.Exp, [segk], [segk])
                    kb.tt(MT, seg, PSS[:, g * 128:(g + 1) * 128].unsqueeze(1).to_broadcast([128, 4, 128]), ALU.mult, [segk, 'PSS'], [MTk])
                    kb.mmx([(PSY[:, (hh + j) * 64:(hh + j + 1) * 64], MT[:, j, :], vd[:, g * 8 + hh + j, :], True, True) for j in range(4)],
                           [MTk, vdk], ['PSY'])
                kb.mmx([(PSI[:, :], BCT[:, 4 + g, :], HSb[:, g, :], True, True)], [BCTk, HSbk], ['PSI'])
                kb.tt(tmp.rearrange("p (h f) -> p h f", f=64), PSI[:, :].rearrange("p (h f) -> p h f", f=64),
                      sm[:, EAC, g * 8:g * 8 + 8].unsqueeze(2).to_broadcast([128, 8, 64]), ALU.mult, ['PSI', smk], [tmpk])
                kb.tt(ych[:, g * 512:(g + 1) * 512], PSY[:, :], tmp, ALU.add, ['PSY', tmpk], [ychk])
                kb.mmx([(PSI[:, :], bcb[:, g * 128:(g + 1) * 128], vte[:, g * 8:g * 8 + 8, :].rearrange("p h f -> p (h f)"), True, True)],
                       [bcbk, vtek], ['PSI'])
                hsg = HS[:, g, :].rearrange("p (h f) -> p h f", f=64)
                kb.tt(hsg, hsg, sm[:, CDv, g * 8:g * 8 + 8].unsqueeze(2).to_broadcast([128, 8, 64]), ALU.mult, [HSk, smk], [HSk])
                kb.tt(HS[:, g, :], HS[:, g, :], PSI[:, :], ALU.add, [HSk, 'PSI'], [HSk])
            for rnd in range(4):
                kb.tr([(PST[:, j * 128:(j + 1) * 128], qkb[:, (rnd * 8 + j) * 128:(rnd * 8 + j + 1) * 128]) for j in range(8)],
                      IDb[:], [qkbk, 'IDb'], ['PST'])
                kb.cp(QKT[:, rnd * 8:rnd * 8 + 8, :], PST[:, :].rearrange("p (a b) -> p a b", b=128), ['PST'], [QKTk], eng=alt_eng())
            kb.cp(HRb, HR, [HRk], [HRbk])
            for hq in (0, 4):
                items = []
                for j in range(4):
                    h = hq + j
                    for kc in range(2):
                        items.append((PSS[:, j * 128:(j + 1) * 128], QKT[:, 16 + 2 * h + kc, :], QKT[:, 2 * h + kc, :], kc == 0, kc == 1))
                kb.mmx(items, [QKTk], ['PSS'])
                kb.tt(MT, PSS[:, :].rearrange("p (a b) -> p a b", b=128), DM[d][0][:, hq:hq + 4, :], ALU.mult, ['PSS', DM[d][1]], [MTk])
                for jp in (0, 2):
                    h0 = hq + jp
                    kb.mmx([(PSY[:, jj * 256:(jj + 1) * 256], MT[:, jp + jj, :], vb[:, (h0 + jj) * 256:(h0 + jj + 1) * 256], True, True)
                            for jj in range(2)], [MTk, vbk], ['PSY'])
                    items = []
                    for jj in range(2):
                        for kc in range(2):
                            items.append((PSI[:, jj * 256:(jj + 1) * 256], QKT[:, 2 * (h0 + jj) + kc, :], HRb[:, h0 + jj, kc, :], kc == 0, kc == 1))
                    kb.mmx(items, [QKTk, HRbk], ['PSI'])
                    for jj in range(2):
                        kb.tt(tmp[:, jj * 256:(jj + 1) * 256], PSI[:, jj * 256:(jj + 1) * 256],
                              GP[d][0][:, h0 + jj:h0 + jj + 1].to_broadcast([128, 256]), ALU.mult, ['PSI', GP[d][1]], [tmpk])
                    kb.tt(ych[:, 2048 + h0 * 256:2048 + (h0 + 2) * 256], PSY[:, :], tmp, ALU.add, ['PSY', tmpk], [ychk])
            for h in range(8):
                kb.tt(kte[:, h * 256:(h + 1) * 256], qkb[:, 2048 + h * 256:2048 + (h + 1) * 256],
                      TE[d][0][:, h:h + 1].to_broadcast([128, 256]), ALU.mult, [qkbk, TE[d][1]], [ktek])
                for kc in range(2):
                    kb.mmx([(PSI[:, 0:256], kte[:, h * 256 + kc * 128:h * 256 + (kc + 1) * 128], vb[:, h * 256:(h + 1) * 256], True, True)],
                           [ktek, vbk], ['PSI'])
                    kb.stt(HR[:, h, kc, :], HR[:, h, kc, :], CDr[d][0][:, h:h + 1], PSI[:, 0:256], ALU.mult, ALU.add, [HRk, CDr[d][1], 'PSI'], [HRk])
            if d == 0:
                kb.ld(YF[r0:r0 + 128, :], ych, [ychk], [('YF', c)], q='pool')
                return
            kb.ld(yfl, YF[r0:r0 + 128, :], [('YF', c)], [yflk])
            kb.tt(ych, ych, yfl, ALU.add, [ychk, yflk], [ychk])
            zt_ = yfl[:, 0:2048]; gt_ = yfl[:, 2048:4096]
            kb.ld(zt_, PROJA[r0:r0 + 128, 0:2048], [('PROJA', c)], [yflk])
            kb.ld(gt_, PROJB[r0:r0 + 128, 6144:8192], [('PROJB', c)], [yflk])
            ys_ = ych[:, 0:2048]
            kb.tt(vt.rearrange("p (h f) -> p h f", f=64), xsv, Drow.unsqueeze(2).to_broadcast([128, 32, 64]), ALU.mult, [xbck, Dk], [vtk])
            kb.tt(ys_, ys_, vt, ALU.add, [ychk, vtk], [ychk])
            kb.act(zt_, zt_, AF.Silu, [yflk], [yflk])
            kb.tt(ys_, ys_, zt_, ALU.mult, [ychk, yflk], [ychk])
            kb.tt(vt, ys_, ys_, ALU.mult, [ychk], [vtk])
            P.op('dve', lambda e: e.reduce_sum(out=st8[:, 0, 0:4], in_=vt.rearrange("p (g f) -> p g f", f=512), axis=AX.X), [vtk], [st8k])
            kb.ts(st8[:, 0, 0:4], st8[:, 0, 0:4], 1.0 / 512, ALU.mult, [st8k], [st8k], s2=EPS, op1=ALU.add)
            kb.act(st8[:, 0, 0:4], st8[:, 0, 0:4], AF.Sqrt, [st8k], [st8k])
            P.op('dve', lambda e: e.reciprocal(out=st8[:, 1, 0:4], in_=st8[:, 0, 0:4]), [st8k], [st8k])
            for g in range(4):
                kb.stt(yg[:, g * 512:(g + 1) * 512], ys_[:, g * 512:(g + 1) * 512], st8[:, 1, g:g + 1], nws[:, g * 512:(g + 1) * 512],
                       ALU.mult, ALU.mult, [ychk, st8k, nwsk], [ygk])
            yr3 = ych[:, 2048:4096].rearrange("p (h f) -> p h f", f=256)
            vt3 = vt.rearrange("p (h f) -> p h f", f=256)
            P.op('dve', lambda e: e.reduce_sum(out=st8[:, 2, :], in_=yr3, axis=AX.X), [ychk], [st8k])
            kb.ts(st8[:, 2, :], st8[:, 2, :], 1.0 / 256, ALU.mult, [st8k], [st8k])
            kb.tt(yr3, yr3, st8[:, 2, :].unsqueeze(2).to_broadcast([128, 8, 256]), ALU.subtract, [ychk, st8k], [ychk])
            kb.tt(vt3, yr3, yr3, ALU.mult, [ychk], [vtk])
            P.op('dve', lambda e: e.reduce_sum(out=st8[:, 3, :], in_=vt3, axis=AX.X), [vtk], [st8k])
            kb.ts(st8[:, 3, :], st8[:, 3, :], 1.0 / 256, ALU.mult, [st8k], [st8k], s2=EPS, op1=ALU.add)
            kb.act(st8[:, 3, :], st8[:, 3, :], AF.Sqrt, [st8k], [st8k])
            P.op('dve', lambda e: e.reciprocal(out=st8[:, 2, :], in_=st8[:, 3, :]), [st8k], [st8k])
            kb.tt(yr3, yr3, st8[:, 2, :].unsqueeze(2).to_broadcast([128, 8, 256]), ALU.mult, [ychk, st8k], [ychk])
            kb.tt(ych[:, 2048:4096], ych[:, 2048:4096], nwr, ALU.mult, [ychk, nwrk], [ychk])
            kb.act(gt_, gt_, AF.Silu, [yflk], [yflk])
            kb.tt(yg[:, 2048:4096], ych[:, 2048:4096], gt_, ALU.mult, [ychk, yflk], [ygk])
            kb.ld(YG[r0:r0 + 128, :], yg, [ygk], [('YG', c)], q='pool')

        for si, (c0, nchk, cd) in enumerate(kb.seqs):
            sample = (cd == 1)
            for d in range(2):
                if not sample:
                    kb.memset(HS, 0.0, [HSk]); kb.memset(HR, 0.0, [HRk])
                else:
                    for rb in range(16):
                        kb.ld(blk[:, 0:128], I['sssd'][li, d, rb * 128:(rb + 1) * 128, :], [], [blkk])
                        kb.tr([(PSI[:, 0:128], blk[:, 0:128])], IDf[:], [blkk, 'IDf'], ['PSI'])
                        kb.cp(HS[:, rb // 4, (rb % 4) * 128:(rb % 4 + 1) * 128], PSI[:, 0:128], ['PSI'], [HSk])
                    for h in range(8):
                        for dvb in range(2):
                            kb.ld(blk, I['sret'][li, d, h * 256 + dvb * 128:h * 256 + (dvb + 1) * 128, :], [], [blkk])
                            for kc in range(2):
                                kb.tr([(PSI[:, 0:128], blk[:, kc * 128:(kc + 1) * 128])], IDf[:], [blkk, 'IDf'], ['PSI'])
                                kb.cp(HR[:, h, kc, dvb * 128:(dvb + 1) * 128], PSI[:, 0:128], ['PSI'], [HRk])
                order = range(c0, c0 + nchk) if d == 0 else range(c0 + nchk - 1, c0 - 1, -1)
                for c in order:
                    chunk(c, d, sample)
                if not sample:
                    for rb in range(16):
                        kb.tr([(PSI[:, 0:128], HS[:, rb // 4, (rb % 4) * 128:(rb % 4 + 1) * 128])], IDf[:], [HSk, 'IDf'], ['PSI'])
                        kb.cp(blk[:, 0:128], PSI[:, 0:128], ['PSI'], [blkk])
                        kb.ld(O['nssd'][si, li, d, rb * 128:(rb + 1) * 128, :], blk[:, 0:128], [blkk], [('NS', si, li, d, rb)])
                    for h in range(8):
                        for dvb in range(2):
                            for kc in range(2):
                                kb.tr([(PSI[:, 0:128], HR[:, h, kc, dvb * 128:(dvb + 1) * 128])], IDf[:], [HRk, 'IDf'], ['PSI'])
                                kb.cp(blk[:, kc * 128:(kc + 1) * 128], PSI[:, 0:128], ['PSI'], [blkk])
                            kb.ld(O['nret'][si, li, d, h * 256 + dvb * 128:h * 256 + (dvb + 1) * 128, :], blk, [blkk], [('NR', si, li, d, h, dvb)])
        P.barrier(); AR.reset()
        linear(make_prep_load(YG, 'YG', 2 * D, True), 2 * D, [(I['ab_w_out'][li], 0, D, MIXO, 'MIXO', 0)], None)


    def stage_hy(l, resid_from):
        li = l // 2
        T_ = AR.tile
        linear(make_prep_norm(l, 0, I['norm1_w'][l, :], resid_from), D, [(I['hy_w_in'][li], 0, 3 * D, PROJB, 'PROJB', 0)], None)

        def post_setup():
            def post(c, col0, ncol, acc, akey):
                kb.ld(XG[c * 128:(c + 1) * 128, col0:col0 + ncol], acc[:, :ncol], [akey], [('XG', c)], q='pool')
            return post
        conv3(PROJB, 'PROJB', 0, 3 * D, I['hy_short_w'][li], I['hy_short_b'][li], post_setup)
        MAGIC = 12582912.0
        TWO_PI = 2.0 * math.pi
        for L in LENS:
            nch = L // 128
            N2 = 2 * L
            ft, ftk = T_([128, L]); H1, H1k = T_([128, L]); H2, H2k = T_([128, L])
            w1, w1k = T_([128, 64]); w2, w2k = T_([128, 64]); w3, w3k = T_([128, 4 * D])
            bb, bbk = T_([128, 2]); u, uk = T_([128, 512]); r_, rk = T_([128, 512])
            dl, dlk = T_([128, D]); ng, ngk = T_([128, nch])
            kb.ld(ft[0:33, :], I[f'k_featsT{L}'], [], [ftk])
            kb.ld(w1[0:33, :], I['hy_f_w1'][li], [], [w1k]); kb.ld(w2[0:64, :], I['hy_f_w2'][li], [], [w2k])
            kb.ld(w3[0:64, :], I['hy_f_w3'][li], [], [w3k])
            kb.ld(bb[0:64, 0:1], I['hy_f_b1'][li, :].rearrange("(p o) -> p o", o=1), [], [bbk])
            kb.ld(bb[0:64, 1:2], I['hy_f_b2'][li, :].rearrange("(p o) -> p o", o=1), [], [bbk])
            bcast_row(dl, dlk, I['k_delta']); kb.ld(ng, I[f'k_negt{L}'], [], [ngk])
            for layer in range(2):
                src, srck, kk = (ft, ftk, 33) if layer == 0 else (H1, H1k, 64)
                wt, wtk = (w1, w1k) if layer == 0 else (w2, w2k)
                dstT, dstk = (H1, H1k) if layer == 0 else (H2, H2k)
                for b0 in range(0, L, 512):
                    bw = min(512, L - b0)
                    kb.mmx([(PSL[0][0:64, :bw], wt[0:kk, :], src[0:kk, b0:b0 + bw], True, True)], [wtk, srck], ['PSL0'])
                    kb.act(u[0:64, :bw], PSL[0][0:64, :bw], AF.Identity, ['PSL0', bbk], [uk], bias=bb[0:64, layer:layer + 1])
                    kb.ts(r_[0:64, :bw], u[0:64, :bw], 1.0 / TWO_PI, ALU.mult, [uk], [rk], s2=MAGIC, op1=ALU.add)
                    kb.ts(r_[0:64, :bw], r_[0:64, :bw], -MAGIC, ALU.add, [rk], [rk])
                    kb.stt(u[0:64, :bw], r_[0:64, :bw], -TWO_PI, u[0:64, :bw], ALU.mult, ALU.add, [rk, uk], [uk])
                    kb.ts(u[0:64, :bw], u[0:64, :bw], -3.14159, ALU.max, [uk], [uk], s2=3.14159, op1=ALU.min)
                    kb.act(dstT[0:64, b0:b0 + bw], u[0:64, :bw], AF.Sin, [uk], [dstk])
            win, wink = T_([128, 512]); hw, hwk = T_([128, 512]); ab, abk = T_([128, 512]); rn, rnk = T_([128, 512])
            for cbk in range(16):
                ccol = (cbk * 512) % D
                for tch in range(nch):
                    kb.mmx([(PSL[0][:, :], H2[0:64, tch * 128:(tch + 1) * 128], w3[0:64, cbk * 512:(cbk + 1) * 512], True, True)],
                           [H2k, w3k], ['PSL0'])
                    kb.act(win, dl[:, ccol:ccol + 512], AF.Exp, [dlk, ngk], [wink], scale=ng[:, tch:tch + 1])
                    kb.tt(hw, PSL[0][:, :], win, ALU.mult, ['PSL0', wink], [hwk])
                    kb.ld(FILT[L][tch * 128:(tch + 1) * 128, cbk * 512:(cbk + 1) * 512], hw, [hwk], [('FILT', tch, cbk)])
                    kb.ts(ab, hw, -1.0, ALU.mult, [hwk], [abk])
                    kb.tt(ab, ab, hw, ALU.max, [abk, hwk], [abk])
                    kb.mmx([(PSA[0][:, :], ONES[:], ab, tch == 0, tch == nch - 1)], ['ONES', abk], ['PSA0'])
                kb.ts(rn, PSA[0][:, :], EPS, ALU.add, ['PSA0'], [rnk])
                P.op('dve', lambda e, rn=rn: e.reciprocal(out=rn, in_=rn), [rnk], [rnk])
                kb.ld(RN[L][cbk * 512:(cbk + 1) * 512].rearrange("(o f) -> o f", o=1), rn[0:1, :], [rnk], [('RN', cbk)])
            P.barrier(); AR.reset()
            CW = 512
            FW = FWT
            hp, hpk = T_([128, nch, CW], BF16); hm, hmk = T_([128, nch, CW], BF16)
            CBS = [T_([128, nch, FW], BF16) for _ in range(2)]; SBS = [T_([128, nch, FW], BF16) for _ in range(2)]
            tbi = {'i': 0}
            hf, hfk = T_([128, CW]); hb_, hbk = T_([128, CW]); rf, rfk = T_([128, CW]); rb, rbk = T_([128, CW])
            fr, frk = T_([128, CW]); fi, fik = T_([128, CW])
            cv = I[f'k_dftc{L}'].rearrange("(tc p) f -> p tc f", p=128)
            sv = I[f'k_dfts{L}'].rearrange("(tc p) f -> p tc f", p=128)
            stv = I[f'k_dftst{L}'].rearrange("(tc p) f -> p tc f", p=128)
            for o in range(2):
                for cb in range(D // CW):
                    colf = (0 * 2 + o) * D + cb * CW; colb = (1 * 2 + o) * D + cb * CW
                    bcast_row(rf, rfk, RN[L][colf:colf + CW], [('RN', colf // 512)])
                    bcast_row(rb, rbk, RN[L][colb:colb + CW], [('RN', colb // 512)])
                    for tch in range(nch):
                        kb.ld(hf, FILT[L][tch * 128:(tch + 1) * 128, colf:colf + CW], [('FILT', tch, colf // 512)], [hfk])
                        kb.ld(hb_, FILT[L][tch * 128:(tch + 1) * 128, colb:colb + CW], [('FILT', tch, colb // 512)], [hbk])
                        if tch == 0:
                            kb.ld(hb_[0:1, :], ZERO[:, :CW], ['ZERO'], [hbk])
                        kb.tt(hf, hf, rf, ALU.mult, [hfk, rfk], [hfk])
                        kb.tt(hb_, hb_, rb, ALU.mult, [hbk, rbk], [hbk])
                        kb.tt(hp[:, tch, :], hf, hb_, ALU.add, [hfk, hbk], [hpk])
                        kb.tt(hm[:, tch, :], hf, hb_, ALU.subtract, [hfk, hbk], [hmk])
                    for f0 in range(0, L, FW):
                        (Cb, Cbk), (Sb, Sbk) = CBS[tbi['i'] % 2], SBS[tbi['i'] % 2]; tbi['i'] += 1
                        kb.ld(Cb.rearrange("p a b -> p (a b)"), TAB[L]['c'][f0 // FW], [], [Cbk], q='pool')
                        kb.ld(Sb.rearrange("p a b -> p (a b)"), TAB[L]['s'][f0 // FW], [], [Sbk], q='pool')
                        for fc in range(FW // 128):
                            fch = f0 // 128 + fc
                            kb.mmx([(PSL[0][:, :CW], Cb[:, t_, fc * 128:(fc + 1) * 128], hp[:, t_, :], t_ == 0, t_ == nch - 1) for t_ in range(nch)],
                                   [Cbk, hpk], ['PSL0'])
                            kb.mmx([(PSL[1][:, :CW], Sb[:, t_, fc * 128:(fc + 1) * 128], hm[:, t_, :], t_ == 0, t_ == nch - 1) for t_ in range(nch)],
                                   [Sbk, hmk], ['PSL1'])
                            kb.ts(fr, PSL[0][:, :CW], 2.0 / N2, ALU.mult, ['PSL0'], [frk])
                            kb.ts(fi, PSL[1][:, :CW], 2.0 / N2, ALU.mult, ['PSL1'], [fik])
                            if fch == 0:
                                kb.mmx([(PSA[1][:, :CW], ALTb[:], hp[:, t_, :], t_ == 0, t_ == nch - 1) for t_ in range(nch)], ['ALTb', hpk], ['PSA1'])
                                kb.ts(fi[0:1, :], PSA[1][0:1, :CW], 1.0 / N2, ALU.mult, ['PSA1', fik], [fik])
                                kb.ts(fr[0:1, :], fr[0:1, :], 0.5, ALU.mult, [frk], [frk])
                            kb.ld(FSP[L][o, 0, fch * 128:(fch + 1) * 128, cb * CW:(cb + 1) * CW], fr, [frk], [('FSP', o, 0, fch, cb)])
                            kb.ld(FSP[L][o, 1, fch * 128:(fch + 1) * 128, cb * CW:(cb + 1) * CW], fi, [fik], [('FSP', o, 1, fch, cb)])
            P.barrier(); AR.reset()
            Yb, Ybk = T_([128, nch, 2, CW], BF16); zb, zbk = T_([128, nch, CW], BF16)
            CBS = [T_([128, nch, FW], BF16) for _ in range(2)]; SBS = [T_([128, nch, FW], BF16) for _ in range(2)]
            zf, zfk = T_([128, CW]); gt, gtk = T_([128, CW]); brow, browk = T_([128, CW])
            fr, frk = T_([128, CW]); fi, fik = T_([128, CW]); t1, t1k = T_([128, CW]); t2, t2k = T_([128, CW])
            for (c0, nchk, cd) in kb.seqs:
                if nchk != nch:
                    continue
                for o in range(2):
                    zsrc, zname, zcol = (XG, 'XG', 2 * D) if o == 0 else (Z1, 'Z1', 0)
                    gcol = o * D
                    zdst, zdname = (Z1, 'Z1') if o == 0 else (Z2, 'Z2')
                    for cb in range(D // CW):
                        bcast_row(brow, browk, I['hy_bias'][li, o, cb * CW:(cb + 1) * CW])
                        for tch in range(nch):
                            c = c0 + tch
                            kb.ld(zf, zsrc[c * 128:(c + 1) * 128, zcol + cb * CW:zcol + (cb + 1) * CW], [(zname, c)], [zfk])
                            kb.cp(zb[:, tch, :], zf, [zfk], [zbk], eng=alt_eng())
                        for f0 in range(0, L, FW):
                            (Cb, Cbk), (Sb, Sbk) = CBS[tbi['i'] % 2], SBS[tbi['i'] % 2]; tbi['i'] += 1
                            kb.ld(Cb.rearrange("p a b -> p (a b)"), TAB[L]['c'][f0 // FW], [], [Cbk], q='pool')
                            kb.ld(Sb.rearrange("p a b -> p (a b)"), TAB[L]['s'][f0 // FW], [], [Sbk], q='pool')
                            for fc in range(FW // 128):
                                fch = f0 // 128 + fc
                                kb.mmx([(PSL[0][:, :CW], Cb[:, t_, fc * 128:(fc + 1) * 128], zb[:, t_, :], t_ == 0, t_ == nch - 1) for t_ in range(nch)],
                                       [Cbk, zbk], ['PSL0'])
                                kb.mmx([(PSL[1][:, :CW], Sb[:, t_, fc * 128:(fc + 1) * 128], zb[:, t_, :], t_ == 0, t_ == nch - 1) for t_ in range(nch)],
                                       [Sbk, zbk], ['PSL1'])
                                kb.ld(fr, FSP[L][o, 0, fch * 128:(fch + 1) * 128, cb * CW:(cb + 1) * CW], [('FSP', o, 0, fch, cb)], [frk])
                                kb.ld(fi, FSP[L][o, 1, fch * 128:(fch + 1) * 128, cb * CW:(cb + 1) * CW], [('FSP', o, 1, fch, cb)], [fik])
                                kb.tt(t1, PSL[0][:, :CW], fr, ALU.mult, ['PSL0', frk], [t1k])
                                kb.tt(t2, PSL[1][:, :CW], fi, ALU.mult, ['PSL1', fik], [t2k])
                                kb.tt(Yb[:, fch, 0, :], t1, t2, ALU.subtract, [t1k, t2k], [Ybk])
                                if fch == 0:
                                    kb.cp(Yb[0:1, 0, 0, :], t1[0:1, :], [t1k], [Ybk])
                                    kb.cp(Yb[0:1, 0, 1, :], t2[0:1, :], [t2k], [Ybk])
                                kb.tt(t1, PSL[0][:, :CW], fi, ALU.mult, ['PSL0', fik], [t1k])
                                kb.tt(t2, PSL[1][:, :CW], fr, ALU.mult, ['PSL1', frk], [t2k])
                                if fch == 0:
                                    kb.tt(Yb[1:128, fch, 1, :], t1[1:128, :], t2[1:128, :], ALU.add, [t1k, t2k], [Ybk]) if False else None
                                    kb.tt(t1, t1, t2, ALU.add, [t1k, t2k], [t1k])
                                    kb.cp(t1[0:1, :], Yb[0:1, 0, 1, :], [Ybk], [t1k])
                                    kb.cp(Yb[:, fch, 1, :], t1, [t1k], [Ybk])
                                else:
                                    kb.tt(Yb[:, fch, 1, :], t1, t2, ALU.add, [t1k, t2k], [Ybk])
                        for t0 in range(0, L, FW):
                            (Cb, Cbk), (Sb, Sbk) = CBS[tbi['i'] % 2], SBS[tbi['i'] % 2]; tbi['i'] += 1
                            kb.ld(Cb.rearrange("p a b -> p (a b)"), TAB[L]['c'][t0 // FW], [], [Cbk], q='pool')
                            kb.ld(Sb.rearrange("p a b -> p (a b)"), TAB[L]['st'][t0 // FW], [], [Sbk], q='pool')
                            for tc in range(FW // 128):
                                tch = t0 // 128 + tc
                                c = c0 + tch
                                items = []
                                for f_ in range(nch):
                                    items.append((PSL[0][:, :CW], Cb[:, f_, tc * 128:(tc + 1) * 128], Yb[:, f_, 0, :], f_ == 0, False))
                                    items.append((PSL[0][:, :CW], Sb[:, f_, tc * 128:(tc + 1) * 128], Yb[:, f_, 1, :], False, f_ == nch - 1))
                                kb.mmx(items, [Cbk, Sbk, Ybk], ['PSL0'])
                                kb.ld(zf, zsrc[c * 128:(c + 1) * 128, zcol + cb * CW:zcol + (cb + 1) * CW], [(zname, c)], [zfk])
                                kb.ld(gt, XG[c * 128:(c + 1) * 128, gcol + cb * CW:gcol + (cb + 1) * CW], [('XG', c)], [gtk])
                                kb.tt(zf, zf, brow, ALU.mult, [zfk, browk], [zfk])
                                kb.tt(zf, zf, PSL[0][:, :CW], ALU.add, [zfk, 'PSL0'], [zfk])
                                kb.tt(zf, zf, gt, ALU.mult, [zfk, gtk], [zfk])
                                kb.ld(zdst[c * 128:(c + 1) * 128, cb * CW:(cb + 1) * CW], zf, [zfk], [(zdname, c)])
            P.barrier(); AR.reset()
        linear(make_prep_load(Z2, 'Z2', D, False), D, [(I['hy_w_out'][li], 0, D, MIXO, 'MIXO', 0)], None)

    def stage_final(resid_from):
        nw, nk = AR.tile([128, D]); bcast_row(nw, nk, I['final_norm_w'])
        grow = []
        for cd in range(2):
            g, gk = AR.tile([128, D])
            if resid_from is not None:
                gl, gidx = resid_from
                bcast_row(g, gk, MODD[gl * 2 + cd, gidx * D:(gidx + 1) * D], ['MODD'])
            grow.append((g, gk))
        xts = [AR.tile([128, D]) for _ in range(2)]
        mts = [AR.tile([128, D]) for _ in range(2)]
        sts = [AR.tile([128, 4]) for _ in range(2)]
        for c in range(NCH):
            i = c % 2
            cd = kb.cond_of[c]
            xt, xk = xts[i]; mt, mk = mts[i]; s4, s4k = sts[i]
            kb.ld(xt, X[c * 128:(c + 1) * 128, :], [('X', c)], [xk])
            if resid_from is not None:
                kb.ld(mt, MIXO[c * 128:(c + 1) * 128, :], [('MIXO', c)], [mk])
                g, gk = grow[cd]
                kb.tt(mt, mt, g, ALU.mult, [mk, gk], [mk])
                kb.tt(xt, xt, mt, ALU.add, [xk, mk], [xk])
            kb.tt(mt, xt, xt, ALU.mult, [xk], [mk])
            P.op('dve', lambda e, s4=s4, mt=mt: e.reduce_sum(out=s4[:, 0:1], in_=mt, axis=AX.X), [mk], [s4k])
            kb.ts(s4[:, 1:2], s4[:, 0:1], 1.0 / D, ALU.mult, [s4k], [s4k], s2=EPS, op1=ALU.add)
            kb.act(s4[:, 2:3], s4[:, 1:2], AF.Sqrt, [s4k], [s4k])
            P.op('dve', lambda e, s4=s4: e.reciprocal(out=s4[:, 3:4], in_=s4[:, 2:3]), [s4k], [s4k])
            kb.stt(mt, xt, s4[:, 3:4], nw, ALU.mult, ALU.mult, [xk, s4k, nk], [mk])
            r0 = c * 128
            if r0 < NP * 256:
                dst = O['yp'][r0:r0 + 128, :]
            else:
                dst = O['ys'][r0 - NP * 256:r0 - NP * 256 + 128, :]
            kb.ld(dst, mt, [mk], [('OUT', c)])
        P.barrier(); AR.reset()

    kb._stuff = dict(stage_final=stage_final, stage_ab=stage_ab, stage_hy=stage_hy, I=I, O=O, X=X, MODD=MODD, XBC=XBC, QK=QK, YF=YF, YG=YG, MIXO=MIXO, XG=XG, Z1=Z1, Z2=Z2,
                     FILT=FILT, RN=RN, FSP=FSP, ZERO=ZERO, AR=AR, IDf=IDf, IDb=IDb, ULE=ULE, LGE=LGE, ONES=ONES, MASK=MASK,
                     DIFF=DIFF, POSC=POSC, ALTb=ALTb, PSL=PSL, PST=PST, PSA=PSA, PSS=PSS, PSY=PSY, PSI=PSI,
                     linear=linear, make_prep_norm=make_prep_norm, make_prep_load=make_prep_load, conv3=conv3,
                     bcast_row=bcast_row, alt_eng=alt_eng, stage_mod=stage_mod, stage_ffn=stage_ffn, LENS=LENS)
    return kb


def assemble(kb, mixers=True):
    S = kb._stuff
    if kb.dbg_level >= 1:
        S['stage_mod']()
    pending = None
    for l in range(kb.DEPTH):
        if mixers:
            if l % 2 == 0:
                S['stage_ab'](l, pending)
            else:
                S['stage_hy'](l, pending)
            pending = (l, 2)
        S['stage_ffn'](l, pending)
        pending = (l, 5)
    if kb.dbg_level >= 5:
        S['stage_final'](pending)
    kb.P.emit(kb.es)
    kb.es.close()
    return kb.nc


def make_consts(LS):
    K = {}
    j = np.arange(128)[:, None]; i = np.arange(128)[None, :]
    K['k_ident'] = np.eye(128, dtype=np.float32)
    K['k_ule'] = (j <= i).astype(np.float32)
    K['k_lge'] = (j >= i).astype(np.float32)
    K['k_maskf'] = np.where(i >= j, 0.0, -1e30).astype(np.float32)
    K['k_maskb'] = np.where(i <= j, 0.0, -1e30).astype(np.float32)
    K['k_difff'] = np.where(i >= j, (i - j).astype(np.float64), 1e9).astype(np.float32)
    K['k_diffb'] = np.where(j >= i, (j - i).astype(np.float64), 1e9).astype(np.float32)
    p = np.arange(128, dtype=np.float64)
    K['k_posc'] = np.stack([p + 1, 128 - p, 127 - p, p], axis=1).astype(np.float32)
    K['k_alt'] = np.broadcast_to(((-1.0) ** p)[:, None], (128, 128)).astype(np.float32).copy()
    n = 64
    inv = (np.float32(10000.0) ** (-np.arange(n, dtype=np.float32) / np.float32(n))).astype(np.float32)
    t = np.arange(LS)
    rows = (t // 64).astype(np.float32); cols = (t % 64).astype(np.float32)
    angr = (rows[:, None] * inv[None, :]).astype(np.float32); angc = (cols[:, None] * inv[None, :]).astype(np.float32)
    K['k_rope'] = np.stack([np.cos(angr), np.sin(angr), np.cos(angc), np.sin(angc)]).astype(np.float32)
    K['k_delta'] = np.abs(np.linspace(math.log(1e-2) / 1.5, math.log(1e-2) / 0.3, 2048, dtype=np.float32)).astype(np.float32)
    for L in sorted({256, LS}):
        tt = (np.arange(L, dtype=np.float32) / np.float32(L)).astype(np.float32)
        bands = np.linspace(1e-4, 15, 16, dtype=np.float32)
        ang = (np.float32(2 * math.pi) * tt[:, None] * bands[None, :]).astype(np.float32)
        feats = np.concatenate([tt[:, None], np.cos(ang), np.sin(ang)], axis=-1).astype(np.float32)
        K[f'k_featsT{L}'] = np.ascontiguousarray(feats.T)
        K[f'k_negt{L}'] = np.ascontiguousarray((-tt).reshape(L // 128, 128).T)
        N = 2 * L
        tf = (np.arange(L)[:, None] * np.arange(L)[None, :]) % N
        ang2 = 2.0 * np.pi * tf / N
        Cm = np.cos(ang2); Sm = -np.sin(ang2)
        Sm[:, 0] = (-1.0) ** np.arange(L)
        K[f'k_dftc{L}'] = Cm.astype(np.float32)
        K[f'k_dfts{L}'] = Sm.astype(np.float32)
        K[f'k_dftst{L}'] = np.ascontiguousarray(Sm.T).astype(np.float32)
    return K


def core_inputs(inputs, K, core, NP, b, DEPTH=4):
    f = lambda a: np.ascontiguousarray(np.asarray(a, dtype=np.float32))
    m = {}
    m['xp'] = f(inputs['x_prompt'][core * NP:(core + 1) * NP]).reshape(NP * 256, D)
    m['xs'] = f(inputs['x_sample'][b])
    m['cvec'] = f(np.stack([np.asarray(inputs['c_ctx']), np.asarray(inputs['c'])[b]]))
    m['sssd'] = f(inputs['state_ssd'][b]).reshape(2, 2, 2048, 128)
    m['sret'] = f(inputs['state_ret'][b]).reshape(2, 2, 2048, 256)
    for k in ['ada_w', 'ada_b', 'norm1_w', 'norm2_w', 'ab_w_in', 'ab_conv_w', 'ab_conv_b', 'ssd_norm_w', 'ret_norm_w',
              'ab_w_out', 'hy_w_in', 'hy_short_w', 'hy_short_b', 'hy_f_w1', 'hy_f_b1', 'hy_f_w2', 'hy_f_b2', 'hy_f_w3',
              'hy_bias', 'hy_w_out', 'ffn_w_gate', 'ffn_w_up', 'ffn_conv_w', 'ffn_conv_b', 'ffn_w_down', 'final_norm_w', 'ssd_d']:
        m[k] = f(inputs[k])
    m['ssd_dt_bias'] = f(inputs['ssd_dt_bias']).reshape(2, 64)
    m['ssd_a_log'] = f(inputs['ssd_a_log']).reshape(2, 64)
    m['ret_log_gamma'] = f(inputs['ret_log_gamma']).reshape(2, 16)
    NLA = (DEPTH + 1) // 2; NLH = max(DEPTH // 2, 1)
    for k in list(m.keys()):
        if k in ('ada_w', 'ada_b', 'norm1_w', 'norm2_w', 'ffn_w_gate', 'ffn_w_up', 'ffn_conv_w', 'ffn_conv_b', 'ffn_w_down'):
            m[k] = np.ascontiguousarray(m[k][:DEPTH])
        elif k.startswith('ab_') or k.startswith('ssd_') or k.startswith('ret_') or k in ('sssd', 'sret'):
            m[k] = np.ascontiguousarray(m[k][:NLA])
        elif k.startswith('hy_'):
            m[k] = np.ascontiguousarray(m[k][:NLH])
    m.update(K)
    return m


_NC_CACHE = {}


def kernel(**inputs):
    NCORES, NP, LS, DEPTH = 8, 2, 4096, 4
    key = (NP, LS, DEPTH)
    if key not in _NC_CACHE:
        _NC_CACHE[key] = assemble(build(NP, LS, DEPTH))
    nc = _NC_CACHE[key]
    K = make_consts(LS)
    in_maps = [core_inputs(inputs, K, c, NP, c % 2) for c in range(NCORES)]
    res = run_bass_kernel_spmd(nc, in_maps, core_ids=list(range(NCORES)))
    R = res.results
    y_prompt = np.concatenate([R[c]['yp'].reshape(NP, 256, D) for c in range(NCORES)], axis=0)
    y_sample = np.stack([R[0]['ys'], R[1]['ys']], axis=0)
    nssd = np.concatenate([R[c]['nssd'].reshape(NP, 2, 2, 32, 64, 128) for c in range(NCORES)], axis=0)
    nret = np.concatenate([R[c]['nret'].reshape(NP, 2, 2, 8, 256, 256) for c in range(NCORES)], axis=0)
    return (y_prompt.astype(np.float32), y_sample.astype(np.float32), nssd.astype(np.float32), nret.astype(np.float32))
```
